# Optimizing a Trainium2 kernel written in Bass

```python
import math
import jax, jax.numpy as jnp
from jax import lax
import numpy as np

D_MODEL = 2048
BATCH = 1
SEQ = 16384
DEPTH = 1
DEC_BATCH = 2
DEC_SEQ = 8192
PAST_LEN = 128

DA_HEADS = 8
DA_QK_DIM = 64
DA_V_DIM = 2 * DA_QK_DIM
DA_ROT = DA_QK_DIM // 4
DA_Q_W = DA_HEADS * 2 * DA_QK_DIM
DA_V_W = DA_HEADS * DA_V_DIM
MLA_HEADS = 8
MLA_NOPE = 128
MLA_ROPE = 64
MLA_V = 128
MLA_Q_RANK = 512
MLA_KV_RANK = 256
ROPE_THETA = 500000.0
D_FF = 5632
Q_BLOCK = 128
EPS = 1e-6
N_MOD = 9
IN_COLS = 2 * DA_Q_W + DA_V_W + MLA_Q_RANK + MLA_KV_RANK + MLA_ROPE
MIX_W = DA_V_W + MLA_HEADS * MLA_V

kernel_name = "hymba_diffattn_mla_macaron_adaln_encoder"


def rms_norm(x, g):
    xf = x.astype(jnp.float32)
    y = xf * lax.rsqrt(jnp.mean(xf * xf, axis=-1, keepdims=True) + EPS)
    return (y * g.astype(jnp.float32)).astype(x.dtype)


def rope_tables(seq, dim):
    inv = ROPE_THETA ** (-jnp.arange(0, dim, 2, dtype=jnp.float32) / dim)
    ang = jnp.arange(seq, dtype=jnp.float32)[:, None] * inv[None, :]
    return jnp.cos(ang), jnp.sin(ang)


def apply_rope(x, cos, sin):
    half = x.shape[-1] // 2
    shp = (cos.shape[0],) + (1,) * (x.ndim - 3) + (half,)
    c = cos.reshape(shp).astype(x.dtype)
    s = sin.reshape(shp).astype(x.dtype)
    x1, x2 = x[..., :half], x[..., half:]
    return jnp.concatenate([x1 * c - x2 * s, x1 * s + x2 * c], axis=-1)


def blocked_queries(fn, *qs):
    B, S = qs[0].shape[:2]
    nb = S // Q_BLOCK
    qb = tuple(jnp.moveaxis(q.reshape((B, nb, Q_BLOCK) + q.shape[2:]), 1, 0) for q in qs)
    out = lax.map(lambda a: fn(*a), qb)
    return jnp.moveaxis(out, 0, 1).reshape((B, S) + out.shape[3:])


def diff_attention(q, k, v, lam, subln_g, lambda_init):
    scale = DA_QK_DIM ** -0.5
    kf = k.astype(jnp.float32)
    vf = v.astype(jnp.float32)

    def block(qb):
        s = jnp.einsum('bqhjd,bkhjd->bhjqk', qb.astype(jnp.float32) * scale, kf)
        p = jax.nn.softmax(s, axis=-1)
        a = p[:, :, 0] - lam * p[:, :, 1]
        return jnp.einsum('bhqk,bkhd->bqhd', a, vf)

    o = blocked_queries(block, q)
    o = rms_norm(o, subln_g) * (1.0 - lambda_init)
    return o.astype(v.dtype)


def mla_attention(c_q, c_kv, k_pe, q_norm_g, kv_norm_g, w_uq, w_ukv, cos, sin):
    B, S, _ = c_q.shape
    q = (rms_norm(c_q, q_norm_g) @ w_uq).reshape(B, S, MLA_HEADS, MLA_NOPE + MLA_ROPE)
    q_nope = q[..., :MLA_NOPE]
    q_pe = apply_rope(q[..., MLA_NOPE:], cos, sin)
    kv = (rms_norm(c_kv, kv_norm_g) @ w_ukv).reshape(B, S, MLA_HEADS, MLA_NOPE + MLA_V)
    k_nope = kv[..., :MLA_NOPE].astype(jnp.float32)
    vf = kv[..., MLA_NOPE:].astype(jnp.float32)
    kpe = apply_rope(k_pe[:, :, None, :], cos, sin)[:, :, 0].astype(jnp.float32)
    scale = (MLA_NOPE + MLA_ROPE) ** -0.5

    def block(qn, qp):
        s = (jnp.einsum('bqhd,bkhd->bhqk', qn.astype(jnp.float32), k_nope)
             + jnp.einsum('bqhd,bkd->bhqk', qp.astype(jnp.float32), kpe))
        p = jax.nn.softmax(s * scale, axis=-1)
        return jnp.einsum('bhqk,bkhd->bqhd', p, vf)

    o = blocked_queries(block, q_nope, q_pe)
    return o.astype(c_q.dtype)


def swiglu(h, w1, w3, w2):
    return (jax.nn.silu(h @ w1) * (h @ w3)) @ w2


def modulate(x, g, shift, scale):
    return rms_norm(x, g) * (1.0 + scale) + shift


def encoder(x, c, ffn1_norm, ffn1_w1, ffn1_w3, ffn1_w2, attn_norm, w_in,
            da_lambda_q1, da_lambda_k1, da_lambda_q2, da_lambda_k2, da_subln,
            mla_q_norm, mla_w_uq, mla_kv_norm, mla_w_ukv, w_o,
            ffn2_norm, ffn2_w1, ffn2_w3, ffn2_w2, w_ada, b_ada, final_norm):
    B, S, _ = x.shape
    cos_da, sin_da = rope_tables(S, DA_ROT)
    cos_mla, sin_mla = rope_tables(S, MLA_ROPE)
    for l in range(DEPTH):
        m = jax.nn.silu(c) @ w_ada[l] + b_ada[l]
        sh1, sc1, g1, sh2, sc2, g2, sh3, sc3, g3 = jnp.split(m[:, None, :], N_MOD, axis=-1)

        h = modulate(x, ffn1_norm[l], sh1, sc1)
        x = x + 0.5 * g1 * swiglu(h, ffn1_w1[l], ffn1_w3[l], ffn1_w2[l])

        h = modulate(x, attn_norm[l], sh2, sc2)
        proj = h @ w_in[l]
        o1 = DA_Q_W
        o2 = o1 + DA_Q_W
        o3 = o2 + DA_V_W
        o4 = o3 + MLA_Q_RANK
        o5 = o4 + MLA_KV_RANK
        q_da = proj[..., :o1].reshape(B, S, DA_HEADS, 2, DA_QK_DIM)
        k_da = proj[..., o1:o2].reshape(B, S, DA_HEADS, 2, DA_QK_DIM)
        v_da = proj[..., o2:o3].reshape(B, S, DA_HEADS, DA_V_DIM)
        c_q = proj[..., o3:o4]
        c_kv = proj[..., o4:o5]
        k_pe = proj[..., o5:]

        q_da = jnp.concatenate([apply_rope(q_da[..., :DA_ROT], cos_da, sin_da), q_da[..., DA_ROT:]], axis=-1)
        k_da = jnp.concatenate([apply_rope(k_da[..., :DA_ROT], cos_da, sin_da), k_da[..., DA_ROT:]], axis=-1)
        lambda_init = 0.8 - 0.6 * math.exp(-0.3 * l)
        lam = (jnp.exp(jnp.sum(da_lambda_q1[l].astype(jnp.float32) * da_lambda_k1[l].astype(jnp.float32)))
               - jnp.exp(jnp.sum(da_lambda_q2[l].astype(jnp.float32) * da_lambda_k2[l].astype(jnp.float32)))
               + lambda_init)
        o_da = diff_attention(q_da, k_da, v_da, lam, da_subln[l], lambda_init)
        o_mla = mla_attention(c_q, c_kv, k_pe, mla_q_norm[l], mla_kv_norm[l],
                              mla_w_uq[l], mla_w_ukv[l], cos_mla, sin_mla)
        mixed = jnp.concatenate([o_da.reshape(B, S, DA_V_W), o_mla.reshape(B, S, MLA_HEADS * MLA_V)], axis=-1)
        x = x + g2 * (mixed @ w_o[l])

        h = modulate(x, ffn2_norm[l], sh3, sc3)
        x = x + 0.5 * g3 * swiglu(h, ffn2_w1[l], ffn2_w3[l], ffn2_w2[l])
    return rms_norm(x, final_norm)


def setup_inputs(seed: int = 0) -> dict:
    key = jax.random.key(seed)
    ks = iter(jax.random.split(key, 40))

    def w(shape, fan_in, mult=1.0):
        return jax.random.normal(next(ks), shape, jnp.float32) * (mult * fan_in ** -0.5)

    def gain(shape):
        return 1.0 + 0.01 * jax.random.normal(next(ks), shape, jnp.float32)

    def small(shape, s):
        return s * jax.random.normal(next(ks), shape, jnp.float32)

    L, D = DEPTH, D_MODEL
    return {
        "x_prompt": jax.random.normal(next(ks), (BATCH, SEQ, D), jnp.float32),
        "x_sample": jax.random.normal(next(ks), (DEC_BATCH, DEC_SEQ, D), jnp.float32),
        "c_prompt": jax.random.normal(next(ks), (BATCH, D), jnp.float32),
        "c_sample": jax.random.normal(next(ks), (DEC_BATCH, D), jnp.float32),
        "ffn1_norm": gain((L, D)),
        "ffn1_w1": w((L, D, D_FF), D),
        "ffn1_w3": w((L, D, D_FF), D),
        "ffn1_w2": w((L, D_FF, D), D_FF),
        "attn_norm": gain((L, D)),
        "w_in": w((L, D, IN_COLS), D),
        "da_lambda_q1": small((L, DA_QK_DIM), 0.1),
        "da_lambda_k1": small((L, DA_QK_DIM), 0.1),
        "da_lambda_q2": small((L, DA_QK_DIM), 0.1),
        "da_lambda_k2": small((L, DA_QK_DIM), 0.1),
        "da_subln": gain((L, DA_V_DIM)),
        "mla_q_norm": gain((L, MLA_Q_RANK)),
        "mla_w_uq": w((L, MLA_Q_RANK, MLA_HEADS * (MLA_NOPE + MLA_ROPE)), MLA_Q_RANK),
        "mla_kv_norm": gain((L, MLA_KV_RANK)),
        "mla_w_ukv": w((L, MLA_KV_RANK, MLA_HEADS * (MLA_NOPE + MLA_V)), MLA_KV_RANK),
        "w_o": w((L, MIX_W, D), MIX_W),
        "ffn2_norm": gain((L, D)),
        "ffn2_w1": w((L, D, D_FF), D),
        "ffn2_w3": w((L, D, D_FF), D),
        "ffn2_w2": w((L, D_FF, D), D_FF),
        "w_ada": w((L, D, N_MOD * D), D, 0.5),
        "b_ada": small((L, N_MOD * D), 0.02),
        "final_norm": gain((D,)),
    }


def reference(x_prompt, x_sample, c_prompt, c_sample, ffn1_norm, ffn1_w1, ffn1_w3, ffn1_w2,
              attn_norm, w_in, da_lambda_q1, da_lambda_k1, da_lambda_q2, da_lambda_k2, da_subln,
              mla_q_norm, mla_w_uq, mla_kv_norm, mla_w_ukv, w_o,
              ffn2_norm, ffn2_w1, ffn2_w3, ffn2_w2, w_ada, b_ada, final_norm):
    weights = (ffn1_norm, ffn1_w1, ffn1_w3, ffn1_w2, attn_norm, w_in,
               da_lambda_q1, da_lambda_k1, da_lambda_q2, da_lambda_k2, da_subln,
               mla_q_norm, mla_w_uq, mla_kv_norm, mla_w_ukv, w_o,
               ffn2_norm, ffn2_w1, ffn2_w3, ffn2_w2, w_ada, b_ada, final_norm)
    y_prompt = encoder(x_prompt, c_prompt, *weights)
    y_sample = encoder(x_sample, c_sample, *weights)
    return (y_prompt, y_sample)
```

```python
import contextlib
import math
import numpy as np
import concourse.bass as bass
import concourse.mybir as mybir
from concourse.bass_utils import run_bass_kernel_spmd

F32 = mybir.dt.float32
BF16 = mybir.dt.bfloat16
AF = mybir.ActivationFunctionType
ALU = mybir.AluOpType

EPS = 1e-6
ROPE_THETA = 500000.0
T = 512
NSUB = 4
QK = 64
DV = 128
NOPE = 128
ROPE = 64
DA_ROT = 16
VA = DV + 1


class Cfg:
    def __init__(self, D=2048, FF=5632, H=8, QR=512, KVR=256, n_cores=8, sp=16384, ss=8192):
        self.D, self.FF, self.H, self.QR, self.KVR = D, FF, H, QR, KVR
        self.n_cores = n_cores
        self.Sp, self.Ss = sp, ss
        self.NG = 1
        nq = sp // (n_cores // 2)
        assert 2 * nq == ss and nq % T == 0
        self.nq = nq
        self.S = (sp,)
        self.KC = D // 128
        self.FCN = FF // 128
        self.QKC = QR // 128
        self.KVKC = KVR // 128
        self.ntile = (sp // T,)
        self.own_tiles = nq // T
        assert FF % 256 == 0 and D % 512 == 0 and QR % 128 == 0 and KVR % 128 == 0
        assert self.FCN % 4 == 0
        self.W2Q = 4
        self.W2F = self.FCN // self.W2Q
        self.NMOD = 9 * D
        HW = H * 128
        self.segs = [("q", HW), ("k", HW), ("v", HW), ("cq", QR), ("ckv", KVR), ("kpe", ROPE)]
        self.slot_elems = max(self.KC * 256, self.W2F * 512, self.QKC * 512, self.KVKC * 512, 2 * H * 256)
        assert H % 4 == 0


class Ev:
    __slots__ = ("sem", "sid", "val")

    def __init__(self, sem, sid, val):
        self.sem, self.sid, self.val = sem, sid, val


class Buf:
    def __init__(self, K, name, dma=False):
        self.name = name
        self.w = {}
        self.r = {}


class Eng:
    def __init__(self, K, name, eng, compute=True):
        self.K, self.name, self.e = K, name, eng
        self.seen = {}
        self.sid = K.new_sid()
        if compute:
            self.sem = K.new_sem("e_" + name)
            self.cnt = 0
        self.pend_r = []

    def wait(self, ev, same_ok):
        if self.seen.get(ev.sid, 0) >= ev.val:
            return
        self.e.wait_ge(ev.sem, ev.val)
        self.seen[ev.sid] = ev.val


class Kern:
    def __init__(self, nc):
        self.nc = nc
        self.es = contextlib.ExitStack()
        self._sid = 0
        self.nsem = 0
        self.pe = Eng(self, "pe", nc.tensor)
        self.act = Eng(self, "act", nc.scalar)
        self.dve = Eng(self, "dve", nc.vector)
        self.pool = Eng(self, "pool", nc.gpsimd)
        self.sp = Eng(self, "sp", nc.sync, compute=False)
        self.dsems = {"sp": [[self.new_sem("dsp%d" % i), self.new_sid(), 0] for i in range(24)],
                      "pool": [[self.new_sem("dpl%d" % i), self.new_sid(), 0] for i in range(8)]}
        self.dnext = {"sp": 0, "pool": 0}
        self.dram_bufs = []

    def new_sid(self):
        self._sid += 1
        return self._sid

    def new_sem(self, name):
        self.nsem += 1
        return self.es.enter_context(self.nc.semaphore(name))

    def sbuf(self, name, shape, dt, es=None):
        return (es or self.es).enter_context(self.nc.sbuf_tensor(name, list(shape), dt))

    def psum(self, name, shape, dt, es=None):
        return (es or self.es).enter_context(self.nc.psum_tensor(name, list(shape), dt))

    def _waits(self, E, reads, writes):
        for b in reads:
            for ev in b.w.values():
                E.wait(ev, False)
        for b in writes:
            for ev in b.w.values():
                E.wait(ev, True)
            for ev in b.r.values():
                E.wait(ev, True)

    def op(self, E, fn, reads=(), writes=(), partial=False):
        self._waits(E, reads, writes)
        ins = fn(E.e)
        E.cnt += 1
        ins.then_inc(E.sem, 1)
        ev = Ev(E.sem, E.sid, E.cnt)
        for b in reads:
            b.r[E.sid] = ev
        for b in writes:
            if partial:
                b.w[E.sid] = ev
            else:
                b.w = {E.sid: ev}
            b.r = {}
        return ev

    def mm(self, bank, out, lhsT, rhs, start, stop, reads=(), mark=None, **kw):
        E = self.pe
        for b in reads:
            for ev in b.w.values():
                E.wait(ev, False)
        if start:
            for ev in bank.w.values():
                E.wait(ev, True)
            for ev in bank.r.values():
                E.wait(ev, True)
        ins = E.e.matmul(out, lhsT, rhs, start=start, stop=stop, **kw)
        for b in reads:
            if b not in E.pend_r:
                E.pend_r.append(b)
        if mark is None:
            mark = stop
        if mark:
            E.cnt += 1
            ins.then_inc(E.sem, 1)
            ev = Ev(E.sem, E.sid, E.cnt)
            for b in E.pend_r:
                b.r[E.sid] = ev
            E.pend_r = []
            if stop:
                bank.w = {E.sid: ev}
                bank.r = {}
        return ins

    def dma(self, Q, pairs, reads, writes, sb=None, partial=False, **kw):
        for b in reads:
            for ev in b.w.values():
                Q.wait(ev, False)
        for b in writes:
            for ev in list(b.w.values()) + list(b.r.values()):
                Q.wait(ev, False)
        evs = []
        for (o, i) in pairs:
            pool = self.dsems[Q.name]
            st = pool[self.dnext[Q.name] % len(pool)]
            self.dnext[Q.name] += 1
            if st[2] > 0:
                Q.wait(Ev(st[0], st[1], st[2]), False)
            ins = Q.e.dma_start(out=o, in_=i, **kw)
            st[2] += 16
            ins.then_inc(st[0], 16)
            evs.append(Ev(st[0], st[1], st[2]))
        for b in reads:
            for ev in evs:
                b.r[ev.sid] = ev
        for b in writes:
            if not partial:
                b.w = {}
            for ev in evs:
                b.w[ev.sid] = ev
            b.r = {}
        return evs


class Ring:
    def __init__(self, K, name, ns, elems, es):
        self.K = K
        self.ns = ns
        self.tiles = [K.sbuf(f"{name}{i}", [128, elems], BF16, es) for i in range(ns)]
        self.bufs = [Buf(K, f"{name}{i}", dma=True) for i in range(ns)]
        self.pieces = []
        self.issued = 0
        self.taken = 0
        self.consumed = 0

    def plan(self, src_ap, dbuf, elems):
        self.pieces.append((src_ap, dbuf, elems))

    def _issue(self):
        while self.issued < len(self.pieces) and self.issued - self.ns < self.consumed:
            i = self.issued
            src, dbuf, elems = self.pieces[i]
            s = i % self.ns
            self.K.dma(self.K.sp, [(self.tiles[s][:, 0:elems], src)], [dbuf], [self.bufs[s]])
            self.issued += 1

    def take(self):
        i = self.taken
        assert i < len(self.pieces), "ring plan exhausted"
        self._issue()
        assert i < self.issued, "ring too small for this consumption pattern"
        self.taken += 1
        s = i % self.ns
        return self.tiles[s], self.bufs[s]

    def done(self, n=1):
        self.consumed += n
        assert self.consumed <= self.taken
        self._issue()


def _pieces_kxn(w, ncols_piece):
    Kd, N = w.shape
    kc = Kd // 128
    npc = (N + ncols_piece - 1) // ncols_piece
    out = np.zeros((npc, 128, kc, ncols_piece), np.float32)
    for p in range(npc):
        c0 = p * ncols_piece
        c1 = min(N, c0 + ncols_piece)
        blk = w[:, c0:c1].reshape(kc, 128, c1 - c0).transpose(1, 0, 2)
        out[p, :, :, : c1 - c0] = blk
    return out.reshape(npc, 128, kc * ncols_piece)


def _pieces_w2(w2, cfg):
    FF, D = w2.shape
    ndb = D // 512
    out = np.zeros((ndb, cfg.W2Q, 128, cfg.W2F, 512), np.float32)
    w = w2.reshape(cfg.W2Q, cfg.W2F, 128, ndb, 512)
    out[:] = w.transpose(3, 0, 2, 1, 4)
    return out.reshape(ndb * cfg.W2Q, 128, cfg.W2F * 512)


def _rope_tables(pos):
    pos = pos.astype(np.float32)

    def tab(dim):
        inv = (np.float32(ROPE_THETA) ** (-np.arange(0, dim, 2, dtype=np.float32) / np.float32(dim))).astype(np.float32)
        ang = (pos[:, None] * inv[None, :]).astype(np.float32)
        c, s = np.cos(ang).astype(np.float32), np.sin(ang).astype(np.float32)
        return np.concatenate([c, c], 1), np.concatenate([-s, s], 1)

    c16, s16 = tab(DA_ROT)
    c64, s64 = tab(ROPE)
    return np.concatenate([c16, s16, c64, s64], 1).astype(np.float32)


def prepare_inputs(cfg, inp):
    D, H = cfg.D, cfg.H
    g = lambda k: np.asarray(inp[k], np.float32)
    shared = {}
    for name, tag in (("ffn1", "f1"), ("ffn2", "f2")):
        shared[f"{tag}_w1"] = _pieces_kxn(g(f"{name}_w1")[0], 256)
        shared[f"{tag}_w3"] = _pieces_kxn(g(f"{name}_w3")[0], 256)
        shared[f"{tag}_w2"] = _pieces_w2(g(f"{name}_w2")[0], cfg)
    w_in = g("w_in")[0]
    segs = []
    c0 = 0
    for (nm, n) in cfg.segs:
        segs.append(_pieces_kxn(w_in[:, c0:c0 + n], 256))
        c0 += n
    shared["w_in"] = np.concatenate(segs, 0)
    wuq = g("mla_w_uq")[0].reshape(cfg.QR, H, NOPE + ROPE)
    wuq = np.concatenate([wuq[:, :, :NOPE].reshape(cfg.QR, H * NOPE), wuq[:, :, NOPE:].reshape(cfg.QR, H * ROPE)], 1)
    shared["w_uq"] = np.concatenate([_pieces_kxn(wuq[:, :H * NOPE], 512), _pieces_kxn(wuq[:, H * NOPE:], 512)], 0)
    wukv = g("mla_w_ukv")[0].reshape(cfg.KVR, H, NOPE + DV)
    wukv = np.concatenate([wukv[:, :, :NOPE].reshape(cfg.KVR, H * NOPE), wukv[:, :, NOPE:].reshape(cfg.KVR, H * DV)], 1)
    shared["w_ukv"] = np.concatenate([_pieces_kxn(wukv[:, :H * NOPE], 512), _pieces_kxn(wukv[:, H * NOPE:], 512)], 0)
    shared["w_o"] = _pieces_kxn(g("w_o")[0], 256)
    shared["w_ada"] = _pieces_kxn(g("w_ada")[0], 512)
    shared["b_ada"] = np.ascontiguousarray(np.broadcast_to(g("b_ada")[0][None, :], (2, cfg.NMOD)))
    ncol = np.stack([g("ffn1_norm")[0], g("attn_norm")[0], g("ffn2_norm")[0]], 0)
    shared["ncol"] = np.ascontiguousarray(ncol.reshape(3, cfg.KC, 128).transpose(2, 0, 1))
    shared["final_norm"] = g("final_norm").reshape(1, D)
    shared["q_norm"] = g("mla_q_norm").reshape(1, cfg.QR)
    shared["kv_norm"] = g("mla_kv_norm").reshape(1, cfg.KVR)
    shared["subln"] = g("da_subln").reshape(1, DV)
    shared["lambdas"] = np.concatenate([g("da_lambda_q1")[0], g("da_lambda_k1")[0], g("da_lambda_q2")[0], g("da_lambda_k2")[0]]).reshape(1, 4 * QK)
    shared["ident"] = np.eye(128, dtype=np.float32)
    sel = np.zeros((2, 4, 128), np.float32)
    sel[0, 0] = 1.0
    sel[1, 1] = 1.0
    sel[0, 2] = 0.5
    sel[1, 3] = 0.5
    shared["sel"] = sel

    xp = g("x_prompt")
    xs = g("x_sample")
    cp = g("c_prompt")
    cs = g("c_sample")
    nq = cfg.nq
    Sp, Ss = cfg.Sp, cfg.Ss
    half = cfg.n_cores // 2
    maps = []
    for c in range(cfg.n_cores):
        m = dict(shared)
        xa = np.zeros((Sp, D), np.float32)
        pos = np.zeros((Sp,), np.int64)
        flag = np.zeros((Sp, 1), np.float32)
        if c < half:
            seq, cvec, j, L = xp[0], cp[0], c, Sp
        else:
            sq = (c - half) // 2
            seq, cvec, j, L = xs[sq], cs[sq], (c - half) % 2, Ss
        order = np.concatenate([np.arange(j * nq, (j + 1) * nq), np.arange(0, j * nq), np.arange((j + 1) * nq, L)])
        xa[:L] = seq[order]
        pos[:L] = order
        flag[:L] = 1.0
        m["x_p"] = xa
        m["rope_p"] = np.concatenate([_rope_tables(pos), flag], 1)
        m["kflag"] = np.ascontiguousarray(flag.reshape(Sp // 128, 128).T)
        cc = np.stack([cvec, cvec], 0)
        m["cT"] = np.ascontiguousarray(cc.reshape(2, cfg.KC, 128).transpose(2, 1, 0))
        maps.append(m)
    return maps


def cdiv(a, b):
    return (a + b - 1) // b


def build_program(cfg, debug=()):
    nc = bass.Bass("TRN2", target_bir_lowering=False)
    K = Kern(nc)
    D, FF, H, QR, KVR, KC, FCN = cfg.D, cfg.FF, cfg.H, cfg.QR, cfg.KVR, cfg.KC, cfg.FCN
    HW = H * 128
    nq = cfg.nq
    NDB = D // 512
    NMOD = cfg.NMOD
    QKC, KVKC = cfg.QKC, cfg.KVKC

    def din(name, shape, dt=F32):
        return nc.dram_tensor(name, list(shape), dt, kind="ExternalInput").ap()

    def dscr(name, shape, dt):
        if name in debug:
            return nc.dram_tensor(name, list(shape), dt, kind="ExternalOutput").ap()
        return nc.dram_tensor(name, list(shape), dt).ap()

    NP13 = FF // 256
    NPW2 = NDB * cfg.W2Q
    seg_np = [cdiv(n, 256) for (_, n) in cfg.segs]
    seg_base = [sum(seg_np[:i]) for i in range(len(seg_np))]
    NPIN = sum(seg_np)
    NUQ_N, NUQ_R = HW // 512, cdiv(H * ROPE, 512)
    NUKV = HW // 512
    wshapes = {
        "f1_w1": (NP13, KC * 256), "f1_w3": (NP13, KC * 256), "f1_w2": (NPW2, cfg.W2F * 512),
        "w_in": (NPIN, KC * 256), "w_uq": (NUQ_N + NUQ_R, QKC * 512), "w_ukv": (2 * NUKV, KVKC * 512),
        "w_o": (D // 256, 2 * H * 256),
        "f2_w1": (NP13, KC * 256), "f2_w3": (NP13, KC * 256), "f2_w2": (NPW2, cfg.W2F * 512),
    }
    w32 = {k: din(k, (v[0], 128, v[1])) for k, v in wshapes.items()}
    wb = {k: dscr("wb_" + k, (v[0], 128, v[1]), BF16) for k, v in wshapes.items()}
    wbB = {k: Buf(K, "wb_" + k) for k in wshapes}
    w_ada = din("w_ada", (NMOD // 512, 128, KC * 512))
    b_ada = din("b_ada", (2, NMOD))
    ncol_d = din("ncol", (128, 3, KC))
    final_norm_d = din("final_norm", (1, D))
    q_norm_d = din("q_norm", (1, QR))
    kv_norm_d = din("kv_norm", (1, KVR))
    subln_d = din("subln", (1, DV))
    lambdas_d = din("lambdas", (1, 4 * QK))
    ident_d = din("ident", (128, 128))
    sel_d = din("sel", (2, 4, 128))
    x_d = [din("x_p", (cfg.S[0], D))]
    rope_d = [din("rope_p", (cfg.S[0], 161))]
    cT_d = din("cT", (128, KC, 2))
    kflag_d = din("kflag", (128, cfg.S[0] // 128))
    y_d = [nc.dram_tensor(n, [nq, D], F32, kind="ExternalOutput").ap() for n in ("y_p",)]
    dbg = {}

    def dbg_out(name, shape, dt=F32):
        if name in dbg:
            return dbg[name]
        if name in debug:
            dbg[name] = nc.dram_tensor("dbg_" + name, list(shape), dt, kind="ExternalOutput").ap()
            return dbg[name]
        return None

    gbc = dscr("gbc", (3, 2, 128, D), F32)
    gbcB = Buf(K, "gbc")
    x1_d = [dscr(f"x1_{g}", (nq, D), F32) for g in range(cfg.NG)]
    x1B = [Buf(K, f"x1_{g}") for g in range(cfg.NG)]
    QT_da = [dscr(f"QT_da{g}", (H, 128, nq), BF16) for g in range(cfg.NG)]
    QT_nope = [dscr(f"QT_nope{g}", (H, 128, nq), BF16) for g in range(cfg.NG)]
    QT_pe = [dscr(f"QT_pe{g}", (H // 2, 128, nq), BF16) for g in range(cfg.NG)]
    KT_da = [dscr(f"KT_da{g}", (H, 128, cfg.S[g]), BF16) for g in range(cfg.NG)]
    KT_nope = [dscr(f"KT_nope{g}", (H, 128, cfg.S[g]), BF16) for g in range(cfg.NG)]
    KT_pe = [dscr(f"KT_pe{g}", (64, cfg.S[g]), BF16) for g in range(cfg.NG)]
    V_da = [dscr(f"V_da{g}", (H, 128, cfg.S[g] // 128, VA), BF16) for g in range(cfg.NG)]
    V_mla = [dscr(f"V_mla{g}", (H, 128, cfg.S[g] // 128, VA), BF16) for g in range(cfg.NG)]
    mixT = [dscr(f"mixT{g}", (2 * H, 128, nq), BF16) for g in range(cfg.NG)]
    QKVB = [Buf(K, f"qkv{g}") for g in range(cfg.NG)]
    mixB = [Buf(K, f"mix{g}") for g in range(cfg.NG)]
    XIN = Buf(K, "xin")

    ident32 = K.sbuf("ident32", [128, 128], F32)
    identb = K.sbuf("identb", [128, 128], BF16)
    modcol = K.sbuf("modcol", [128, 3, 2, KC, 2], F32)
    epsT = K.sbuf("epsT", [128, 1], F32)
    neglam = K.sbuf("neglam", [128, 1], F32)
    subg = K.sbuf("subg", [128, DV], F32)
    qng = K.sbuf("qng", [128, QR], F32)
    kvng = K.sbuf("kvng", [128, KVR], F32)
    ones_row = K.sbuf("ones_row", [1, 128], F32)
    CONST = Buf(K, "const", dma=True)
    MODC = Buf(K, "modcol")
    banks = [K.psum(f"bank{i}", [128, 512], F32) for i in range(8)]
    bankB = [Buf(K, f"bank{i}") for i in range(8)]
    rr_state = [0]

    def rr():
        i = rr_state[0] % 4
        rr_state[0] += 1
        return banks[i], bankB[i]

    acc = [(banks[4 + i], bankB[4 + i]) for i in range(4)]
    ev_state = [0]

    def evac_eng():
        ev_state[0] += 1
        return K.act if ev_state[0] % 2 else K.dve

    def copy_op(E, out, in_, reads, writes, partial=False):
        if E is K.act:
            return K.op(E, lambda e: e.copy(out=out, in_=in_), reads, writes, partial)
        return K.op(E, lambda e: e.tensor_copy(out=out, in_=in_), reads, writes, partial)

    def scale_copy_op(E, out, in_, sc, reads, writes):
        if E is K.act:
            return K.op(E, lambda e: e.activation(out=out, in_=in_, func=AF.Identity, scale=sc), reads, writes, True)
        return K.op(E, lambda e: e.tensor_scalar(out=out, in0=in_, scalar1=sc, scalar2=None, op0=ALU.mult), reads, writes, True)

    K.dma(K.sp, [(ident32[:], ident_d), (qng[:], q_norm_d.partition_broadcast(128)),
                 (kvng[:], kv_norm_d.partition_broadcast(128)), (subg[:], subln_d.partition_broadcast(128))],
          [XIN], [CONST], CONST)
    K.op(K.dve, lambda e: e.tensor_copy(out=identb[:], in_=ident32[:]), [CONST], [CONST], partial=True)
    K.op(K.dve, lambda e: e.memset(epsT[:], EPS), [], [CONST], partial=True)
    K.op(K.dve, lambda e: e.memset(ones_row[:], 1.0), [], [CONST], partial=True)
    K.op(K.dve, lambda e: e.tensor_scalar(out=subg[:], in0=subg[:], scalar1=0.8, scalar2=None, op0=ALU.mult), [CONST], [CONST], partial=True)

    with contextlib.ExitStack() as es:
        cTs = K.sbuf("cTs", [128, KC, 2], F32, es)
        siluT = K.sbuf("siluT", [128, KC, 2], F32, es)
        b2 = [K.sbuf(f"b2_{i}", [2, 512], F32, es) for i in range(2)]
        mrow = K.sbuf("mrow", [2, NMOD], F32, es)
        wblk = [K.sbuf(f"wblk{i}", [128, KC * 512], F32, es) for i in range(2)]
        wblkB = [Buf(K, f"wblk{i}", dma=True) for i in range(2)]
        ncols = K.sbuf("ncols", [128, 3, KC], F32, es)
        sels = K.sbuf("sels", [2, 4, 128], F32, es)
        mcol = K.sbuf("mcol", [128, 6, KC, 2], F32, es)
        gst = [K.sbuf(f"gst{i}", [128, D], F32, es) for i in range(2)]
        gstB = [Buf(K, f"gst{i}", dma=True) for i in range(2)]
        lambf = K.sbuf("lamb", [128, 4 * QK], F32, es)
        lamb = lambf[:].rearrange("p (a b) -> p a b", b=QK)
        prod = K.sbuf("prod", [128, 2, QK], F32, es)
        s12 = K.sbuf("s12", [128, 2], F32, es)
        MB = Buf(K, "mphase", dma=True)
        MROW = Buf(K, "mrow")
        K.dma(K.sp, [(cTs[:], cT_d), (ncols[:], ncol_d), (sels[:], sel_d),
                     (lambf[:], lambdas_d.partition_broadcast(128))], [XIN], [MB], MB)
        K.op(K.act, lambda e: e.activation(out=siluT[:], in_=cTs[:], func=AF.Silu), [MB], [MB], partial=True)
        NB = NMOD // 512
        for nb in range(NB):
            i = nb % 2
            K.dma(K.sp, [(wblk[i][:], w_ada[nb]), (b2[i][:], b_ada[:, nb * 512:(nb + 1) * 512])], [XIN], [wblkB[i]], wblkB[i])
            bk, bb = rr()
            for kc in range(KC):
                K.mm(bb, bk[0:2, 0:512], siluT[:, kc, :], wblk[i][:, kc * 512:(kc + 1) * 512], kc == 0, kc == KC - 1, reads=[wblkB[i], MB])
            K.op(K.dve, lambda e: e.tensor_tensor(out=mrow[:, nb * 512:(nb + 1) * 512], in0=bk[0:2, 0:512], in1=b2[i][:], op=ALU.add),
                 [bb, wblkB[i]], [MROW], partial=True)
        vec_idx = [0, 1, 3, 4, 6, 7]
        bk, bb = rr()
        for vi, v in enumerate(vec_idx):
            for kc in range(KC):
                o = (vi * KC + kc) * 2
                K.mm(bb, bk[:, o:o + 2], mrow[0:2, v * D + kc * 128: v * D + (kc + 1) * 128], ident32[0:2, 0:2], True, True, reads=[MROW, CONST])
        K.op(K.dve, lambda e: e.tensor_copy(out=mcol[:].rearrange("p a k g -> p (a k g)"), in_=bk[:, 0:6 * KC * 2]), [bb], [MB], partial=True)
        for l in range(3):
            K.op(K.dve, lambda e: e.scalar_tensor_tensor(out=modcol[:, l, 0, :, :], in0=mcol[:, 2 * l + 1, :, :], scalar=1.0,
                                                         in1=ncols[:, l, :].unsqueeze(2).broadcast_to([128, KC, 2]), op0=ALU.add, op1=ALU.mult),
                 [MB], [MODC], partial=True)
            K.op(K.dve, lambda e: e.tensor_copy(out=modcol[:, l, 1, :, :], in_=mcol[:, 2 * l, :, :]), [MB], [MODC], partial=True)
        gi = 0
        for l in range(3):
            v = 3 * l + 2
            for g in range(2):
                si = g + (0 if l == 1 else 2)
                st, sB = gst[gi % 2], gstB[gi % 2]
                gi += 1
                for db in range(NDB):
                    bk, bb = rr()
                    K.mm(bb, bk[:, 0:512], sels[0:2, si, :], mrow[0:2, v * D + db * 512: v * D + (db + 1) * 512], True, True, reads=[MROW, MB])
                    copy_op(evac_eng(), st[:, db * 512:(db + 1) * 512], bk[:, 0:512], [bb], [sB], partial=True)
                K.dma(K.pool, [(gbc[l, g], st[:])], [sB], [gbcB], sB, partial=True)
        K.op(K.dve, lambda e: e.tensor_tensor(out=prod[:, 0, :], in0=lamb[:, 0, :], in1=lamb[:, 1, :], op=ALU.mult), [MB], [MB], partial=True)
        K.op(K.dve, lambda e: e.tensor_tensor(out=prod[:, 1, :], in0=lamb[:, 2, :], in1=lamb[:, 3, :], op=ALU.mult), [MB], [MB], partial=True)
        K.op(K.dve, lambda e: e.reduce_sum(out=s12[:], in_=prod[:], axis=mybir.AxisListType.X), [MB], [MB], partial=True)
        K.op(K.act, lambda e: e.activation(out=s12[:], in_=s12[:], func=AF.Exp), [MB], [MB], partial=True)
        K.op(K.dve, lambda e: e.tensor_tensor(out=neglam[:], in0=s12[:, 1:2], in1=s12[:, 0:1], op=ALU.subtract), [MB], [CONST], partial=True)
        K.op(K.dve, lambda e: e.tensor_scalar(out=neglam[:], in0=neglam[:], scalar1=-0.2, scalar2=None, op0=ALU.add), [CONST], [CONST], partial=True)
        d = dbg_out("modcol", (128, 3 * 2 * KC * 2))
        if d is not None:
            K.dma(K.pool, [(d, modcol[:].rearrange("p l a k g -> p (l a k g)"))], [MODC], [], MB)
        d = dbg_out("neglam", (128, 1))
        if d is not None:
            K.dma(K.pool, [(d, neglam[:])], [CONST], [], MB)
        phase_barrier(K, [MB, MROW, MODC, CONST] + wblkB + gstB)

    with contextlib.ExitStack() as es:
        EMAX = max(v[1] for v in wshapes.values())
        NST = 3
        st32 = [K.sbuf(f"st32_{i}", [128, EMAX], F32, es) for i in range(NST)]
        st16 = [K.sbuf(f"st16_{i}", [128, EMAX], BF16, es) for i in range(NST)]
        s32B = [Buf(K, f"st32_{i}", dma=True) for i in range(NST)]
        s16B = [Buf(K, f"st16_{i}", dma=True) for i in range(NST)]
        ci = 0
        engs = [K.dve, K.act, K.pool]
        for name, (npc, E) in wshapes.items():
            for p in range(npc):
                i = ci % NST
                ci += 1
                K.dma(K.sp, [(st32[i][:, 0:E], w32[name][p])], [XIN], [s32B[i]], s32B[i])
                copy_op(engs[ci % 3], st16[i][:, 0:E], st32[i][:, 0:E], [s32B[i]], [s16B[i]])
                K.dma(K.pool, [(wb[name][p], st16[i][:, 0:E])], [s16B[i]], [wbB[name]], s16B[i], partial=True)
        phase_barrier(K, s32B + s16B)


    NSL = 4
    HWp = HW // 256
    BIGN = max(FCN * 512, 4 * HW + 2 * 4 * H * VA + 4 * KVR + 4 * ROPE + 4 * HW + D)

    def rms_rstd(ES, src_fn, n, nsub, reads, ssb, ssB, junk, JB):
        K.op(K.dve, lambda e: e.memset(ssb[:, 0:nsub], 0.0), [], [ssB])
        for s_ in range(nsub):
            K.op(K.act, lambda e: e.activation(out=junk[:, 0:n], in_=src_fn(s_), func=AF.Square, accum_out=ssb[:, s_:s_ + 1]),
                 reads + [ssB], [JB, ssB], partial=True)
        K.op(K.act, lambda e: e.activation(out=ssb[:, 0:nsub], in_=ssb[:, 0:nsub], func=AF.Sqrt, scale=1.0 / n, bias=epsT[:, 0:1]), [ssB, CONST], [ssB])
        K.op(K.dve, lambda e: e.reciprocal(out=ssb[:, 0:nsub], in_=ssb[:, 0:nsub]), [ssB], [ssB])

    def make_ffn_env(es, tag):
        env = {}
        env["xt"] = K.sbuf("xt" + tag, [128, NSUB, D], F32, es)
        env["X"] = Buf(K, "X" + tag)
        env["hT"] = K.sbuf("hT" + tag, [128, 2 * H if False else KC, T], BF16, es)
        env["HT"] = Buf(K, "HT" + tag)
        env["big"] = K.sbuf("big" + tag, [128, BIGN], BF16, es)
        env["GT"] = Buf(K, "GT" + tag)
        env["stmp"] = [K.sbuf(f"stmp{tag}{i}", [128, T], F32, es) for i in range(2)]
        env["STMP"] = [Buf(K, f"stmp{tag}{i}") for i in range(2)]
        env["ss"] = K.sbuf("ss" + tag, [128, NSUB], F32, es)
        env["SS"] = Buf(K, "ss" + tag)
        env["xs"] = [K.sbuf(f"xs{tag}{i}", [128, D], BF16, es) for i in range(2)]
        env["XS"] = [Buf(K, f"xs{tag}{i}") for i in range(2)]
        env["junk"] = env["xs"][0]
        env["JB"] = env["XS"][0]
        env["utmp"] = env["stmp"]
        env["UT"] = env["STMP"]
        env["cnt"] = 0
        return env

    def norm_to_hT(env, l, g):
        xt, X = env["xt"], env["X"]
        rms_rstd(None, lambda s_: xt[:, s_, :], D, NSUB, [X], env["ss"], env["SS"], env["junk"], env["JB"])
        first = True
        base = env["cnt"]
        env["cnt"] += NSUB

        def prescale(s_):
            i = (base + s_) % 2
            K.op(K.act, lambda e: e.activation(out=env["xs"][i][:], in_=xt[:, s_, :], func=AF.Identity, scale=env["ss"][:, s_:s_ + 1]), [X, env["SS"]], [env["XS"][i]])

        prescale(0)
        for s_ in range(NSUB):
            i = (base + s_) % 2
            xs, XS = env["xs"][i], env["XS"][i]
            bks = []
            for kg in range(KC // 4):
                bk, bb = rr()
                bks.append((bk, bb))
                for j in range(4):
                    kc = kg * 4 + j
                    K.mm(bb, bk[:, j * 128:(j + 1) * 128], xs[:, kc * 128:(kc + 1) * 128], identb[:], True, True, reads=[XS, CONST])
                if kg == 0 and s_ + 1 < NSUB:
                    prescale(s_ + 1)
            for kg, (bk, bb) in enumerate(bks):
                E = evac_eng()
                for j in range(4):
                    kc = kg * 4 + j
                    a_ap = modcol[:, l, 0, kc, g:g + 1]
                    b_ap = modcol[:, l, 1, kc, g:g + 1]
                    o_ap = env["hT"][:, kc, s_ * 128:(s_ + 1) * 128]
                    i_ap = bk[:, j * 128:(j + 1) * 128]
                    if E is K.act:
                        K.op(E, lambda e: e.activation(out=o_ap, in_=i_ap, func=AF.Identity, scale=a_ap, bias=b_ap), [bb, MODC], [env["HT"]], partial=not first)
                    else:
                        K.op(E, lambda e: e.tensor_scalar(out=o_ap, in0=i_ap, scalar1=a_ap, scalar2=b_ap, op0=ALU.mult, op1=ALU.add), [bb, MODC], [env["HT"]], partial=not first)
                    first = False

    def plan_ffn(ring, tag):
        for fp in range(NP13):
            ring.plan(wb[tag + "_w1"][fp], wbB[tag + "_w1"], KC * 256)
            ring.plan(wb[tag + "_w3"][fp], wbB[tag + "_w3"], KC * 256)
        for p in range(NPW2):
            ring.plan(wb[tag + "_w2"][p], wbB[tag + "_w2"], cfg.W2F * 512)

    def ffn(env, ring, gate_ap, GATEB):
        hT, HT, big, GT = env["hT"], env["HT"], env["big"], env["GT"]
        for fp in range(NP13):
            w1t, w1b = ring.take()
            w3t, w3b = ring.take()
            for fl in range(2):
                fc = fp * 2 + fl
                b1, B1 = rr()
                b3, B3 = rr()
                for kc in range(KC):
                    K.mm(B1, b1[:, 0:T], w1t[:, kc * 256 + fl * 128: kc * 256 + (fl + 1) * 128], hT[:, kc, :], kc == 0, kc == KC - 1, reads=[w1b, HT])
                for kc in range(KC):
                    K.mm(B3, b3[:, 0:T], w3t[:, kc * 256 + fl * 128: kc * 256 + (fl + 1) * 128], hT[:, kc, :], kc == 0, kc == KC - 1, reads=[w3b, HT])
                i = env["cnt"] % 2
                env["cnt"] += 1
                K.op(K.act, lambda e: e.activation(out=env["stmp"][i][:], in_=b1[:, 0:T], func=AF.Silu), [B1], [env["STMP"][i]])
                K.op(K.dve, lambda e: e.tensor_tensor(out=big[:, fc * T:(fc + 1) * T], in0=env["stmp"][i][:], in1=b3[:, 0:T], op=ALU.mult),
                     [env["STMP"][i], B3], [GT], partial=(fc > 0))
            ring.done(2)
        W2F = cfg.W2F
        for db in range(NDB):
            for q in range(cfg.W2Q):
                wt, wB = ring.take()
                for s_ in range(NSUB):
                    ab, AB = acc[s_]
                    for fl in range(W2F):
                        fc = q * W2F + fl
                        first = (q == 0 and fl == 0)
                        last = (q == cfg.W2Q - 1 and fl == W2F - 1)
                        K.mm(AB, ab[:, 0:512], big[:, fc * T + s_ * 128: fc * T + (s_ + 1) * 128], wt[:, fl * 512:(fl + 1) * 512], first, last,
                             reads=[wB, GT], mark=(last or (s_ == NSUB - 1 and fl == W2F - 1)))
                ring.done(1)
            for s_ in range(NSUB):
                ab, AB = acc[s_]
                i = env["cnt"] % 2
                env["cnt"] += 1
                K.op(K.dve, lambda e: e.tensor_tensor(out=env["utmp"][i][:], in0=ab[:, 0:512], in1=gate_ap[:, db * 512:(db + 1) * 512], op=ALU.mult),
                     [AB, GATEB], [env["UT"][i]])
                K.op(K.pool, lambda e: e.tensor_tensor(out=env["xt"][:, s_, db * 512:(db + 1) * 512], in0=env["xt"][:, s_, db * 512:(db + 1) * 512],
                                                        in1=env["utmp"][i][:], op=ALU.add),
                     [env["UT"][i], env["X"]], [env["X"]], partial=True)

    with contextlib.ExitStack() as es:
        env = make_ffn_env(es, "A")
        xt, X, hT, HT, big, GT = env["xt"], env["X"], env["hT"], env["HT"], env["big"], env["GT"]
        ring = Ring(K, "ringA", NSL, cfg.slot_elems, es)
        gate1 = K.sbuf("gate1", [128, D], F32, es)
        G1B = Buf(K, "gate1")
        ropet = K.sbuf("ropet", [128, NSUB, 161], F32, es)
        RP = Buf(K, "ropet")
        own_st = K.sbuf("own_st", [128, 4 * HW + 4 * QR + 4 * HW + 4 * H * ROPE], BF16, es)
        OWN = Buf(K, "own_st")
        cqnT = K.sbuf("cqnT", [128, QKC, T], BF16, es)
        CQT = Buf(K, "cqnT")
        ckvnT = K.sbuf("ckvnT", [128, KVKC, T], BF16, es)
        CKT = Buf(K, "ckvnT")
        NSTG = 4
        stg = [K.sbuf(f"stg{i}", [128, T], BF16, es) for i in range(NSTG)]
        STG = [Buf(K, f"stg{i}") for i in range(NSTG)]
        stg_i = [0]
        rtmp = [K.sbuf(f"rtmp{i}", [128, 512], F32, es) for i in range(2)]
        RT = [Buf(K, f"rtmp{i}") for i in range(2)]
        ss2 = K.sbuf("ss2", [128, NSUB], F32, es)
        SS2 = Buf(K, "ss2")
        qkt = [K.sbuf(f"qkt{i}", [128, 512], F32, es) for i in range(2)]
        QKT = [Buf(K, f"qkt{i}") for i in range(2)]
        o_ = [0]

        def carve(buf, n):
            a = buf[:, o_[0]:o_[0] + n]
            o_[0] += n
            return a
        kda_tok = carve(big, 4 * HW).rearrange("p (s n) -> p s n", s=NSUB)
        vda = carve(big, 4 * H * VA).rearrange("p (h s c) -> p h s c", s=NSUB, h=H)
        vmla = carve(big, 4 * H * VA).rearrange("p (h s c) -> p h s c", s=NSUB, h=H)
        ckvn = carve(big, 4 * KVR).rearrange("p (s n) -> p s n", s=NSUB)
        kpe_tok = carve(big, 4 * ROPE).rearrange("p (s n) -> p s n", s=NSUB)
        knope_tok = carve(big, 4 * HW).rearrange("p (s n) -> p s n", s=NSUB)
        o_[0] = 0
        qda_tok = carve(own_st, 4 * HW).rearrange("p (s n) -> p s n", s=NSUB)
        cqn = carve(own_st, 4 * QR).rearrange("p (s n) -> p s n", s=NSUB)
        qnope_tok = carve(own_st, 4 * HW).rearrange("p (s n) -> p s n", s=NSUB)
        qpe_tok = carve(own_st, 4 * H * ROPE).rearrange("p (s n) -> p s n", s=NSUB)

        tiles = []
        for g in range(cfg.NG):
            for ti in range(cfg.ntile[g]):
                tiles.append((g, ti, ti < cfg.own_tiles))
        for (g, ti, own) in tiles:
            plan_ffn(ring, "f1")
            for si, (nm, n) in enumerate(cfg.segs):
                if nm in ("q", "cq") and not own:
                    continue
                for p in range(seg_np[si]):
                    ring.plan(wb["w_in"][seg_base[si] + p], wbB["w_in"], KC * 256)
            if own:
                for p in range(NUQ_N + NUQ_R):
                    ring.plan(wb["w_uq"][p], wbB["w_uq"], QKC * 512)
            for p in range(2 * NUKV):
                ring.plan(wb["w_ukv"][p], wbB["w_ukv"], KVKC * 512)

        def rope_apply(bk, BB, ncol_blocks, blk, rot, dst, dstB, s_, tab_off):
            half = rot // 2
            src = bk[:, 0:ncol_blocks * blk].rearrange("p (b d) -> p b d", d=blk)
            Ct = ropet[:, s_, tab_off:tab_off + rot].unsqueeze(1).broadcast_to([128, ncol_blocks, rot])
            S1 = ropet[:, s_, tab_off + rot:tab_off + rot + half].unsqueeze(1).broadcast_to([128, ncol_blocks, half])
            S2 = ropet[:, s_, tab_off + rot + half:tab_off + 2 * rot].unsqueeze(1).broadcast_to([128, ncol_blocks, half])
            A = rtmp[0][:, 0:ncol_blocks * rot].rearrange("p (b d) -> p b d", d=rot)
            Bt = rtmp[1][:, 0:ncol_blocks * rot].rearrange("p (b d) -> p b d", d=rot)
            nops = int(([d.split(":")[1] for d in debug if d.startswith("ropeops:")] or ["4"])[0])
            if nops >= 1:
                K.op(K.dve, lambda e: e.tensor_tensor(out=A, in0=src[:, :, 0:rot], in1=Ct, op=ALU.mult), [BB, RP], [RT[0]])
            if nops >= 2:
                K.op(K.dve, lambda e: e.tensor_tensor(out=Bt[:, :, 0:half], in0=src[:, :, half:rot], in1=S1, op=ALU.mult), [BB, RP], [RT[1]])
            if nops >= 3:
                K.op(K.dve, lambda e: e.tensor_tensor(out=Bt[:, :, half:rot], in0=src[:, :, 0:half], in1=S2, op=ALU.mult), [BB, RP], [RT[1]], partial=True)
            if nops >= 4:
                K.op(K.dve, lambda e: e.tensor_tensor(out=dst[:, :, 0:rot], in0=A, in1=Bt, op=ALU.add), [RT[0], RT[1]], dstB, partial=True)

        def transpose_out(src_fn, nrows, dram_ap, reads, DST):
            bk, bb = rr()
            for s_ in range(NSUB):
                K.mm(bb, bk[0:nrows, s_ * 128:(s_ + 1) * 128], src_fn(s_), identb[:], True, True, reads=reads + [CONST])
            i = stg_i[0] % NSTG
            stg_i[0] += 1
            copy_op(evac_eng(), stg[i][0:nrows, :], bk[0:nrows, 0:T], [bb], [STG[i]])
            K.dma(K.sp, [(dram_ap, stg[i][0:nrows, :])], [STG[i]], DST, partial=True)

        stopA = [d.split(":")[1] for d in debug if d.startswith("stopA:")]
        stopA = stopA[0] if stopA else None

        def load_x(g, ti):
            K.dma(K.sp, [(xt[:], x_d[g][ti * T:(ti + 1) * T, :].rearrange("(s p) d -> p s d", p=128))], [XIN], [X])

        def do_tile(g, ti, own):
            t0 = ti * T
            if ti == 0:
                K.dma(K.sp, [(gate1[:], gbc[0, g])], [gbcB], [G1B])
            if (g, ti) == (0, 0):
                load_x(g, ti)
            K.dma(K.sp, [(ropet[:], rope_d[g][t0:t0 + T, :].rearrange("(s p) c -> p s c", p=128))], [XIN], [RP])
            if stopA == "load":
                return True
            norm_to_hT(env, 0, g)
            if stopA == "norm":
                return True
            if g == 0 and ti == 0:
                d = dbg_out("hT0", (128, KC * T), BF16)
                if d is not None:
                    K.dma(K.pool, [(d, hT[:].rearrange("p k t -> p (k t)"))], [HT], [])
            ffn(env, ring, gate1[:], G1B)
            if g == 0 and ti == 0:
                d = dbg_out("gt0", (128, FCN * T), BF16)
                if d is not None:
                    K.dma(K.pool, [(d, big[:, 0:FCN * T])], [GT], [])
                d = dbg_out("gate", (128, 2 * D))
                if d is not None:
                    K.dma(K.pool, [(d[:, 0:D], gate1[:])], [G1B], [])
            if stopA == "ffn":
                return True
            if own:
                K.dma(K.pool, [(x1_d[g][t0:t0 + T, :].rearrange("(s p) d -> p s d", p=128), xt[:])], [X], [x1B[g]], partial=True)
            if stopA == "x1s":
                return True
            norm_to_hT(env, 1, g)
            nxt = tiles.index((g, ti, own)) + 1
            if nxt < len(tiles) and stopA is None:
                load_x(tiles[nxt][0], tiles[nxt][1])
            if stopA == "norm2":
                return True
            BIGW = [GT]
            for s_ in range(NSUB):
                fl_b = ropet[:, s_, 160:161].unsqueeze(1).broadcast_to([128, H, 1])
                K.op(K.pool, lambda e: e.tensor_copy(out=vda[:, :, s_, DV:VA], in_=fl_b), [RP], BIGW, partial=(s_ > 0))
                K.op(K.pool, lambda e: e.tensor_copy(out=vmla[:, :, s_, DV:VA], in_=fl_b), [RP], BIGW, partial=True)
            if stopA == "memset":
                return True
            for si, (nm, n) in enumerate(cfg.segs):
                if stopA == "seg_" + nm:
                    return True
                if nm in ("q", "cq") and not own:
                    continue
                npc = seg_np[si]
                for p in range(npc):
                    wt, wB = ring.take()
                    ncols = min(256, n - p * 256)
                    for s_ in range(NSUB):
                        if nm in ("cq", "ckv"):
                            bk, bb = acc[s_]
                            o0 = p * 256
                        else:
                            bk, bb = rr()
                            o0 = 0
                        for kc in range(KC):
                            K.mm(bb, bk[:, o0:o0 + ncols], hT[:, kc, s_ * 128:(s_ + 1) * 128], wt[:, kc * 256: kc * 256 + ncols], kc == 0, kc == KC - 1, reads=[wB, HT])
                        if stopA == "q_mm":
                            return True
                        if nm in ("q", "k"):
                            dst = (qda_tok if nm == "q" else kda_tok)[:, s_, p * 256:(p + 1) * 256].rearrange("p (b d) -> p b d", d=QK)
                            dB = [OWN] if nm == "q" else BIGW
                            qi = stg_i[0] % 2
                            stg_i[0] += 1
                            copy_op(K.act, qkt[qi][:, 0:256], bk[:, 0:256], [bb], [QKT[qi]])
                            srcv = qkt[qi][:, 0:256].rearrange("p (b d) -> p b d", d=QK)
                            copy_op(K.pool, dst[:, :, DA_ROT:QK], srcv[:, :, DA_ROT:QK], [QKT[qi]], dB, partial=True)
                            if stopA == "q_copy":
                                return True
                            rope_apply(qkt[qi], QKT[qi], 4, QK, DA_ROT, dst, dB, s_, 0)
                            if stopA == "q_rope":
                                return True
                        elif nm == "v":
                            scale_copy_op(evac_eng(), vda[:, 2 * p:2 * p + 2, s_, 0:DV], bk[:, 0:256].rearrange("p (h c) -> p h c", c=DV), ropet[:, s_, 160:161], [bb, RP], BIGW)
                        elif nm == "kpe":
                            dst = kpe_tok[:, s_, :].rearrange("p (b d) -> p b d", d=ROPE)
                            qi = stg_i[0] % 2
                            stg_i[0] += 1
                            copy_op(K.act, qkt[qi][:, 0:ROPE], bk[:, 0:ROPE], [bb], [QKT[qi]])
                            rope_apply(qkt[qi], QKT[qi], 1, ROPE, ROPE, dst, BIGW, s_, 32)
                    ring.done(1)
                if nm in ("cq", "ckv"):
                    nn = QR if nm == "cq" else KVR
                    gt_ = qng if nm == "cq" else kvng
                    dstt = cqn if nm == "cq" else ckvn
                    dB = [OWN] if nm == "cq" else BIGW
                    rms_rstd(None, lambda s_: acc[s_][0][:, 0:nn], nn, NSUB, [acc[i][1] for i in range(NSUB)], ss2, SS2, env["junk"], env["JB"])
                    for s_ in range(NSUB):
                        K.op(K.dve, lambda e: e.scalar_tensor_tensor(out=dstt[:, s_, :], in0=acc[s_][0][:, 0:nn], scalar=ss2[:, s_:s_ + 1], in1=gt_[:, 0:nn],
                                                                     op0=ALU.mult, op1=ALU.mult), [acc[s_][1], SS2, CONST], dB, partial=True)
            if stopA == "proj":
                return True
            lat = [(ckvn, KVKC, ckvnT, CKT, BIGW)]
            if own:
                lat.append((cqn, QKC, cqnT, CQT, [OWN]))
            for (srct, nk, dstT, DTB, SB_) in lat:
                for kc in range(nk):
                    bk, bb = rr()
                    for s_ in range(NSUB):
                        K.mm(bb, bk[:, s_ * 128:(s_ + 1) * 128], srct[:, s_, kc * 128:(kc + 1) * 128], identb[:], True, True, reads=SB_ + [CONST])
                    copy_op(evac_eng(), dstT[:, kc, :], bk[:, 0:T], [bb], [DTB], partial=(kc > 0))
            if own:
                for p in range(NUQ_N + NUQ_R):
                    wt, wB = ring.take()
                    for s_ in range(NSUB):
                        bk, bb = rr()
                        for kc in range(QKC):
                            K.mm(bb, bk[:, 0:512], cqnT[:, kc, s_ * 128:(s_ + 1) * 128], wt[:, kc * 512:(kc + 1) * 512], kc == 0, kc == QKC - 1, reads=[wB, CQT])
                        if p < NUQ_N:
                            copy_op(evac_eng(), qnope_tok[:, s_, p * 512:(p + 1) * 512], bk[:, 0:512], [bb], [OWN], partial=True)
                        else:
                            pr = p - NUQ_N
                            nv = min(512, H * ROPE - pr * 512)
                            dst = qpe_tok[:, s_, pr * 512: pr * 512 + nv].rearrange("p (b d) -> p b d", d=ROPE)
                            qi = stg_i[0] % 2
                            stg_i[0] += 1
                            copy_op(K.act, qkt[qi][:, 0:nv], bk[:, 0:nv], [bb], [QKT[qi]])
                            rope_apply(qkt[qi], QKT[qi], nv // ROPE, ROPE, ROPE, dst, [OWN], s_, 32)
                    ring.done(1)
            for p in range(2 * NUKV):
                wt, wB = ring.take()
                for s_ in range(NSUB):
                    bk, bb = rr()
                    for kc in range(KVKC):
                        K.mm(bb, bk[:, 0:512], ckvnT[:, kc, s_ * 128:(s_ + 1) * 128], wt[:, kc * 512:(kc + 1) * 512], kc == 0, kc == KVKC - 1, reads=[wB, CKT])
                    if p < NUKV:
                        copy_op(evac_eng(), knope_tok[:, s_, p * 512:(p + 1) * 512], bk[:, 0:512], [bb], BIGW, partial=True)
                    else:
                        pv = p - NUKV
                        scale_copy_op(evac_eng(), vmla[:, 4 * pv:4 * pv + 4, s_, 0:DV], bk[:, 0:512].rearrange("p (h c) -> p h c", c=DV), ropet[:, s_, 160:161], [bb, RP], BIGW)
                ring.done(1)
            if stopA == "mla":
                return True
            for h in range(H):
                transpose_out(lambda s_: kda_tok[:, s_, h * 128:(h + 1) * 128], 128, KT_da[g][h, :, t0:t0 + T], BIGW, [QKVB[g]])
                transpose_out(lambda s_: knope_tok[:, s_, h * 128:(h + 1) * 128], 128, KT_nope[g][h, :, t0:t0 + T], BIGW, [QKVB[g]])
            transpose_out(lambda s_: kpe_tok[:, s_, :], ROPE, KT_pe[g][:, t0:t0 + T], BIGW, [QKVB[g]])
            K.dma(K.pool, [(V_da[g][:, :, ti * NSUB:(ti + 1) * NSUB, :].rearrange("h p s c -> p h s c"), vda),
                           (V_mla[g][:, :, ti * NSUB:(ti + 1) * NSUB, :].rearrange("h p s c -> p h s c"), vmla)],
                  BIGW, [QKVB[g]], partial=True)
            if own:
                for h in range(H):
                    transpose_out(lambda s_: qda_tok[:, s_, h * 128:(h + 1) * 128], 128, QT_da[g][h, :, t0:t0 + T], [OWN], [QKVB[g]])
                    transpose_out(lambda s_: qnope_tok[:, s_, h * 128:(h + 1) * 128], 128, QT_nope[g][h, :, t0:t0 + T], [OWN], [QKVB[g]])
                for hp in range(H // 2):
                    transpose_out(lambda s_: qpe_tok[:, s_, hp * 128:(hp + 1) * 128], 128, QT_pe[g][hp, :, t0:t0 + T], [OWN], [QKVB[g]])
            return False

        for (g, ti, own) in tiles:
            if do_tile(g, ti, own):
                break
        for nm_, aps in (("x1", x1_d), ):
            for g in range(cfg.NG):
                d = dbg_out(f"{nm_}{g}", (nq, D))
                if d is not None:
                    for ti in range(cfg.own_tiles):
                        K.dma(K.sp, [(xt[:], x1_d[g][ti * T:(ti + 1) * T, :].rearrange("(s p) d -> p s d", p=128))], [x1B[g]], [X])
                        K.dma(K.pool, [(d[ti * T:(ti + 1) * T, :].rearrange("(s p) d -> p s d", p=128), xt[:])], [X], [])
        phase_barrier(K, [X, HT, GT, OWN, CQT, CKT, RP, G1B, SS2, env["SS"], env["JB"]] + env["XS"] + STG + RT + QKT + env["STMP"] + env["UT"] + ring.bufs + bankB)


    with contextlib.ExitStack() as es:
        SMAX = max(cfg.S)
        NKTM = SMAX // 128
        NQT = nq // 512
        KTb = [K.sbuf(f"KTb{i}", [128, SMAX], BF16, es) for i in range(2)]
        Vb = [K.sbuf(f"Vb{i}", [128, NKTM, VA], BF16, es) for i in range(2)]
        NQB = 3
        qbuf = [K.sbuf(f"qbuf{i}", [128, 512], BF16, es) for i in range(NQB)]
        qpebuf = [K.sbuf(f"qpebuf{i}", [64, 512], BF16, es) for i in range(NQB)]
        QBUF = [Buf(K, f"qbuf{i}") for i in range(NQB)]
        kpeb = K.sbuf("kpeb", [64, SMAX], BF16, es)
        KVQ = [Buf(K, f"kvq{i}") for i in range(2)]
        KPE = Buf(K, "kpeb")
        NE = 6
        ebuf = [K.sbuf(f"ebuf{i}", [128, 512], BF16, es) for i in range(NE)]
        EB = [Buf(K, f"ebuf{i}") for i in range(NE)]
        accS = [K.sbuf(f"accS{i}", [128, 8, VA], F32, es) for i in range(2)]
        ACS = [Buf(K, f"accS{i}") for i in range(2)]
        rc = K.sbuf("rc", [128, 8], F32, es)
        rl = K.sbuf("rl", [128, 8], F32, es)
        RC = Buf(K, "rc")
        t1 = K.sbuf("t1", [128, DV], F32, es)
        T1 = Buf(K, "t1")
        o32 = K.sbuf("o32", [128, 4, DV], F32, es)
        O32 = Buf(K, "o32")
        ssq = K.sbuf("ssq", [128, 4], F32, es)
        SSQ = Buf(K, "ssq")
        junkB = K.sbuf("junkB", [128, DV], BF16, es)
        JKB = Buf(K, "junkB")
        obf = [K.sbuf(f"obf{i}", [128, 4, DV], BF16, es) for i in range(2)]
        OBF = [Buf(K, f"obf{i}") for i in range(2)]
        mstg = [K.sbuf(f"mstg{i}", [128, 512], BF16, es) for i in range(2)]
        kflag_sb = K.sbuf("kflag_sb", [128, SMAX // 128], F32, es)
        ones_mat = K.sbuf("ones_mat", [128, 128], F32, es)
        KF = Buf(K, "kflag")
        K.dma(K.sp, [(kflag_sb[:], kflag_d)], [XIN], [KF])
        K.op(K.pool, lambda e: e.memset(ones_mat[:], 1.0), [], [KF], partial=True)
        eacc = [K.sbuf(f"eacc{i}", [128, 512], F32, es) for i in range(2)]
        EACC = [Buf(K, f"eacc{i}") for i in range(2)]
        rbc = K.sbuf("rbc", [128, 512], F32, es)
        RBC = Buf(K, "rbc")
        MST = [Buf(K, f"mstg{i}") for i in range(2)]
        da_slots = {(0, 0): 0, (0, 1): 1, (0, 2): 2, (1, 0): 3, (1, 1): 4, (1, 2): 5, (0, 3): 6, (1, 3): 7}

        def acc_slot(sl):
            b = 4 + sl // 3
            o = (sl % 3) * VA
            return banks[b][:, o:o + VA], bankB[b]

        units = [(g, fam, h) for g in range(cfg.NG) for fam in ("da", "mla") for h in range(H)]

        def load_unit(u):
            g, fam, h = units[u]
            par = u % 2
            Sg = cfg.S[g]
            nkt = Sg // 128
            if fam == "mla" and h == 0:
                K.dma(K.sp, [(kpeb[0:64, 0:Sg], KT_pe[g])], [QKVB[g]], [KPE])
            if fam == "da":
                pairs = [(KTb[par][:, 0:Sg], KT_da[g][h]), (Vb[par][:, 0:nkt, :], V_da[g][h])]
            else:
                pairs = [(KTb[par][:, 0:Sg], KT_nope[g][h]), (Vb[par][:, 0:nkt, :], V_mla[g][h])]
            K.dma(K.sp, pairs, [QKVB[g]], [KVQ[par]])

        def load_q(n):
            if n >= len(units) * NQT:
                return
            u, qt = n // NQT, n % NQT
            g, fam, h = units[u]
            qi = n % NQB
            q0 = qt * 512
            if fam == "da":
                pairs = [(qbuf[qi][:], QT_da[g][h, :, q0:q0 + 512])]
            else:
                pairs = [(qbuf[qi][:], QT_nope[g][h, :, q0:q0 + 512]),
                         (qpebuf[qi][0:64, :], QT_pe[g][h // 2, (h % 2) * 64:(h % 2) * 64 + 64, q0:q0 + 512])]
            K.dma(K.sp, pairs, [QKVB[g]], [QBUF[qi]])

        steps = []
        for u, (g, fam, h) in enumerate(units):
            for qt in range(NQT):
                for kt in range(cfg.S[g] // 128):
                    steps.append((u, qt, kt))
        pending = []
        epi_cnt = [0]
        mst_i = [0]
        SC_MLA = float((NOPE + ROPE) ** -0.5)
        SC_DA = float(QK ** -0.5)

        def emit_qk(i):
            u, qt, kt = steps[i]
            g, fam, h = units[u]
            par = u % 2
            qi = (u * NQT + qt) % NQB
            if fam == "da":
                b0, b1 = 2 * (i % 2), 2 * (i % 2) + 1
                K.mm(bankB[b0], banks[b0][:, 0:512], KTb[par][0:64, kt * 128:(kt + 1) * 128], qbuf[qi][0:64, :], True, True, reads=[KVQ[par], QBUF[qi]])
                K.mm(bankB[b1], banks[b1][:, 0:512], KTb[par][64:128, kt * 128:(kt + 1) * 128], qbuf[qi][64:128, :], True, True, reads=[KVQ[par], QBUF[qi]])
            else:
                b0 = i % 4
                K.mm(bankB[b0], banks[b0][:, 0:512], KTb[par][:, kt * 128:(kt + 1) * 128], qbuf[qi][:], True, False, reads=[KVQ[par], QBUF[qi]])
                K.mm(bankB[b0], banks[b0][:, 0:512], kpeb[0:64, kt * 128:(kt + 1) * 128], qpebuf[qi][0:64, :], False, True, reads=[KVQ[par], KPE, QBUF[qi]])

        def emit_exp_pv(i):
            u, qt, kt = steps[i]
            g, fam, h = units[u]
            par = u % 2
            nkt = cfg.S[g] // 128
            if fam == "da":
                for j in range(2):
                    b = 2 * (i % 2) + j
                    ei = (2 * i + j) % NE
                    K.op(K.act, lambda e: e.activation(out=ebuf[ei][:], in_=banks[b][:, 0:512], func=AF.Exp, scale=SC_DA), [bankB[b]], [EB[ei]])
                for j in range(2):
                    ei = (2 * i + j) % NE
                    for qs in range(4):
                        sl = da_slots[(j, qs)]
                        ap, AB = acc_slot(sl)
                        K.mm(AB, ap, ebuf[ei][:, qs * 128:(qs + 1) * 128], Vb[par][:, kt, :], kt == 0 and sl in (0, 3, 6), kt == nkt - 1 and sl in (2, 5, 7),
                             reads=[EB[ei], KVQ[par]], mark=(qs == 3 or (kt == nkt - 1 and sl in (2, 5, 7))))
            else:
                b = i % 4
                ei = i % NE
                K.op(K.act, lambda e: e.activation(out=ebuf[ei][:], in_=banks[b][:, 0:512], func=AF.Exp, scale=SC_MLA), [bankB[b]], [EB[ei]])
                K.mm(bankB[4], banks[4][:, 0:512], Vb[par][:, kt, 0:DV], ebuf[ei][:], kt == 0, kt == nkt - 1, reads=[EB[ei], KVQ[par]], mark=True)
                a = (u * NQT + qt) % 2
                if kt == 0:
                    K.op(K.dve, lambda e: e.tensor_scalar(out=eacc[a][:], in0=ebuf[ei][:], scalar1=kflag_sb[:, 0:1], scalar2=None, op0=ALU.mult), [EB[ei], KF], [EACC[a]])
                else:
                    K.op(K.dve, lambda e: e.scalar_tensor_tensor(out=eacc[a][:], in0=ebuf[ei][:], scalar=kflag_sb[:, kt:kt + 1], in1=eacc[a][:], op0=ALU.mult, op1=ALU.add),
                         [EB[ei], KF, EACC[a]], [EACC[a]])
            if kt == nkt - 1:
                epilogue(i, u, qt)

        def epilogue(i, u, qt):
            g, fam, h = units[u]
            ep = epi_cnt[0] % 2
            epi_cnt[0] += 1
            if fam == "mla":
                a = (u * NQT + qt) % 2
                K.mm(bankB[7], banks[7][:, 0:512], ones_mat[:], eacc[a][:], True, True, reads=[EACC[a], KF])
                K.op(K.dve, lambda e: e.reciprocal(out=rbc[:], in_=banks[7][:, 0:512]), [bankB[7]], [RBC])
                mi = mst_i[0] % 2
                mst_i[0] += 1
                K.op(K.dve, lambda e: e.tensor_tensor(out=mstg[mi][:], in0=banks[4][:, 0:512], in1=rbc[:], op=ALU.mult), [bankB[4], RBC], [MST[mi]])
                K.dma(K.pool, [(mixT[g][H + h, :, qt * 512:(qt + 1) * 512], mstg[mi][:])], [MST[mi]], [mixB[g]], partial=True)
                return
            aS, AS = accS[ep], ACS[ep]
            nb = 3
            for b in range(nb):
                ns = 3 if b < 2 else 2
                K.op(K.dve, lambda e: e.tensor_copy(out=aS[:, 3 * b:3 * b + ns, :], in_=banks[4 + b][:, 0:ns * VA].rearrange("p (s c) -> p s c", c=VA)),
                     [bankB[4 + b]], [AS], partial=(b > 0))
            K.op(K.dve, lambda e: e.reciprocal(out=rc[:, 0:8], in_=aS[:, 0:8, DV]), [AS], [RC])
            ob, OB = obf[ep], OBF[ep]
            K.op(K.dve, lambda e: e.tensor_scalar(out=rl[:, 0:8], in0=rc[:, 0:8], scalar1=neglam[:, 0:1], scalar2=None, op0=ALU.mult), [RC, CONST], [RC], partial=True)
            for qs in range(4):
                s0, s1 = da_slots[(0, qs)], da_slots[(1, qs)]
                K.op(K.dve, lambda e: e.tensor_scalar(out=t1[:], in0=aS[:, s1, 0:DV], scalar1=rl[:, s1:s1 + 1], scalar2=None, op0=ALU.mult), [AS, RC], [T1])
                K.op(K.dve, lambda e: e.scalar_tensor_tensor(out=o32[:, qs, :], in0=aS[:, s0, 0:DV], scalar=rc[:, s0:s0 + 1], in1=t1[:], op0=ALU.mult, op1=ALU.add),
                     [AS, RC, T1], [O32], partial=(qs > 0))
            rms_rstd(None, lambda s_: o32[:, s_, :], DV, 4, [O32], ssq, SSQ, junkB, JKB)
            for qs in range(4):
                K.op(K.dve, lambda e: e.scalar_tensor_tensor(out=ob[:, qs, :], in0=o32[:, qs, :], scalar=ssq[:, qs:qs + 1], in1=subg[:], op0=ALU.mult, op1=ALU.mult),
                     [O32, SSQ, CONST], [OB], partial=(qs > 0))
            hh = h

            def fin():
                bk, bb = banks[7], bankB[7]
                for qs in range(4):
                    K.mm(bb, bk[:, qs * 128:(qs + 1) * 128], ob[:, qs, :], identb[:], True, True, reads=[OB, CONST])
                mi = mst_i[0] % 2
                mst_i[0] += 1
                K.op(K.dve, lambda e: e.tensor_copy(out=mstg[mi][:], in_=bk[:, 0:512]), [bb], [MST[mi]])
                K.dma(K.pool, [(mixT[g][hh, :, qt * 512:(qt + 1) * 512], mstg[mi][:])], [MST[mi]], [mixB[g]], partial=True)
            pending.append((i + 4, fin))

        load_unit(0)
        load_q(0)
        load_q(1)
        qk_next = [0]

        def ensure_qk(upto):
            while qk_next[0] <= min(upto, len(steps) - 1):
                emit_qk(qk_next[0])
                qk_next[0] += 1

        for i in range(len(steps)):
            u, qt, kt = steps[i]
            if qt == 0 and kt == 0 and u + 1 < len(units):
                load_unit(u + 1)
            if kt == 0:
                load_q(u * NQT + qt + 2)
            ensure_qk(i + (2 if units[u][1] == "mla" else 1))
            emit_exp_pv(i)
            while pending and pending[0][0] <= i:
                pending.pop(0)[1]()
        while pending:
            pending.pop(0)[1]()
        phase_barrier(K, KVQ + QBUF + [KPE, RC, T1, O32, SSQ, JKB, KF, RBC] + EACC + EB + ACS + OBF + MST + bankB)

    with contextlib.ExitStack() as es:
        env = make_ffn_env(es, "C")
        xt, X = env["xt"], env["X"]
        ring = Ring(K, "ringC", NSL, cfg.slot_elems, es)
        mT = K.sbuf("mT", [128, 2 * H, T], BF16, es)
        MT = Buf(K, "mT")
        gate23 = K.sbuf("gate23", [128, 2, D], F32, es)
        G23 = Buf(K, "gate23")
        fing = K.sbuf("fing", [128, D], F32, es)
        FING = Buf(K, "fing")
        K.dma(K.sp, [(fing[:], final_norm_d.partition_broadcast(128))], [XIN], [FING])
        ctiles = [(g, ti) for g in range(cfg.NG) for ti in range(cfg.own_tiles)]
        for _ in ctiles:
            for p in range(D // 256):
                ring.plan(wb["w_o"][p], wbB["w_o"], 2 * H * 256)
            plan_ffn(ring, "f2")
        for (g, ti) in ctiles:
            t0 = ti * T
            if ti == 0:
                K.dma(K.sp, [(gate23[:, l - 1, :], gbc[l, g]) for l in (1, 2)], [gbcB], [G23])
            K.dma(K.sp, [(xt[:], x1_d[g][t0:t0 + T, :].rearrange("(s p) d -> p s d", p=128))], [x1B[g]], [X])
            K.dma(K.sp, [(mT[:], mixT[g][:, :, t0:t0 + T].rearrange("h p t -> p h t"))], [mixB[g]], [MT])
            for p in range(D // 256):
                wt, wB = ring.take()
                for s_ in range(NSUB):
                    bk, bb = rr()
                    for kc in range(2 * H):
                        K.mm(bb, bk[:, 0:256], mT[:, kc, s_ * 128:(s_ + 1) * 128], wt[:, kc * 256:(kc + 1) * 256], kc == 0, kc == 2 * H - 1, reads=[wB, MT])
                    i = env["cnt"] % 2
                    env["cnt"] += 1
                    K.op(K.dve, lambda e: e.tensor_tensor(out=env["utmp"][i][:, 0:256], in0=bk[:, 0:256], in1=gate23[:, 0, p * 256:(p + 1) * 256], op=ALU.mult),
                         [bb, G23], [env["UT"][i]])
                    K.op(K.pool, lambda e: e.tensor_tensor(out=xt[:, s_, p * 256:(p + 1) * 256], in0=xt[:, s_, p * 256:(p + 1) * 256], in1=env["utmp"][i][:, 0:256], op=ALU.add),
                         [env["UT"][i], X], [X], partial=True)
                ring.done(1)
            d = dbg_out(f"x2{g}", (nq, D))
            if d is not None:
                K.dma(K.pool, [(d[t0:t0 + T, :].rearrange("(s p) d -> p s d", p=128), xt[:])], [X], [])
            norm_to_hT(env, 2, g)
            ffn(env, ring, gate23[:, 1, :], G23)
            d = dbg_out(f"x3{g}", (nq, D))
            if d is not None:
                K.dma(K.pool, [(d[t0:t0 + T, :].rearrange("(s p) d -> p s d", p=128), xt[:])], [X], [])
            rms_rstd(None, lambda s_: xt[:, s_, :], D, NSUB, [X], env["ss"], env["SS"], env["junk"], env["JB"])
            for s_ in range(NSUB):
                K.op(K.dve, lambda e: e.scalar_tensor_tensor(out=xt[:, s_, :], in0=xt[:, s_, :], scalar=env["ss"][:, s_:s_ + 1], in1=fing[:], op0=ALU.mult, op1=ALU.mult),
                     [X, env["SS"], FING], [X], partial=True)
            K.dma(K.pool, [(y_d[g][t0:t0 + T, :].rearrange("(s p) d -> p s d", p=128), xt[:])], [X], [])
        phase_barrier(K, [X, env["HT"], env["GT"], MT, G23, FING, env["SS"], env["JB"]] + env["XS"] + env["STMP"] + env["UT"] + ring.bufs + bankB)

    return nc, K, locals()


def phase_barrier(K, bufs):
    for E in (K.pe, K.act, K.dve, K.pool, K.sp):
        for b in bufs:
            for ev in list(b.w.values()) + list(b.r.values()):
                E.wait(ev, False)


_CACHE = {}


def kernel(**inputs):
    cfg = Cfg()
    if "nc" not in _CACHE:
        _CACHE["nc"] = build_program(cfg)[0]
    nc = _CACHE["nc"]
    maps = prepare_inputs(cfg, inputs)
    res = run_bass_kernel_spmd(nc, maps, core_ids=list(range(cfg.n_cores)))
    nq = cfg.nq
    half = cfg.n_cores // 2
    y_p = np.zeros((1, cfg.Sp, cfg.D), np.float32)
    y_s = np.zeros((half // 2, cfg.Ss, cfg.D), np.float32)
    for c in range(cfg.n_cores):
        r = res.results[c]["y_p"]
        if c < half:
            y_p[0, c * nq:(c + 1) * nq] = r
        else:
            sq, j = (c - half) // 2, (c - half) % 2
            y_s[sq, j * nq:(j + 1) * nq] = r
    return (y_p, y_s)
```

```python
import contextlib
import math
import numpy as np
import concourse.bass as bass
import concourse.mybir as mybir
from concourse.bass_utils import run_bass_kernel_spmd

F32 = mybir.dt.float32
BF16 = mybir.dt.bfloat16
AF = mybir.ActivationFunctionType
ALU = mybir.AluOpType

EPS = 1e-6
ROPE_THETA = 500000.0
T = 512
NSUB = 4
QK = 64
DV = 128
NOPE = 128
ROPE = 64
DA_ROT = 16
VA = DV + 1


class Cfg:
    def __init__(self, D=2048, FF=5632, H=8, QR=512, KVR=256, n_cores=8, sp=16384, ss=8192):
        self.D, self.FF, self.H, self.QR, self.KVR = D, FF, H, QR, KVR
        self.n_cores = n_cores
        self.Sp, self.Ss = sp, ss
        self.NG = 1
        nq = sp // (n_cores // 2)
        assert 2 * nq == ss and nq % T == 0
        self.nq = nq
        self.S = (sp,)
        self.KC = D // 128
        self.FCN = FF // 128
        self.QKC = QR // 128
        self.KVKC = KVR // 128
        self.ntile = (sp // T,)
        self.own_tiles = nq // T
        assert FF % 256 == 0 and D % 512 == 0 and QR % 128 == 0 and KVR % 128 == 0
        assert self.FCN % 4 == 0
        self.W2Q = 4
        self.W2F = self.FCN // self.W2Q
        self.NMOD = 9 * D
        HW = H * 128
        self.segs = [("q", HW), ("k", HW), ("v", HW), ("cq", QR), ("ckv", KVR), ("kpe", ROPE)]
        self.slot_elems = max(self.KC * 256, self.W2F * 512, self.QKC * 512, self.KVKC * 512, 2 * H * 256)
        assert H % 4 == 0


class Ev:
    __slots__ = ("sem", "sid", "val")

    def __init__(self, sem, sid, val):
        self.sem, self.sid, self.val = sem, sid, val


class Buf:
    def __init__(self, K, name, dma=False):
        self.name = name
        self.w = {}
        self.r = {}


class Eng:
    def __init__(self, K, name, eng, compute=True):
        self.K, self.name, self.e = K, name, eng
        self.seen = {}
        self.sid = K.new_sid()
        if compute:
            self.sem = K.new_sem("e_" + name)
            self.cnt = 0
        self.pend_r = []

    def wait(self, ev, same_ok):
        if self.seen.get(ev.sid, 0) >= ev.val:
            return
        self.e.wait_ge(ev.sem, ev.val)
        self.seen[ev.sid] = ev.val


class Kern:
    def __init__(self, nc):
        self.nc = nc
        self.es = contextlib.ExitStack()
        self._sid = 0
        self.nsem = 0
        self.pe = Eng(self, "pe", nc.tensor)
        self.act = Eng(self, "act", nc.scalar)
        self.dve = Eng(self, "dve", nc.vector)
        self.pool = Eng(self, "pool", nc.gpsimd)
        self.sp = Eng(self, "sp", nc.sync, compute=False)
        self.dsems = {"sp": [[self.new_sem("dsp%d" % i), self.new_sid(), 0] for i in range(24)],
                      "pool": [[self.new_sem("dpl%d" % i), self.new_sid(), 0] for i in range(8)]}
        self.dnext = {"sp": 0, "pool": 0}
        self.dram_bufs = []

    def new_sid(self):
        self._sid += 1
        return self._sid

    def new_sem(self, name):
        self.nsem += 1
        return self.es.enter_context(self.nc.semaphore(name))

    def sbuf(self, name, shape, dt, es=None):
        return (es or self.es).enter_context(self.nc.sbuf_tensor(name, list(shape), dt))

    def psum(self, name, shape, dt, es=None):
        return (es or self.es).enter_context(self.nc.psum_tensor(name, list(shape), dt))

    def _waits(self, E, reads, writes):
        for b in reads:
            for ev in b.w.values():
                E.wait(ev, False)
        for b in writes:
            for ev in b.w.values():
                E.wait(ev, True)
            for ev in b.r.values():
                E.wait(ev, True)

    def op(self, E, fn, reads=(), writes=(), partial=False):
        self._waits(E, reads, writes)
        ins = fn(E.e)
        E.cnt += 1
        ins.then_inc(E.sem, 1)
        ev = Ev(E.sem, E.sid, E.cnt)
        for b in reads:
            b.r[E.sid] = ev
        for b in writes:
            if partial:
                b.w[E.sid] = ev
            else:
                b.w = {E.sid: ev}
            b.r = {}
        return ev

    def mm(self, bank, out, lhsT, rhs, start, stop, reads=(), mark=None, **kw):
        E = self.pe
        for b in reads:
            for ev in b.w.values():
                E.wait(ev, False)
        if start:
            for ev in bank.w.values():
                E.wait(ev, True)
            for ev in bank.r.values():
                E.wait(ev, True)
        ins = E.e.matmul(out, lhsT, rhs, start=start, stop=stop, **kw)
        for b in reads:
            if b not in E.pend_r:
                E.pend_r.append(b)
        if mark is None:
            mark = stop
        if mark:
            E.cnt += 1
            ins.then_inc(E.sem, 1)
            ev = Ev(E.sem, E.sid, E.cnt)
            for b in E.pend_r:
                b.r[E.sid] = ev
            E.pend_r = []
            if stop:
                bank.w = {E.sid: ev}
                bank.r = {}
        return ins

    def dma(self, Q, pairs, reads, writes, sb=None, partial=False, **kw):
        for b in reads:
            for ev in b.w.values():
                Q.wait(ev, False)
        for b in writes:
            for ev in list(b.w.values()) + list(b.r.values()):
                Q.wait(ev, False)
        evs = []
        for (o, i) in pairs:
            pool = self.dsems[Q.name]
            st = pool[self.dnext[Q.name] % len(pool)]
            self.dnext[Q.name] += 1
            if st[2] > 0:
                Q.wait(Ev(st[0], st[1], st[2]), False)
            ins = Q.e.dma_start(out=o, in_=i, **kw)
            st[2] += 16
            ins.then_inc(st[0], 16)
            evs.append(Ev(st[0], st[1], st[2]))
        for b in reads:
            for ev in evs:
                b.r[ev.sid] = ev
        for b in writes:
            if not partial:
                b.w = {}
            for ev in evs:
                b.w[ev.sid] = ev
            b.r = {}
        return evs


class Ring:
    def __init__(self, K, name, ns, elems, es):
        self.K = K
        self.ns = ns
        self.tiles = [K.sbuf(f"{name}{i}", [128, elems], BF16, es) for i in range(ns)]
        self.bufs = [Buf(K, f"{name}{i}", dma=True) for i in range(ns)]
        self.pieces = []
        self.issued = 0
        self.taken = 0
        self.consumed = 0

    def plan(self, src_ap, dbuf, elems):
        self.pieces.append((src_ap, dbuf, elems))

    def _issue(self):
        while self.issued < len(self.pieces) and self.issued - self.ns < self.consumed:
            i = self.issued
            src, dbuf, elems = self.pieces[i]
            s = i % self.ns
            self.K.dma(self.K.sp, [(self.tiles[s][:, 0:elems], src)], [dbuf], [self.bufs[s]])
            self.issued += 1

    def take(self):
        i = self.taken
        assert i < len(self.pieces), "ring plan exhausted"
        self._issue()
        assert i < self.issued, "ring too small for this consumption pattern"
        self.taken += 1
        s = i % self.ns
        return self.tiles[s], self.bufs[s]

    def done(self, n=1):
        self.consumed += n
        assert self.consumed <= self.taken
        self._issue()


def _pieces_kxn(w, ncols_piece):
    Kd, N = w.shape
    kc = Kd // 128
    npc = (N + ncols_piece - 1) // ncols_piece
    out = np.zeros((npc, 128, kc, ncols_piece), np.float32)
    for p in range(npc):
        c0 = p * ncols_piece
        c1 = min(N, c0 + ncols_piece)
        blk = w[:, c0:c1].reshape(kc, 128, c1 - c0).transpose(1, 0, 2)
        out[p, :, :, : c1 - c0] = blk
    return out.reshape(npc, 128, kc * ncols_piece)


def _pieces_w2(w2, cfg):
    FF, D = w2.shape
    ndb = D // 512
    out = np.zeros((ndb, cfg.W2Q, 128, cfg.W2F, 512), np.float32)
    w = w2.reshape(cfg.W2Q, cfg.W2F, 128, ndb, 512)
    out[:] = w.transpose(3, 0, 2, 1, 4)
    return out.reshape(ndb * cfg.W2Q, 128, cfg.W2F * 512)


def _rope_tables(pos):
    pos = pos.astype(np.float32)

    def tab(dim):
        inv = (np.float32(ROPE_THETA) ** (-np.arange(0, dim, 2, dtype=np.float32) / np.float32(dim))).astype(np.float32)
        ang = (pos[:, None] * inv[None, :]).astype(np.float32)
        c, s = np.cos(ang).astype(np.float32), np.sin(ang).astype(np.float32)
        return np.concatenate([c, c], 1), np.concatenate([-s, s], 1)

    c16, s16 = tab(DA_ROT)
    c64, s64 = tab(ROPE)
    return np.concatenate([c16, s16, c64, s64], 1).astype(np.float32)


def prepare_inputs(cfg, inp):
    D, H = cfg.D, cfg.H
    g = lambda k: np.asarray(inp[k], np.float32)
    shared = {}
    for name, tag in (("ffn1", "f1"), ("ffn2", "f2")):
        shared[f"{tag}_w1"] = _pieces_kxn(g(f"{name}_w1")[0], 256)
        shared[f"{tag}_w3"] = _pieces_kxn(g(f"{name}_w3")[0], 256)
        shared[f"{tag}_w2"] = _pieces_w2(g(f"{name}_w2")[0], cfg)
    w_in = g("w_in")[0]
    segs = []
    c0 = 0
    for (nm, n) in cfg.segs:
        segs.append(_pieces_kxn(w_in[:, c0:c0 + n], 256))
        c0 += n
    shared["w_in"] = np.concatenate(segs, 0)
    wuq = g("mla_w_uq")[0].reshape(cfg.QR, H, NOPE + ROPE)
    wuq = np.concatenate([wuq[:, :, :NOPE].reshape(cfg.QR, H * NOPE), wuq[:, :, NOPE:].reshape(cfg.QR, H * ROPE)], 1)
    shared["w_uq"] = np.concatenate([_pieces_kxn(wuq[:, :H * NOPE], 512), _pieces_kxn(wuq[:, H * NOPE:], 512)], 0)
    wukv = g("mla_w_ukv")[0].reshape(cfg.KVR, H, NOPE + DV)
    wukv = np.concatenate([wukv[:, :, :NOPE].reshape(cfg.KVR, H * NOPE), wukv[:, :, NOPE:].reshape(cfg.KVR, H * DV)], 1)
    shared["w_ukv"] = np.concatenate([_pieces_kxn(wukv[:, :H * NOPE], 512), _pieces_kxn(wukv[:, H * NOPE:], 512)], 0)
    shared["w_o"] = _pieces_kxn(g("w_o")[0], 256)
    shared["w_ada"] = _pieces_kxn(g("w_ada")[0], 512)
    shared["b_ada"] = np.ascontiguousarray(np.broadcast_to(g("b_ada")[0][None, :], (2, cfg.NMOD)))
    ncol = np.stack([g("ffn1_norm")[0], g("attn_norm")[0], g("ffn2_norm")[0]], 0)
    shared["ncol"] = np.ascontiguousarray(ncol.reshape(3, cfg.KC, 128).transpose(2, 0, 1))
    shared["final_norm"] = g("final_norm").reshape(1, D)
    shared["q_norm"] = g("mla_q_norm").reshape(1, cfg.QR)
    shared["kv_norm"] = g("mla_kv_norm").reshape(1, cfg.KVR)
    shared["subln"] = g("da_subln").reshape(1, DV)
    shared["lambdas"] = np.concatenate([g("da_lambda_q1")[0], g("da_lambda_k1")[0], g("da_lambda_q2")[0], g("da_lambda_k2")[0]]).reshape(1, 4 * QK)
    shared["ident"] = np.eye(128, dtype=np.float32)
    sel = np.zeros((2, 4, 128), np.float32)
    sel[0, 0] = 1.0
    sel[1, 1] = 1.0
    sel[0, 2] = 0.5
    sel[1, 3] = 0.5
    shared["sel"] = sel

    xp = g("x_prompt")
    xs = g("x_sample")
    cp = g("c_prompt")
    cs = g("c_sample")
    nq = cfg.nq
    Sp, Ss = cfg.Sp, cfg.Ss
    half = cfg.n_cores // 2
    maps = []
    for c in range(cfg.n_cores):
        m = dict(shared)
        xa = np.zeros((Sp, D), np.float32)
        pos = np.zeros((Sp,), np.int64)
        flag = np.zeros((Sp, 1), np.float32)
        if c < half:
            seq, cvec, j, L = xp[0], cp[0], c, Sp
        else:
            sq = (c - half) // 2
            seq, cvec, j, L = xs[sq], cs[sq], (c - half) % 2, Ss
        order = np.concatenate([np.arange(j * nq, (j + 1) * nq), np.arange(0, j * nq), np.arange((j + 1) * nq, L)])
        xa[:L] = seq[order]
        pos[:L] = order
        flag[:L] = 1.0
        m["x_p"] = xa
        m["rope_p"] = np.concatenate([_rope_tables(pos), flag], 1)
        cc = np.stack([cvec, cvec], 0)
        m["cT"] = np.ascontiguousarray(cc.reshape(2, cfg.KC, 128).transpose(2, 1, 0))
        maps.append(m)
    return maps


def cdiv(a, b):
    return (a + b - 1) // b


def build_program(cfg, debug=()):
    nc = bass.Bass("TRN2", target_bir_lowering=False)
    K = Kern(nc)
    D, FF, H, QR, KVR, KC, FCN = cfg.D, cfg.FF, cfg.H, cfg.QR, cfg.KVR, cfg.KC, cfg.FCN
    HW = H * 128
    nq = cfg.nq
    NDB = D // 512
    NMOD = cfg.NMOD
    QKC, KVKC = cfg.QKC, cfg.KVKC

    def din(name, shape, dt=F32):
        return nc.dram_tensor(name, list(shape), dt, kind="ExternalInput").ap()

    def dscr(name, shape, dt):
        if name in debug:
            return nc.dram_tensor(name, list(shape), dt, kind="ExternalOutput").ap()
        return nc.dram_tensor(name, list(shape), dt).ap()

    NP13 = FF // 256
    NPW2 = NDB * cfg.W2Q
    seg_np = [cdiv(n, 256) for (_, n) in cfg.segs]
    seg_base = [sum(seg_np[:i]) for i in range(len(seg_np))]
    NPIN = sum(seg_np)
    NUQ_N, NUQ_R = HW // 512, cdiv(H * ROPE, 512)
    NUKV = HW // 512
    wshapes = {
        "f1_w1": (NP13, KC * 256), "f1_w3": (NP13, KC * 256), "f1_w2": (NPW2, cfg.W2F * 512),
        "w_in": (NPIN, KC * 256), "w_uq": (NUQ_N + NUQ_R, QKC * 512), "w_ukv": (2 * NUKV, KVKC * 512),
        "w_o": (D // 256, 2 * H * 256),
        "f2_w1": (NP13, KC * 256), "f2_w3": (NP13, KC * 256), "f2_w2": (NPW2, cfg.W2F * 512),
    }
    w32 = {k: din(k, (v[0], 128, v[1])) for k, v in wshapes.items()}
    wb = {k: dscr("wb_" + k, (v[0], 128, v[1]), BF16) for k, v in wshapes.items()}
    wbB = {k: Buf(K, "wb_" + k) for k in wshapes}
    w_ada = din("w_ada", (NMOD // 512, 128, KC * 512))
    b_ada = din("b_ada", (2, NMOD))
    ncol_d = din("ncol", (128, 3, KC))
    final_norm_d = din("final_norm", (1, D))
    q_norm_d = din("q_norm", (1, QR))
    kv_norm_d = din("kv_norm", (1, KVR))
    subln_d = din("subln", (1, DV))
    lambdas_d = din("lambdas", (1, 4 * QK))
    ident_d = din("ident", (128, 128))
    sel_d = din("sel", (2, 4, 128))
    x_d = [din("x_p", (cfg.S[0], D))]
    rope_d = [din("rope_p", (cfg.S[0], 161))]
    cT_d = din("cT", (128, KC, 2))
    y_d = [nc.dram_tensor(n, [nq, D], F32, kind="ExternalOutput").ap() for n in ("y_p",)]
    dbg = {}

    def dbg_out(name, shape, dt=F32):
        if name in dbg:
            return dbg[name]
        if name in debug:
            dbg[name] = nc.dram_tensor("dbg_" + name, list(shape), dt, kind="ExternalOutput").ap()
            return dbg[name]
        return None

    gbc = dscr("gbc", (3, 2, 128, D), F32)
    gbcB = Buf(K, "gbc")
    x1_d = [dscr(f"x1_{g}", (nq, D), F32) for g in range(cfg.NG)]
    x1B = [Buf(K, f"x1_{g}") for g in range(cfg.NG)]
    QT_da = [dscr(f"QT_da{g}", (H, 128, nq), BF16) for g in range(cfg.NG)]
    QT_nope = [dscr(f"QT_nope{g}", (H, 128, nq), BF16) for g in range(cfg.NG)]
    QT_pe = [dscr(f"QT_pe{g}", (H // 2, 128, nq), BF16) for g in range(cfg.NG)]
    KT_da = [dscr(f"KT_da{g}", (H, 128, cfg.S[g]), BF16) for g in range(cfg.NG)]
    KT_nope = [dscr(f"KT_nope{g}", (H, 128, cfg.S[g]), BF16) for g in range(cfg.NG)]
    KT_pe = [dscr(f"KT_pe{g}", (64, cfg.S[g]), BF16) for g in range(cfg.NG)]
    V_da = [dscr(f"V_da{g}", (H, 128, cfg.S[g] // 128, VA), BF16) for g in range(cfg.NG)]
    V_mla = [dscr(f"V_mla{g}", (H, 128, cfg.S[g] // 128, VA), BF16) for g in range(cfg.NG)]
    mixT = [dscr(f"mixT{g}", (2 * H, 128, nq), BF16) for g in range(cfg.NG)]
    QKVB = [Buf(K, f"qkv{g}") for g in range(cfg.NG)]
    mixB = [Buf(K, f"mix{g}") for g in range(cfg.NG)]
    XIN = Buf(K, "xin")

    ident32 = K.sbuf("ident32", [128, 128], F32)
    identb = K.sbuf("identb", [128, 128], BF16)
    modcol = K.sbuf("modcol", [128, 3, 2, KC, 2], F32)
    epsT = K.sbuf("epsT", [128, 1], F32)
    neglam = K.sbuf("neglam", [128, 1], F32)
    subg = K.sbuf("subg", [128, DV], F32)
    qng = K.sbuf("qng", [128, QR], F32)
    kvng = K.sbuf("kvng", [128, KVR], F32)
    ones_row = K.sbuf("ones_row", [1, 128], F32)
    CONST = Buf(K, "const", dma=True)
    MODC = Buf(K, "modcol")
    sc2 = [K.psum(f"sc2_{i}", [128, 1024], F32) for i in range(2)]
    banks = [sc2[0][:, 0:512], sc2[0][:, 512:1024], sc2[1][:, 0:512], sc2[1][:, 512:1024]] + [K.psum(f"bank{i}", [128, 512], F32) for i in range(4, 8)]
    bankB = [Buf(K, f"bank{i}") for i in range(8)]
    rr_state = [0]

    def rr():
        i = rr_state[0] % 4
        rr_state[0] += 1
        return banks[i], bankB[i]

    acc = [(banks[4 + i], bankB[4 + i]) for i in range(4)]
    ev_state = [0]

    def evac_eng():
        ev_state[0] += 1
        return K.act if ev_state[0] % 2 else K.dve

    def copy_op(E, out, in_, reads, writes, partial=False):
        if E is K.act:
            return K.op(E, lambda e: e.copy(out=out, in_=in_), reads, writes, partial)
        return K.op(E, lambda e: e.tensor_copy(out=out, in_=in_), reads, writes, partial)

    def scale_copy_op(E, out, in_, sc, reads, writes):
        if E is K.act:
            return K.op(E, lambda e: e.activation(out=out, in_=in_, func=AF.Identity, scale=sc), reads, writes, True)
        return K.op(E, lambda e: e.tensor_scalar(out=out, in0=in_, scalar1=sc, scalar2=None, op0=ALU.mult), reads, writes, True)

    K.dma(K.sp, [(ident32[:], ident_d), (qng[:], q_norm_d.partition_broadcast(128)),
                 (kvng[:], kv_norm_d.partition_broadcast(128)), (subg[:], subln_d.partition_broadcast(128))],
          [XIN], [CONST], CONST)
    K.op(K.dve, lambda e: e.tensor_copy(out=identb[:], in_=ident32[:]), [CONST], [CONST], partial=True)
    K.op(K.dve, lambda e: e.memset(epsT[:], EPS), [], [CONST], partial=True)
    K.op(K.dve, lambda e: e.memset(ones_row[:], 1.0), [], [CONST], partial=True)
    K.op(K.dve, lambda e: e.tensor_scalar(out=subg[:], in0=subg[:], scalar1=0.8, scalar2=None, op0=ALU.mult), [CONST], [CONST], partial=True)

    with contextlib.ExitStack() as es:
        cTs = K.sbuf("cTs", [128, KC, 2], F32, es)
        siluT = K.sbuf("siluT", [128, KC, 2], F32, es)
        b2 = [K.sbuf(f"b2_{i}", [2, 512], F32, es) for i in range(2)]
        mrow = K.sbuf("mrow", [2, NMOD], F32, es)
        wblk = [K.sbuf(f"wblk{i}", [128, KC * 512], F32, es) for i in range(2)]
        wblkB = [Buf(K, f"wblk{i}", dma=True) for i in range(2)]
        ncols = K.sbuf("ncols", [128, 3, KC], F32, es)
        sels = K.sbuf("sels", [2, 4, 128], F32, es)
        mcol = K.sbuf("mcol", [128, 6, KC, 2], F32, es)
        gst = [K.sbuf(f"gst{i}", [128, D], F32, es) for i in range(2)]
        gstB = [Buf(K, f"gst{i}", dma=True) for i in range(2)]
        lambf = K.sbuf("lamb", [128, 4 * QK], F32, es)
        lamb = lambf[:].rearrange("p (a b) -> p a b", b=QK)
        prod = K.sbuf("prod", [128, 2, QK], F32, es)
        s12 = K.sbuf("s12", [128, 2], F32, es)
        MB = Buf(K, "mphase", dma=True)
        MROW = Buf(K, "mrow")
        K.dma(K.sp, [(cTs[:], cT_d), (ncols[:], ncol_d), (sels[:], sel_d),
                     (lambf[:], lambdas_d.partition_broadcast(128))], [XIN], [MB], MB)
        K.op(K.act, lambda e: e.activation(out=siluT[:], in_=cTs[:], func=AF.Silu), [MB], [MB], partial=True)
        NB = NMOD // 512
        for nb in range(NB):
            i = nb % 2
            K.dma(K.sp, [(wblk[i][:], w_ada[nb]), (b2[i][:], b_ada[:, nb * 512:(nb + 1) * 512])], [XIN], [wblkB[i]], wblkB[i])
            bk, bb = rr()
            for kc in range(KC):
                K.mm(bb, bk[0:2, 0:512], siluT[:, kc, :], wblk[i][:, kc * 512:(kc + 1) * 512], kc == 0, kc == KC - 1, reads=[wblkB[i], MB])
            K.op(K.dve, lambda e: e.tensor_tensor(out=mrow[:, nb * 512:(nb + 1) * 512], in0=bk[0:2, 0:512], in1=b2[i][:], op=ALU.add),
                 [bb, wblkB[i]], [MROW], partial=True)
        vec_idx = [0, 1, 3, 4, 6, 7]
        bk, bb = rr()
        for vi, v in enumerate(vec_idx):
            for kc in range(KC):
                o = (vi * KC + kc) * 2
                K.mm(bb, bk[:, o:o + 2], mrow[0:2, v * D + kc * 128: v * D + (kc + 1) * 128], ident32[0:2, 0:2], True, True, reads=[MROW, CONST])
        K.op(K.dve, lambda e: e.tensor_copy(out=mcol[:].rearrange("p a k g -> p (a k g)"), in_=bk[:, 0:6 * KC * 2]), [bb], [MB], partial=True)
        for l in range(3):
            K.op(K.dve, lambda e: e.scalar_tensor_tensor(out=modcol[:, l, 0, :, :], in0=mcol[:, 2 * l + 1, :, :], scalar=1.0,
                                                         in1=ncols[:, l, :].unsqueeze(2).broadcast_to([128, KC, 2]), op0=ALU.add, op1=ALU.mult),
                 [MB], [MODC], partial=True)
            K.op(K.dve, lambda e: e.tensor_copy(out=modcol[:, l, 1, :, :], in_=mcol[:, 2 * l, :, :]), [MB], [MODC], partial=True)
        gi = 0
        for l in range(3):
            v = 3 * l + 2
            for g in range(2):
                si = g + (0 if l == 1 else 2)
                st, sB = gst[gi % 2], gstB[gi % 2]
                gi += 1
                for db in range(NDB):
                    bk, bb = rr()
                    K.mm(bb, bk[:, 0:512], sels[0:2, si, :], mrow[0:2, v * D + db * 512: v * D + (db + 1) * 512], True, True, reads=[MROW, MB])
                    copy_op(evac_eng(), st[:, db * 512:(db + 1) * 512], bk[:, 0:512], [bb], [sB], partial=True)
                K.dma(K.pool, [(gbc[l, g], st[:])], [sB], [gbcB], sB, partial=True)
        K.op(K.dve, lambda e: e.tensor_tensor(out=prod[:, 0, :], in0=lamb[:, 0, :], in1=lamb[:, 1, :], op=ALU.mult), [MB], [MB], partial=True)
        K.op(K.dve, lambda e: e.tensor_tensor(out=prod[:, 1, :], in0=lamb[:, 2, :], in1=lamb[:, 3, :], op=ALU.mult), [MB], [MB], partial=True)
        K.op(K.dve, lambda e: e.reduce_sum(out=s12[:], in_=prod[:], axis=mybir.AxisListType.X), [MB], [MB], partial=True)
        K.op(K.act, lambda e: e.activation(out=s12[:], in_=s12[:], func=AF.Exp), [MB], [MB], partial=True)
        K.op(K.dve, lambda e: e.tensor_tensor(out=neglam[:], in0=s12[:, 1:2], in1=s12[:, 0:1], op=ALU.subtract), [MB], [CONST], partial=True)
        K.op(K.dve, lambda e: e.tensor_scalar(out=neglam[:], in0=neglam[:], scalar1=-0.2, scalar2=None, op0=ALU.add), [CONST], [CONST], partial=True)
        d = dbg_out("modcol", (128, 3 * 2 * KC * 2))
        if d is not None:
            K.dma(K.pool, [(d, modcol[:].rearrange("p l a k g -> p (l a k g)"))], [MODC], [], MB)
        d = dbg_out("neglam", (128, 1))
        if d is not None:
            K.dma(K.pool, [(d, neglam[:])], [CONST], [], MB)
        phase_barrier(K, [MB, MROW, MODC, CONST] + wblkB + gstB)

    with contextlib.ExitStack() as es:
        EMAX = max(v[1] for v in wshapes.values())
        NST = 3
        st32 = [K.sbuf(f"st32_{i}", [128, EMAX], F32, es) for i in range(NST)]
        st16 = [K.sbuf(f"st16_{i}", [128, EMAX], BF16, es) for i in range(NST)]
        s32B = [Buf(K, f"st32_{i}", dma=True) for i in range(NST)]
        s16B = [Buf(K, f"st16_{i}", dma=True) for i in range(NST)]
        ci = 0
        engs = [K.dve, K.act, K.pool]
        for name, (npc, E) in wshapes.items():
            for p in range(npc):
                i = ci % NST
                ci += 1
                K.dma(K.sp, [(st32[i][:, 0:E], w32[name][p])], [XIN], [s32B[i]], s32B[i])
                copy_op(engs[ci % 3], st16[i][:, 0:E], st32[i][:, 0:E], [s32B[i]], [s16B[i]])
                K.dma(K.pool, [(wb[name][p], st16[i][:, 0:E])], [s16B[i]], [wbB[name]], s16B[i], partial=True)
        phase_barrier(K, s32B + s16B)


    NSL = 4
    HWp = HW // 256
    BIGN = max(FCN * 512, 4 * HW + 2 * 4 * H * VA + 4 * KVR + 4 * ROPE + 4 * HW + D)

    def rms_rstd(ES, src_fn, n, nsub, reads, ssb, ssB, junk, JB):
        K.op(K.dve, lambda e: e.memset(ssb[:, 0:nsub], 0.0), [], [ssB])
        for s_ in range(nsub):
            K.op(K.act, lambda e: e.activation(out=junk[:, 0:n], in_=src_fn(s_), func=AF.Square, accum_out=ssb[:, s_:s_ + 1]),
                 reads + [ssB], [JB, ssB], partial=True)
        K.op(K.act, lambda e: e.activation(out=ssb[:, 0:nsub], in_=ssb[:, 0:nsub], func=AF.Sqrt, scale=1.0 / n, bias=epsT[:, 0:1]), [ssB, CONST], [ssB])
        K.op(K.dve, lambda e: e.reciprocal(out=ssb[:, 0:nsub], in_=ssb[:, 0:nsub]), [ssB], [ssB])

    def make_ffn_env(es, tag):
        env = {}
        env["xt"] = K.sbuf("xt" + tag, [128, NSUB, D], F32, es)
        env["X"] = Buf(K, "X" + tag)
        env["hT"] = K.sbuf("hT" + tag, [128, 2 * H if False else KC, T], BF16, es)
        env["HT"] = Buf(K, "HT" + tag)
        env["big"] = K.sbuf("big" + tag, [128, BIGN], BF16, es)
        env["GT"] = Buf(K, "GT" + tag)
        env["stmp"] = [K.sbuf(f"stmp{tag}{i}", [128, T], F32, es) for i in range(2)]
        env["STMP"] = [Buf(K, f"stmp{tag}{i}") for i in range(2)]
        env["ss"] = K.sbuf("ss" + tag, [128, NSUB], F32, es)
        env["SS"] = Buf(K, "ss" + tag)
        env["xs"] = [K.sbuf(f"xs{tag}{i}", [128, D], BF16, es) for i in range(2)]
        env["XS"] = [Buf(K, f"xs{tag}{i}") for i in range(2)]
        env["junk"] = env["xs"][0]
        env["JB"] = env["XS"][0]
        env["utmp"] = env["stmp"]
        env["UT"] = env["STMP"]
        env["cnt"] = 0
        return env

    def norm_to_hT(env, l, g):
        xt, X = env["xt"], env["X"]
        rms_rstd(None, lambda s_: xt[:, s_, :], D, NSUB, [X], env["ss"], env["SS"], env["junk"], env["JB"])
        first = True
        base = env["cnt"]
        env["cnt"] += NSUB

        def prescale(s_):
            i = (base + s_) % 2
            K.op(K.act, lambda e: e.activation(out=env["xs"][i][:], in_=xt[:, s_, :], func=AF.Identity, scale=env["ss"][:, s_:s_ + 1]), [X, env["SS"]], [env["XS"][i]])

        prescale(0)
        for s_ in range(NSUB):
            i = (base + s_) % 2
            xs, XS = env["xs"][i], env["XS"][i]
            bks = []
            for kg in range(KC // 4):
                bk, bb = rr()
                bks.append((bk, bb))
                for j in range(4):
                    kc = kg * 4 + j
                    K.mm(bb, bk[:, j * 128:(j + 1) * 128], xs[:, kc * 128:(kc + 1) * 128], identb[:], True, True, reads=[XS, CONST])
                if kg == 0 and s_ + 1 < NSUB:
                    prescale(s_ + 1)
            for kg, (bk, bb) in enumerate(bks):
                E = evac_eng()
                for j in range(4):
                    kc = kg * 4 + j
                    a_ap = modcol[:, l, 0, kc, g:g + 1]
                    b_ap = modcol[:, l, 1, kc, g:g + 1]
                    o_ap = env["hT"][:, kc, s_ * 128:(s_ + 1) * 128]
                    i_ap = bk[:, j * 128:(j + 1) * 128]
                    if E is K.act:
                        K.op(E, lambda e: e.activation(out=o_ap, in_=i_ap, func=AF.Identity, scale=a_ap, bias=b_ap), [bb, MODC], [env["HT"]], partial=not first)
                    else:
                        K.op(E, lambda e: e.tensor_scalar(out=o_ap, in0=i_ap, scalar1=a_ap, scalar2=b_ap, op0=ALU.mult, op1=ALU.add), [bb, MODC], [env["HT"]], partial=not first)
                    first = False

    def plan_ffn(ring, tag):
        for fp in range(NP13):
            ring.plan(wb[tag + "_w1"][fp], wbB[tag + "_w1"], KC * 256)
            ring.plan(wb[tag + "_w3"][fp], wbB[tag + "_w3"], KC * 256)
        for p in range(NPW2):
            ring.plan(wb[tag + "_w2"][p], wbB[tag + "_w2"], cfg.W2F * 512)

    def ffn(env, ring, gate_ap, GATEB):
        hT, HT, big, GT = env["hT"], env["HT"], env["big"], env["GT"]
        for fp in range(NP13):
            w1t, w1b = ring.take()
            w3t, w3b = ring.take()
            for fl in range(2):
                fc = fp * 2 + fl
                b1, B1 = rr()
                b3, B3 = rr()
                for kc in range(KC):
                    K.mm(B1, b1[:, 0:T], w1t[:, kc * 256 + fl * 128: kc * 256 + (fl + 1) * 128], hT[:, kc, :], kc == 0, kc == KC - 1, reads=[w1b, HT])
                for kc in range(KC):
                    K.mm(B3, b3[:, 0:T], w3t[:, kc * 256 + fl * 128: kc * 256 + (fl + 1) * 128], hT[:, kc, :], kc == 0, kc == KC - 1, reads=[w3b, HT])
                i = env["cnt"] % 2
                env["cnt"] += 1
                K.op(K.act, lambda e: e.activation(out=env["stmp"][i][:], in_=b1[:, 0:T], func=AF.Silu), [B1], [env["STMP"][i]])
                K.op(K.dve, lambda e: e.tensor_tensor(out=big[:, fc * T:(fc + 1) * T], in0=env["stmp"][i][:], in1=b3[:, 0:T], op=ALU.mult),
                     [env["STMP"][i], B3], [GT], partial=(fc > 0))
            ring.done(2)
        W2F = cfg.W2F
        for db in range(NDB):
            for q in range(cfg.W2Q):
                wt, wB = ring.take()
                for s_ in range(NSUB):
                    ab, AB = acc[s_]
                    for fl in range(W2F):
                        fc = q * W2F + fl
                        first = (q == 0 and fl == 0)
                        last = (q == cfg.W2Q - 1 and fl == W2F - 1)
                        K.mm(AB, ab[:, 0:512], big[:, fc * T + s_ * 128: fc * T + (s_ + 1) * 128], wt[:, fl * 512:(fl + 1) * 512], first, last,
                             reads=[wB, GT], mark=(last or (s_ == NSUB - 1 and fl == W2F - 1)))
                ring.done(1)
            for s_ in range(NSUB):
                ab, AB = acc[s_]
                i = env["cnt"] % 2
                env["cnt"] += 1
                K.op(K.dve, lambda e: e.tensor_tensor(out=env["utmp"][i][:], in0=ab[:, 0:512], in1=gate_ap[:, db * 512:(db + 1) * 512], op=ALU.mult),
                     [AB, GATEB], [env["UT"][i]])
                K.op(K.pool, lambda e: e.tensor_tensor(out=env["xt"][:, s_, db * 512:(db + 1) * 512], in0=env["xt"][:, s_, db * 512:(db + 1) * 512],
                                                        in1=env["utmp"][i][:], op=ALU.add),
                     [env["UT"][i], env["X"]], [env["X"]], partial=True)

    with contextlib.ExitStack() as es:
        env = make_ffn_env(es, "A")
        xt, X, hT, HT, big, GT = env["xt"], env["X"], env["hT"], env["HT"], env["big"], env["GT"]
        ring = Ring(K, "ringA", NSL, cfg.slot_elems, es)
        gate1 = K.sbuf("gate1", [128, D], F32, es)
        G1B = Buf(K, "gate1")
        ropet = K.sbuf("ropet", [128, NSUB, 161], F32, es)
        RP = Buf(K, "ropet")
        own_st = K.sbuf("own_st", [128, 4 * HW + 4 * QR + 4 * HW + 4 * H * ROPE], BF16, es)
        OWN = Buf(K, "own_st")
        cqnT = K.sbuf("cqnT", [128, QKC, T], BF16, es)
        CQT = Buf(K, "cqnT")
        ckvnT = K.sbuf("ckvnT", [128, KVKC, T], BF16, es)
        CKT = Buf(K, "ckvnT")
        NSTG = 4
        stg = [K.sbuf(f"stg{i}", [128, T], BF16, es) for i in range(NSTG)]
        STG = [Buf(K, f"stg{i}") for i in range(NSTG)]
        stg_i = [0]
        rtmp = [K.sbuf(f"rtmp{i}", [128, 512], F32, es) for i in range(2)]
        RT = [Buf(K, f"rtmp{i}") for i in range(2)]
        ss2 = K.sbuf("ss2", [128, NSUB], F32, es)
        SS2 = Buf(K, "ss2")
        qkt = [K.sbuf(f"qkt{i}", [128, 512], F32, es) for i in range(2)]
        QKT = [Buf(K, f"qkt{i}") for i in range(2)]
        o_ = [0]

        def carve(buf, n):
            a = buf[:, o_[0]:o_[0] + n]
            o_[0] += n
            return a
        kda_tok = carve(big, 4 * HW).rearrange("p (s n) -> p s n", s=NSUB)
        vda = carve(big, 4 * H * VA).rearrange("p (h s c) -> p h s c", s=NSUB, h=H)
        vmla = carve(big, 4 * H * VA).rearrange("p (h s c) -> p h s c", s=NSUB, h=H)
        ckvn = carve(big, 4 * KVR).rearrange("p (s n) -> p s n", s=NSUB)
        kpe_tok = carve(big, 4 * ROPE).rearrange("p (s n) -> p s n", s=NSUB)
        knope_tok = carve(big, 4 * HW).rearrange("p (s n) -> p s n", s=NSUB)
        o_[0] = 0
        qda_tok = carve(own_st, 4 * HW).rearrange("p (s n) -> p s n", s=NSUB)
        cqn = carve(own_st, 4 * QR).rearrange("p (s n) -> p s n", s=NSUB)
        qnope_tok = carve(own_st, 4 * HW).rearrange("p (s n) -> p s n", s=NSUB)
        qpe_tok = carve(own_st, 4 * H * ROPE).rearrange("p (s n) -> p s n", s=NSUB)

        tiles = []
        for g in range(cfg.NG):
            for ti in range(cfg.ntile[g]):
                tiles.append((g, ti, ti < cfg.own_tiles))
        for (g, ti, own) in tiles:
            plan_ffn(ring, "f1")
            for si, (nm, n) in enumerate(cfg.segs):
                if nm in ("q", "cq") and not own:
                    continue
                for p in range(seg_np[si]):
                    ring.plan(wb["w_in"][seg_base[si] + p], wbB["w_in"], KC * 256)
            if own:
                for p in range(NUQ_N + NUQ_R):
                    ring.plan(wb["w_uq"][p], wbB["w_uq"], QKC * 512)
            for p in range(2 * NUKV):
                ring.plan(wb["w_ukv"][p], wbB["w_ukv"], KVKC * 512)

        def rope_apply(bk, BB, ncol_blocks, blk, rot, dst, dstB, s_, tab_off):
            half = rot // 2
            src = bk[:, 0:ncol_blocks * blk].rearrange("p (b d) -> p b d", d=blk)
            Ct = ropet[:, s_, tab_off:tab_off + rot].unsqueeze(1).broadcast_to([128, ncol_blocks, rot])
            S1 = ropet[:, s_, tab_off + rot:tab_off + rot + half].unsqueeze(1).broadcast_to([128, ncol_blocks, half])
            S2 = ropet[:, s_, tab_off + rot + half:tab_off + 2 * rot].unsqueeze(1).broadcast_to([128, ncol_blocks, half])
            A = rtmp[0][:, 0:ncol_blocks * rot].rearrange("p (b d) -> p b d", d=rot)
            Bt = rtmp[1][:, 0:ncol_blocks * rot].rearrange("p (b d) -> p b d", d=rot)
            nops = int(([d.split(":")[1] for d in debug if d.startswith("ropeops:")] or ["4"])[0])
            if nops >= 1:
                K.op(K.dve, lambda e: e.tensor_tensor(out=A, in0=src[:, :, 0:rot], in1=Ct, op=ALU.mult), [BB, RP], [RT[0]])
            if nops >= 2:
                K.op(K.dve, lambda e: e.tensor_tensor(out=Bt[:, :, 0:half], in0=src[:, :, half:rot], in1=S1, op=ALU.mult), [BB, RP], [RT[1]])
            if nops >= 3:
                K.op(K.dve, lambda e: e.tensor_tensor(out=Bt[:, :, half:rot], in0=src[:, :, 0:half], in1=S2, op=ALU.mult), [BB, RP], [RT[1]], partial=True)
            if nops >= 4:
                K.op(K.dve, lambda e: e.tensor_tensor(out=dst[:, :, 0:rot], in0=A, in1=Bt, op=ALU.add), [RT[0], RT[1]], dstB, partial=True)

        def transpose_out(src_fn, nrows, dram_ap, reads, DST):
            bk, bb = rr()
            for s_ in range(NSUB):
                K.mm(bb, bk[0:nrows, s_ * 128:(s_ + 1) * 128], src_fn(s_), identb[:], True, True, reads=reads + [CONST])
            i = stg_i[0] % NSTG
            stg_i[0] += 1
            copy_op(evac_eng(), stg[i][0:nrows, :], bk[0:nrows, 0:T], [bb], [STG[i]])
            K.dma(K.sp, [(dram_ap, stg[i][0:nrows, :])], [STG[i]], DST, partial=True)

        stopA = [d.split(":")[1] for d in debug if d.startswith("stopA:")]
        stopA = stopA[0] if stopA else None

        def load_x(g, ti):
            K.dma(K.sp, [(xt[:], x_d[g][ti * T:(ti + 1) * T, :].rearrange("(s p) d -> p s d", p=128))], [XIN], [X])

        def do_tile(g, ti, own):
            t0 = ti * T
            if ti == 0:
                K.dma(K.sp, [(gate1[:], gbc[0, g])], [gbcB], [G1B])
            if (g, ti) == (0, 0):
                load_x(g, ti)
            K.dma(K.sp, [(ropet[:], rope_d[g][t0:t0 + T, :].rearrange("(s p) c -> p s c", p=128))], [XIN], [RP])
            if stopA == "load":
                return True
            norm_to_hT(env, 0, g)
            if stopA == "norm":
                return True
            if g == 0 and ti == 0:
                d = dbg_out("hT0", (128, KC * T), BF16)
                if d is not None:
                    K.dma(K.pool, [(d, hT[:].rearrange("p k t -> p (k t)"))], [HT], [])
            ffn(env, ring, gate1[:], G1B)
            if g == 0 and ti == 0:
                d = dbg_out("gt0", (128, FCN * T), BF16)
                if d is not None:
                    K.dma(K.pool, [(d, big[:, 0:FCN * T])], [GT], [])
                d = dbg_out("gate", (128, 2 * D))
                if d is not None:
                    K.dma(K.pool, [(d[:, 0:D], gate1[:])], [G1B], [])
            if stopA == "ffn":
                return True
            if own:
                K.dma(K.pool, [(x1_d[g][t0:t0 + T, :].rearrange("(s p) d -> p s d", p=128), xt[:])], [X], [x1B[g]], partial=True)
            if stopA == "x1s":
                return True
            norm_to_hT(env, 1, g)
            nxt = tiles.index((g, ti, own)) + 1
            if nxt < len(tiles) and stopA is None:
                load_x(tiles[nxt][0], tiles[nxt][1])
            if stopA == "norm2":
                return True
            BIGW = [GT]
            for s_ in range(NSUB):
                fl_b = ropet[:, s_, 160:161].unsqueeze(1).broadcast_to([128, H, 1])
                K.op(K.pool, lambda e: e.tensor_copy(out=vda[:, :, s_, DV:VA], in_=fl_b), [RP], BIGW, partial=(s_ > 0))
                K.op(K.pool, lambda e: e.tensor_copy(out=vmla[:, :, s_, DV:VA], in_=fl_b), [RP], BIGW, partial=True)
            if stopA == "memset":
                return True
            for si, (nm, n) in enumerate(cfg.segs):
                if stopA == "seg_" + nm:
                    return True
                if nm in ("q", "cq") and not own:
                    continue
                npc = seg_np[si]
                for p in range(npc):
                    wt, wB = ring.take()
                    ncols = min(256, n - p * 256)
                    for s_ in range(NSUB):
                        if nm in ("cq", "ckv"):
                            bk, bb = acc[s_]
                            o0 = p * 256
                        else:
                            bk, bb = rr()
                            o0 = 0
                        for kc in range(KC):
                            K.mm(bb, bk[:, o0:o0 + ncols], hT[:, kc, s_ * 128:(s_ + 1) * 128], wt[:, kc * 256: kc * 256 + ncols], kc == 0, kc == KC - 1, reads=[wB, HT])
                        if stopA == "q_mm":
                            return True
                        if nm in ("q", "k"):
                            dst = (qda_tok if nm == "q" else kda_tok)[:, s_, p * 256:(p + 1) * 256].rearrange("p (b d) -> p b d", d=QK)
                            dB = [OWN] if nm == "q" else BIGW
                            qi = stg_i[0] % 2
                            stg_i[0] += 1
                            copy_op(K.act, qkt[qi][:, 0:256], bk[:, 0:256], [bb], [QKT[qi]])
                            srcv = qkt[qi][:, 0:256].rearrange("p (b d) -> p b d", d=QK)
                            copy_op(K.pool, dst[:, :, DA_ROT:QK], srcv[:, :, DA_ROT:QK], [QKT[qi]], dB, partial=True)
                            if stopA == "q_copy":
                                return True
                            rope_apply(qkt[qi], QKT[qi], 4, QK, DA_ROT, dst, dB, s_, 0)
                            if stopA == "q_rope":
                                return True
                        elif nm == "v":
                            scale_copy_op(evac_eng(), vda[:, 2 * p:2 * p + 2, s_, 0:DV], bk[:, 0:256].rearrange("p (h c) -> p h c", c=DV), ropet[:, s_, 160:161], [bb, RP], BIGW)
                        elif nm == "kpe":
                            dst = kpe_tok[:, s_, :].rearrange("p (b d) -> p b d", d=ROPE)
                            qi = stg_i[0] % 2
                            stg_i[0] += 1
                            copy_op(K.act, qkt[qi][:, 0:ROPE], bk[:, 0:ROPE], [bb], [QKT[qi]])
                            rope_apply(qkt[qi], QKT[qi], 1, ROPE, ROPE, dst, BIGW, s_, 32)
                    ring.done(1)
                if nm in ("cq", "ckv"):
                    nn = QR if nm == "cq" else KVR
                    gt_ = qng if nm == "cq" else kvng
                    dstt = cqn if nm == "cq" else ckvn
                    dB = [OWN] if nm == "cq" else BIGW
                    rms_rstd(None, lambda s_: acc[s_][0][:, 0:nn], nn, NSUB, [acc[i][1] for i in range(NSUB)], ss2, SS2, env["junk"], env["JB"])
                    for s_ in range(NSUB):
                        K.op(K.dve, lambda e: e.scalar_tensor_tensor(out=dstt[:, s_, :], in0=acc[s_][0][:, 0:nn], scalar=ss2[:, s_:s_ + 1], in1=gt_[:, 0:nn],
                                                                     op0=ALU.mult, op1=ALU.mult), [acc[s_][1], SS2, CONST], dB, partial=True)
            if stopA == "proj":
                return True
            lat = [(ckvn, KVKC, ckvnT, CKT, BIGW)]
            if own:
                lat.append((cqn, QKC, cqnT, CQT, [OWN]))
            for (srct, nk, dstT, DTB, SB_) in lat:
                for kc in range(nk):
                    bk, bb = rr()
                    for s_ in range(NSUB):
                        K.mm(bb, bk[:, s_ * 128:(s_ + 1) * 128], srct[:, s_, kc * 128:(kc + 1) * 128], identb[:], True, True, reads=SB_ + [CONST])
                    copy_op(evac_eng(), dstT[:, kc, :], bk[:, 0:T], [bb], [DTB], partial=(kc > 0))
            if own:
                for p in range(NUQ_N + NUQ_R):
                    wt, wB = ring.take()
                    for s_ in range(NSUB):
                        bk, bb = rr()
                        for kc in range(QKC):
                            K.mm(bb, bk[:, 0:512], cqnT[:, kc, s_ * 128:(s_ + 1) * 128], wt[:, kc * 512:(kc + 1) * 512], kc == 0, kc == QKC - 1, reads=[wB, CQT])
                        if p < NUQ_N:
                            copy_op(evac_eng(), qnope_tok[:, s_, p * 512:(p + 1) * 512], bk[:, 0:512], [bb], [OWN], partial=True)
                        else:
                            pr = p - NUQ_N
                            nv = min(512, H * ROPE - pr * 512)
                            dst = qpe_tok[:, s_, pr * 512: pr * 512 + nv].rearrange("p (b d) -> p b d", d=ROPE)
                            qi = stg_i[0] % 2
                            stg_i[0] += 1
                            copy_op(K.act, qkt[qi][:, 0:nv], bk[:, 0:nv], [bb], [QKT[qi]])
                            rope_apply(qkt[qi], QKT[qi], nv // ROPE, ROPE, ROPE, dst, [OWN], s_, 32)
                    ring.done(1)
            for p in range(2 * NUKV):
                wt, wB = ring.take()
                for s_ in range(NSUB):
                    bk, bb = rr()
                    for kc in range(KVKC):
                        K.mm(bb, bk[:, 0:512], ckvnT[:, kc, s_ * 128:(s_ + 1) * 128], wt[:, kc * 512:(kc + 1) * 512], kc == 0, kc == KVKC - 1, reads=[wB, CKT])
                    if p < NUKV:
                        copy_op(evac_eng(), knope_tok[:, s_, p * 512:(p + 1) * 512], bk[:, 0:512], [bb], BIGW, partial=True)
                    else:
                        pv = p - NUKV
                        scale_copy_op(evac_eng(), vmla[:, 4 * pv:4 * pv + 4, s_, 0:DV], bk[:, 0:512].rearrange("p (h c) -> p h c", c=DV), ropet[:, s_, 160:161], [bb, RP], BIGW)
                ring.done(1)
            if stopA == "mla":
                return True
            for h in range(H):
                transpose_out(lambda s_: kda_tok[:, s_, h * 128:(h + 1) * 128], 128, KT_da[g][h, :, t0:t0 + T], BIGW, [QKVB[g]])
                transpose_out(lambda s_: knope_tok[:, s_, h * 128:(h + 1) * 128], 128, KT_nope[g][h, :, t0:t0 + T], BIGW, [QKVB[g]])
            transpose_out(lambda s_: kpe_tok[:, s_, :], ROPE, KT_pe[g][:, t0:t0 + T], BIGW, [QKVB[g]])
            K.dma(K.pool, [(V_da[g][:, :, ti * NSUB:(ti + 1) * NSUB, :].rearrange("h p s c -> p h s c"), vda),
                           (V_mla[g][:, :, ti * NSUB:(ti + 1) * NSUB, :].rearrange("h p s c -> p h s c"), vmla)],
                  BIGW, [QKVB[g]], partial=True)
            if own:
                for h in range(H):
                    transpose_out(lambda s_: qda_tok[:, s_, h * 128:(h + 1) * 128], 128, QT_da[g][h, :, t0:t0 + T], [OWN], [QKVB[g]])
                    transpose_out(lambda s_: qnope_tok[:, s_, h * 128:(h + 1) * 128], 128, QT_nope[g][h, :, t0:t0 + T], [OWN], [QKVB[g]])
                for hp in range(H // 2):
                    transpose_out(lambda s_: qpe_tok[:, s_, hp * 128:(hp + 1) * 128], 128, QT_pe[g][hp, :, t0:t0 + T], [OWN], [QKVB[g]])
            return False

        for (g, ti, own) in tiles:
            if do_tile(g, ti, own):
                break
        for nm_, aps in (("x1", x1_d), ):
            for g in range(cfg.NG):
                d = dbg_out(f"{nm_}{g}", (nq, D))
                if d is not None:
                    for ti in range(cfg.own_tiles):
                        K.dma(K.sp, [(xt[:], x1_d[g][ti * T:(ti + 1) * T, :].rearrange("(s p) d -> p s d", p=128))], [x1B[g]], [X])
                        K.dma(K.pool, [(d[ti * T:(ti + 1) * T, :].rearrange("(s p) d -> p s d", p=128), xt[:])], [X], [])
        phase_barrier(K, [X, HT, GT, OWN, CQT, CKT, RP, G1B, SS2, env["SS"], env["JB"]] + env["XS"] + STG + RT + QKT + env["STMP"] + env["UT"] + ring.bufs + bankB)


    with contextlib.ExitStack() as es:
        SMAX = max(cfg.S)
        NKTM = SMAX // 128
        NQT = nq // 512
        KTb = [K.sbuf(f"KTb{i}", [128, SMAX], BF16, es) for i in range(2)]
        Vb = [K.sbuf(f"Vb{i}", [128, NKTM, VA], BF16, es) for i in range(2)]
        NQB = 3
        qbuf = [K.sbuf(f"qbuf{i}", [128, 512], BF16, es) for i in range(NQB)]
        qpebuf = [K.sbuf(f"qpebuf{i}", [64, 512], BF16, es) for i in range(NQB)]
        QBUF = [Buf(K, f"qbuf{i}") for i in range(NQB)]
        kpeb = K.sbuf("kpeb", [64, SMAX], BF16, es)
        KVQ = [Buf(K, f"kvq{i}") for i in range(2)]
        KPE = Buf(K, "kpeb")
        NE = 6
        ebuf2 = [K.sbuf(f"ebuf2_{i}", [128, 1024], BF16, es) for i in range(NE // 2)]
        ebuf = [ebuf2[i // 2][:, (i % 2) * 512:(i % 2 + 1) * 512] for i in range(NE)]
        EB = [Buf(K, f"ebuf{i}") for i in range(NE)]
        accS = [K.sbuf(f"accS{i}", [128, 8, VA], F32, es) for i in range(2)]
        ACS = [Buf(K, f"accS{i}") for i in range(2)]
        rc = K.sbuf("rc", [128, 8], F32, es)
        rl = K.sbuf("rl", [128, 8], F32, es)
        RC = Buf(K, "rc")
        t1 = K.sbuf("t1", [128, DV], F32, es)
        T1 = Buf(K, "t1")
        o32 = K.sbuf("o32", [128, 4, DV], F32, es)
        O32 = Buf(K, "o32")
        ssq = K.sbuf("ssq", [128, 4], F32, es)
        SSQ = Buf(K, "ssq")
        junkB = K.sbuf("junkB", [128, DV], BF16, es)
        JKB = Buf(K, "junkB")
        obf = [K.sbuf(f"obf{i}", [128, 4, DV], BF16, es) for i in range(2)]
        OBF = [Buf(K, f"obf{i}") for i in range(2)]
        mstg = [K.sbuf(f"mstg{i}", [128, 512], BF16, es) for i in range(2)]
        rsum = K.sbuf("rsum", [1, 512], F32, es)
        RSUM = Buf(K, "rsum")
        rbc = K.sbuf("rbc", [128, 512], F32, es)
        RBC = Buf(K, "rbc")
        MST = [Buf(K, f"mstg{i}") for i in range(2)]
        da_slots = {(0, 0): 0, (0, 1): 1, (0, 2): 2, (1, 0): 3, (1, 1): 4, (1, 2): 5, (0, 3): 6, (1, 3): 7}

        def acc_slot(sl):
            b = 4 + sl // 3
            o = (sl % 3) * VA
            return banks[b][:, o:o + VA], bankB[b]

        units = [(g, fam, h) for g in range(cfg.NG) for fam in ("da", "mla") for h in range(H)]

        def load_unit(u):
            g, fam, h = units[u]
            par = u % 2
            Sg = cfg.S[g]
            nkt = Sg // 128
            if fam == "mla" and h == 0:
                K.dma(K.sp, [(kpeb[0:64, 0:Sg], KT_pe[g])], [QKVB[g]], [KPE])
            if fam == "da":
                pairs = [(KTb[par][:, 0:Sg], KT_da[g][h]), (Vb[par][:, 0:nkt, :], V_da[g][h])]
            else:
                pairs = [(KTb[par][:, 0:Sg], KT_nope[g][h]), (Vb[par][:, 0:nkt, :], V_mla[g][h])]
            K.dma(K.sp, pairs, [QKVB[g]], [KVQ[par]])

        def load_q(n):
            if n >= len(units) * NQT:
                return
            u, qt = n // NQT, n % NQT
            g, fam, h = units[u]
            qi = n % NQB
            q0 = qt * 512
            if fam == "da":
                pairs = [(qbuf[qi][:], QT_da[g][h, :, q0:q0 + 512])]
            else:
                pairs = [(qbuf[qi][:], QT_nope[g][h, :, q0:q0 + 512]),
                         (qpebuf[qi][0:64, :], QT_pe[g][h // 2, (h % 2) * 64:(h % 2) * 64 + 64, q0:q0 + 512])]
            K.dma(K.sp, pairs, [QKVB[g]], [QBUF[qi]])

        steps = []
        for u, (g, fam, h) in enumerate(units):
            for qt in range(NQT):
                for kt in range(cfg.S[g] // 128):
                    steps.append((u, qt, kt))
        pending = []
        epi_cnt = [0]
        mst_i = [0]
        SC_MLA = float((NOPE + ROPE) ** -0.5)
        SC_DA = float(QK ** -0.5)

        def emit_qk(i):
            u, qt, kt = steps[i]
            g, fam, h = units[u]
            par = u % 2
            qi = (u * NQT + qt) % NQB
            if fam == "da":
                b0, b1 = 2 * (i % 2), 2 * (i % 2) + 1
                K.mm(bankB[b0], banks[b0][:, 0:512], KTb[par][0:64, kt * 128:(kt + 1) * 128], qbuf[qi][0:64, :], True, True, reads=[KVQ[par], QBUF[qi]])
                K.mm(bankB[b1], banks[b1][:, 0:512], KTb[par][64:128, kt * 128:(kt + 1) * 128], qbuf[qi][64:128, :], True, True, reads=[KVQ[par], QBUF[qi]])
            else:
                b0 = i % 4
                K.mm(bankB[b0], banks[b0][:, 0:512], KTb[par][:, kt * 128:(kt + 1) * 128], qbuf[qi][:], True, False, reads=[KVQ[par], QBUF[qi]])
                K.mm(bankB[b0], banks[b0][:, 0:512], kpeb[0:64, kt * 128:(kt + 1) * 128], qpebuf[qi][0:64, :], False, True, reads=[KVQ[par], KPE, QBUF[qi]])

        def emit_exp_pv(i):
            u, qt, kt = steps[i]
            g, fam, h = units[u]
            par = u % 2
            nkt = cfg.S[g] // 128
            if fam == "da":
                pk = i % (NE // 2)
                b0 = 2 * (i % 2)
                K.op(K.act, lambda e: e.activation(out=ebuf2[pk][:, 0:1024], in_=sc2[i % 2][:, 0:1024], func=AF.Exp, scale=SC_DA),
                     [bankB[b0], bankB[b0 + 1]], [EB[2 * pk], EB[2 * pk + 1]])
                for j in range(2):
                    ei = (2 * i + j) % NE
                    for qs in range(4):
                        sl = da_slots[(j, qs)]
                        ap, AB = acc_slot(sl)
                        K.mm(AB, ap, ebuf[ei][:, qs * 128:(qs + 1) * 128], Vb[par][:, kt, :], kt == 0 and sl in (0, 3, 6), kt == nkt - 1 and sl in (2, 5, 7),
                             reads=[EB[ei], KVQ[par]], mark=(qs == 3 or (kt == nkt - 1 and sl in (2, 5, 7))))
            else:
                b = i % 4
                ei = i % NE
                K.op(K.act, lambda e: e.activation(out=ebuf[ei][:], in_=banks[b][:, 0:512], func=AF.Exp, scale=SC_MLA), [bankB[b]], [EB[ei]])
                K.mm(bankB[4], banks[4][:, 0:512], Vb[par][:, kt, 0:DV], ebuf[ei][:], kt == 0, kt == nkt - 1, reads=[EB[ei], KVQ[par]], mark=False)
                K.mm(bankB[5], banks[5][0:1, 0:512], Vb[par][:, kt, DV:VA], ebuf[ei][:], kt == 0, kt == nkt - 1, reads=[EB[ei], KVQ[par]], mark=True)
                if kt == nkt - 1:
                    bankB[4].w = dict(bankB[5].w)
                    bankB[4].r = {}
            if kt == nkt - 1:
                epilogue(i, u, qt)

        def epilogue(i, u, qt):
            g, fam, h = units[u]
            ep = epi_cnt[0] % 2
            epi_cnt[0] += 1
            if fam == "mla":
                K.op(K.dve, lambda e: e.reciprocal(out=rsum[0:1, :], in_=banks[5][0:1, 0:512]), [bankB[5]], [RSUM])
                K.mm(bankB[7], banks[7][:, 0:512], ones_row[0:1, :], rsum[0:1, :], True, True, reads=[RSUM, CONST])
                K.op(K.act, lambda e: e.copy(out=rbc[:], in_=banks[7][:, 0:512]), [bankB[7]], [RBC])
                mi = mst_i[0] % 2
                mst_i[0] += 1
                K.op(K.dve, lambda e: e.tensor_tensor(out=mstg[mi][:], in0=banks[4][:, 0:512], in1=rbc[:], op=ALU.mult), [bankB[4], RBC], [MST[mi]])
                K.dma(K.pool, [(mixT[g][H + h, :, qt * 512:(qt + 1) * 512], mstg[mi][:])], [MST[mi]], [mixB[g]], partial=True)
                return
            aS, AS = accS[ep], ACS[ep]
            nb = 3
            for b in range(nb):
                ns = 3 if b < 2 else 2
                K.op(K.dve, lambda e: e.tensor_copy(out=aS[:, 3 * b:3 * b + ns, :], in_=banks[4 + b][:, 0:ns * VA].rearrange("p (s c) -> p s c", c=VA)),
                     [bankB[4 + b]], [AS], partial=(b > 0))
            K.op(K.dve, lambda e: e.reciprocal(out=rc[:, 0:8], in_=aS[:, 0:8, DV]), [AS], [RC])
            ob, OB = obf[ep], OBF[ep]
            K.op(K.dve, lambda e: e.tensor_scalar(out=rl[:, 0:8], in0=rc[:, 0:8], scalar1=neglam[:, 0:1], scalar2=None, op0=ALU.mult), [RC, CONST], [RC], partial=True)
            for qs in range(4):
                s0, s1 = da_slots[(0, qs)], da_slots[(1, qs)]
                K.op(K.dve, lambda e: e.tensor_scalar(out=t1[:], in0=aS[:, s1, 0:DV], scalar1=rl[:, s1:s1 + 1], scalar2=None, op0=ALU.mult), [AS, RC], [T1])
                K.op(K.dve, lambda e: e.scalar_tensor_tensor(out=o32[:, qs, :], in0=aS[:, s0, 0:DV], scalar=rc[:, s0:s0 + 1], in1=t1[:], op0=ALU.mult, op1=ALU.add),
                     [AS, RC, T1], [O32], partial=(qs > 0))
            rms_rstd(None, lambda s_: o32[:, s_, :], DV, 4, [O32], ssq, SSQ, junkB, JKB)
            for qs in range(4):
                K.op(K.dve, lambda e: e.scalar_tensor_tensor(out=ob[:, qs, :], in0=o32[:, qs, :], scalar=ssq[:, qs:qs + 1], in1=subg[:], op0=ALU.mult, op1=ALU.mult),
                     [O32, SSQ, CONST], [OB], partial=(qs > 0))
            hh = h

            def fin():
                bk, bb = banks[7], bankB[7]
                for qs in range(4):
                    K.mm(bb, bk[:, qs * 128:(qs + 1) * 128], ob[:, qs, :], identb[:], True, True, reads=[OB, CONST])
                mi = mst_i[0] % 2
                mst_i[0] += 1
                K.op(K.dve, lambda e: e.tensor_copy(out=mstg[mi][:], in_=bk[:, 0:512]), [bb], [MST[mi]])
                K.dma(K.pool, [(mixT[g][hh, :, qt * 512:(qt + 1) * 512], mstg[mi][:])], [MST[mi]], [mixB[g]], partial=True)
            pending.append((i + 4, fin))

        load_unit(0)
        load_q(0)
        load_q(1)
        qk_next = [0]

        def ensure_qk(upto):
            while qk_next[0] <= min(upto, len(steps) - 1):
                emit_qk(qk_next[0])
                qk_next[0] += 1

        for i in range(len(steps)):
            u, qt, kt = steps[i]
            if qt == 0 and kt == 0 and u + 1 < len(units):
                load_unit(u + 1)
            if kt == 0:
                load_q(u * NQT + qt + 2)
            ensure_qk(i + (3 if units[u][1] == "mla" else 1))
            emit_exp_pv(i)
            while pending and pending[0][0] <= i:
                pending.pop(0)[1]()
        while pending:
            pending.pop(0)[1]()
        phase_barrier(K, KVQ + QBUF + [KPE, RC, T1, O32, SSQ, JKB, RSUM, RBC] + EB + ACS + OBF + MST + bankB)

    with contextlib.ExitStack() as es:
        env = make_ffn_env(es, "C")
        xt, X = env["xt"], env["X"]
        ring = Ring(K, "ringC", NSL, cfg.slot_elems, es)
        mT = K.sbuf("mT", [128, 2 * H, T], BF16, es)
        MT = Buf(K, "mT")
        gate23 = K.sbuf("gate23", [128, 2, D], F32, es)
        G23 = Buf(K, "gate23")
        fing = K.sbuf("fing", [128, D], F32, es)
        FING = Buf(K, "fing")
        K.dma(K.sp, [(fing[:], final_norm_d.partition_broadcast(128))], [XIN], [FING])
        ctiles = [(g, ti) for g in range(cfg.NG) for ti in range(cfg.own_tiles)]
        for _ in ctiles:
            for p in range(D // 256):
                ring.plan(wb["w_o"][p], wbB["w_o"], 2 * H * 256)
            plan_ffn(ring, "f2")
        for (g, ti) in ctiles:
            t0 = ti * T
            if ti == 0:
                K.dma(K.sp, [(gate23[:, l - 1, :], gbc[l, g]) for l in (1, 2)], [gbcB], [G23])
            K.dma(K.sp, [(xt[:], x1_d[g][t0:t0 + T, :].rearrange("(s p) d -> p s d", p=128))], [x1B[g]], [X])
            K.dma(K.sp, [(mT[:], mixT[g][:, :, t0:t0 + T].rearrange("h p t -> p h t"))], [mixB[g]], [MT])
            for p in range(D // 256):
                wt, wB = ring.take()
                for s_ in range(NSUB):
                    bk, bb = rr()
                    for kc in range(2 * H):
                        K.mm(bb, bk[:, 0:256], mT[:, kc, s_ * 128:(s_ + 1) * 128], wt[:, kc * 256:(kc + 1) * 256], kc == 0, kc == 2 * H - 1, reads=[wB, MT])
                    i = env["cnt"] % 2
                    env["cnt"] += 1
                    K.op(K.dve, lambda e: e.tensor_tensor(out=env["utmp"][i][:, 0:256], in0=bk[:, 0:256], in1=gate23[:, 0, p * 256:(p + 1) * 256], op=ALU.mult),
                         [bb, G23], [env["UT"][i]])
                    K.op(K.pool, lambda e: e.tensor_tensor(out=xt[:, s_, p * 256:(p + 1) * 256], in0=xt[:, s_, p * 256:(p + 1) * 256], in1=env["utmp"][i][:, 0:256], op=ALU.add),
                         [env["UT"][i], X], [X], partial=True)
                ring.done(1)
            d = dbg_out(f"x2{g}", (nq, D))
            if d is not None:
                K.dma(K.pool, [(d[t0:t0 + T, :].rearrange("(s p) d -> p s d", p=128), xt[:])], [X], [])
            norm_to_hT(env, 2, g)
            ffn(env, ring, gate23[:, 1, :], G23)
            d = dbg_out(f"x3{g}", (nq, D))
            if d is not None:
                K.dma(K.pool, [(d[t0:t0 + T, :].rearrange("(s p) d -> p s d", p=128), xt[:])], [X], [])
            rms_rstd(None, lambda s_: xt[:, s_, :], D, NSUB, [X], env["ss"], env["SS"], env["junk"], env["JB"])
            for s_ in range(NSUB):
                K.op(K.dve, lambda e: e.scalar_tensor_tensor(out=xt[:, s_, :], in0=xt[:, s_, :], scalar=env["ss"][:, s_:s_ + 1], in1=fing[:], op0=ALU.mult, op1=ALU.mult),
                     [X, env["SS"], FING], [X], partial=True)
            K.dma(K.pool, [(y_d[g][t0:t0 + T, :].rearrange("(s p) d -> p s d", p=128), xt[:])], [X], [])
        phase_barrier(K, [X, env["HT"], env["GT"], MT, G23, FING, env["SS"], env["JB"]] + env["XS"] + env["STMP"] + env["UT"] + ring.bufs + bankB)

    return nc, K, locals()


def phase_barrier(K, bufs):
    for E in (K.pe, K.act, K.dve, K.pool, K.sp):
        for b in bufs:
            for ev in list(b.w.values()) + list(b.r.values()):
                E.wait(ev, False)


_CACHE = {}


def kernel(**inputs):
    cfg = Cfg()
    if "nc" not in _CACHE:
        _CACHE["nc"] = build_program(cfg)[0]
    nc = _CACHE["nc"]
    maps = prepare_inputs(cfg, inputs)
    res = run_bass_kernel_spmd(nc, maps, core_ids=list(range(cfg.n_cores)))
    nq = cfg.nq
    half = cfg.n_cores // 2
    y_p = np.zeros((1, cfg.Sp, cfg.D), np.float32)
    y_s = np.zeros((half // 2, cfg.Ss, cfg.D), np.float32)
    for c in range(cfg.n_cores):
        r = res.results[c]["y_p"]
        if c < half:
            y_p[0, c * nq:(c + 1) * nq] = r
        else:
            sq, j = (c - half) // 2, (c - half) % 2
            y_s[sq, j * nq:(j + 1) * nq] = r
    return (y_p, y_s)
```

```python
import contextlib
import math
import numpy as np
import concourse.bass as bass
import concourse.mybir as mybir
from concourse.bass_utils import run_bass_kernel_spmd

F32 = mybir.dt.float32
BF16 = mybir.dt.bfloat16
AF = mybir.ActivationFunctionType
ALU = mybir.AluOpType

EPS = 1e-6
ROPE_THETA = 500000.0
T = 512
NSUB = 4
QK = 64
DV = 128
NOPE = 128
ROPE = 64
DA_ROT = 16
VA = DV + 1


class Cfg:
    def __init__(self, D=2048, FF=5632, H=8, QR=512, KVR=256, n_cores=8, sp=16384, ss=8192):
        self.D, self.FF, self.H, self.QR, self.KVR = D, FF, H, QR, KVR
        self.n_cores = n_cores
        self.Sp, self.Ss = sp, ss
        self.NG = 1
        nq = sp // (n_cores // 2)
        assert 2 * nq == ss and nq % T == 0
        self.nq = nq
        self.S = (sp,)
        self.KC = D // 128
        self.FCN = FF // 128
        self.QKC = QR // 128
        self.KVKC = KVR // 128
        self.ntile = (sp // T,)
        self.own_tiles = nq // T
        assert FF % 256 == 0 and D % 512 == 0 and QR % 128 == 0 and KVR % 128 == 0
        assert self.FCN % 4 == 0
        self.W2Q = 4
        self.W2F = self.FCN // self.W2Q
        self.NMOD = 9 * D
        HW = H * 128
        self.segs = [("q", HW), ("k", HW), ("v", HW), ("cq", QR), ("ckv", KVR), ("kpe", ROPE)]
        self.slot_elems = max(self.KC * 256, self.W2F * 512, self.QKC * 512, self.KVKC * 512, 2 * H * 256)
        assert H % 4 == 0


class Ev:
    __slots__ = ("sem", "sid", "val")

    def __init__(self, sem, sid, val):
        self.sem, self.sid, self.val = sem, sid, val


class Buf:
    def __init__(self, K, name, dma=False):
        self.name = name
        self.w = {}
        self.r = {}


class Eng:
    def __init__(self, K, name, eng, compute=True):
        self.K, self.name, self.e = K, name, eng
        self.seen = {}
        self.sid = K.new_sid()
        if compute:
            self.sem = K.new_sem("e_" + name)
            self.cnt = 0
        self.pend_r = []

    def wait(self, ev, same_ok):
        if self.seen.get(ev.sid, 0) >= ev.val:
            return
        self.e.wait_ge(ev.sem, ev.val)
        self.seen[ev.sid] = ev.val


class Kern:
    def __init__(self, nc):
        self.nc = nc
        self.es = contextlib.ExitStack()
        self._sid = 0
        self.nsem = 0
        self.pe = Eng(self, "pe", nc.tensor)
        self.act = Eng(self, "act", nc.scalar)
        self.dve = Eng(self, "dve", nc.vector)
        self.pool = Eng(self, "pool", nc.gpsimd)
        self.sp = Eng(self, "sp", nc.sync, compute=False)
        self.dsems = {"sp": [[self.new_sem("dsp%d" % i), self.new_sid(), 0] for i in range(24)],
                      "pool": [[self.new_sem("dpl%d" % i), self.new_sid(), 0] for i in range(8)]}
        self.dnext = {"sp": 0, "pool": 0}
        self.dram_bufs = []

    def new_sid(self):
        self._sid += 1
        return self._sid

    def new_sem(self, name):
        self.nsem += 1
        return self.es.enter_context(self.nc.semaphore(name))

    def sbuf(self, name, shape, dt, es=None):
        return (es or self.es).enter_context(self.nc.sbuf_tensor(name, list(shape), dt))

    def psum(self, name, shape, dt, es=None):
        return (es or self.es).enter_context(self.nc.psum_tensor(name, list(shape), dt))

    def _waits(self, E, reads, writes):
        for b in reads:
            for ev in b.w.values():
                E.wait(ev, False)
        for b in writes:
            for ev in b.w.values():
                E.wait(ev, True)
            for ev in b.r.values():
                E.wait(ev, True)

    def op(self, E, fn, reads=(), writes=(), partial=False):
        self._waits(E, reads, writes)
        ins = fn(E.e)
        E.cnt += 1
        ins.then_inc(E.sem, 1)
        ev = Ev(E.sem, E.sid, E.cnt)
        for b in reads:
            b.r[E.sid] = ev
        for b in writes:
            if partial:
                b.w[E.sid] = ev
            else:
                b.w = {E.sid: ev}
            b.r = {}
        return ev

    def mm(self, bank, out, lhsT, rhs, start, stop, reads=(), mark=None, **kw):
        E = self.pe
        for b in reads:
            for ev in b.w.values():
                E.wait(ev, False)
        if start:
            for ev in bank.w.values():
                E.wait(ev, True)
            for ev in bank.r.values():
                E.wait(ev, True)
        ins = E.e.matmul(out, lhsT, rhs, start=start, stop=stop, **kw)
        for b in reads:
            if b not in E.pend_r:
                E.pend_r.append(b)
        if mark is None:
            mark = stop
        if mark:
            E.cnt += 1
            ins.then_inc(E.sem, 1)
            ev = Ev(E.sem, E.sid, E.cnt)
            for b in E.pend_r:
                b.r[E.sid] = ev
            E.pend_r = []
            if stop:
                bank.w = {E.sid: ev}
                bank.r = {}
        return ins

    def dma(self, Q, pairs, reads, writes, sb=None, partial=False, **kw):
        for b in reads:
            for ev in b.w.values():
                Q.wait(ev, False)
        for b in writes:
            for ev in list(b.w.values()) + list(b.r.values()):
                Q.wait(ev, False)
        evs = []
        for (o, i) in pairs:
            pool = self.dsems[Q.name]
            st = pool[self.dnext[Q.name] % len(pool)]
            self.dnext[Q.name] += 1
            if st[2] > 0:
                Q.wait(Ev(st[0], st[1], st[2]), False)
            ins = Q.e.dma_start(out=o, in_=i, **kw)
            st[2] += 16
            ins.then_inc(st[0], 16)
            evs.append(Ev(st[0], st[1], st[2]))
        for b in reads:
            for ev in evs:
                b.r[ev.sid] = ev
        for b in writes:
            if not partial:
                b.w = {}
            for ev in evs:
                b.w[ev.sid] = ev
            b.r = {}
        return evs


class Ring:
    def __init__(self, K, name, ns, elems, es):
        self.K = K
        self.ns = ns
        self.tiles = [K.sbuf(f"{name}{i}", [128, elems], BF16, es) for i in range(ns)]
        self.bufs = [Buf(K, f"{name}{i}", dma=True) for i in range(ns)]
        self.pieces = []
        self.issued = 0
        self.taken = 0
        self.consumed = 0

    def plan(self, src_ap, dbuf, elems):
        self.pieces.append((src_ap, dbuf, elems))

    def _issue(self):
        while self.issued < len(self.pieces) and self.issued - self.ns < self.consumed:
            i = self.issued
            src, dbuf, elems = self.pieces[i]
            s = i % self.ns
            self.K.dma(self.K.sp, [(self.tiles[s][:, 0:elems], src)], [dbuf], [self.bufs[s]])
            self.issued += 1

    def take(self):
        i = self.taken
        assert i < len(self.pieces), "ring plan exhausted"
        self._issue()
        assert i < self.issued, "ring too small for this consumption pattern"
        self.taken += 1
        s = i % self.ns
        return self.tiles[s], self.bufs[s]

    def done(self, n=1):
        self.consumed += n
        assert self.consumed <= self.taken
        self._issue()


def _pieces_kxn(w, ncols_piece):
    Kd, N = w.shape
    kc = Kd // 128
    npc = (N + ncols_piece - 1) // ncols_piece
    out = np.zeros((npc, 128, kc, ncols_piece), np.float32)
    for p in range(npc):
        c0 = p * ncols_piece
        c1 = min(N, c0 + ncols_piece)
        blk = w[:, c0:c1].reshape(kc, 128, c1 - c0).transpose(1, 0, 2)
        out[p, :, :, : c1 - c0] = blk
    return out.reshape(npc, 128, kc * ncols_piece)


def _pieces_w2(w2, cfg):
    FF, D = w2.shape
    ndb = D // 512
    out = np.zeros((ndb, cfg.W2Q, 128, cfg.W2F, 512), np.float32)
    w = w2.reshape(cfg.W2Q, cfg.W2F, 128, ndb, 512)
    out[:] = w.transpose(3, 0, 2, 1, 4)
    return out.reshape(ndb * cfg.W2Q, 128, cfg.W2F * 512)


def _rope_tables(pos):
    pos = pos.astype(np.float32)

    def tab(dim):
        inv = (np.float32(ROPE_THETA) ** (-np.arange(0, dim, 2, dtype=np.float32) / np.float32(dim))).astype(np.float32)
        ang = (pos[:, None] * inv[None, :]).astype(np.float32)
        c, s = np.cos(ang).astype(np.float32), np.sin(ang).astype(np.float32)
        return np.concatenate([c, c], 1), np.concatenate([-s, s], 1)

    c16, s16 = tab(DA_ROT)
    c64, s64 = tab(ROPE)
    return np.concatenate([c16, s16, c64, s64], 1).astype(np.float32)


def prepare_inputs(cfg, inp):
    D, H = cfg.D, cfg.H
    g = lambda k: np.asarray(inp[k], np.float32)
    shared = {}
    for name, tag in (("ffn1", "f1"), ("ffn2", "f2")):
        shared[f"{tag}_w1"] = _pieces_kxn(g(f"{name}_w1")[0], 256)
        shared[f"{tag}_w3"] = _pieces_kxn(g(f"{name}_w3")[0], 256)
        shared[f"{tag}_w2"] = _pieces_w2(g(f"{name}_w2")[0], cfg)
    w_in = g("w_in")[0]
    segs = []
    c0 = 0
    for (nm, n) in cfg.segs:
        segs.append(_pieces_kxn(w_in[:, c0:c0 + n], 256))
        c0 += n
    shared["w_in"] = np.concatenate(segs, 0)
    wuq = g("mla_w_uq")[0].reshape(cfg.QR, H, NOPE + ROPE)
    wuq = np.concatenate([wuq[:, :, :NOPE].reshape(cfg.QR, H * NOPE), wuq[:, :, NOPE:].reshape(cfg.QR, H * ROPE)], 1)
    shared["w_uq"] = np.concatenate([_pieces_kxn(wuq[:, :H * NOPE], 512), _pieces_kxn(wuq[:, H * NOPE:], 512)], 0)
    wukv = g("mla_w_ukv")[0].reshape(cfg.KVR, H, NOPE + DV)
    wukv = np.concatenate([wukv[:, :, :NOPE].reshape(cfg.KVR, H * NOPE), wukv[:, :, NOPE:].reshape(cfg.KVR, H * DV)], 1)
    shared["w_ukv"] = np.concatenate([_pieces_kxn(wukv[:, :H * NOPE], 512), _pieces_kxn(wukv[:, H * NOPE:], 512)], 0)
    shared["w_o"] = _pieces_kxn(g("w_o")[0], 256)
    shared["w_ada"] = _pieces_kxn(g("w_ada")[0], 512)
    shared["b_ada"] = np.ascontiguousarray(np.broadcast_to(g("b_ada")[0][None, :], (2, cfg.NMOD)))
    ncol = np.stack([g("ffn1_norm")[0], g("attn_norm")[0], g("ffn2_norm")[0]], 0)
    shared["ncol"] = np.ascontiguousarray(ncol.reshape(3, cfg.KC, 128).transpose(2, 0, 1))
    shared["final_norm"] = g("final_norm").reshape(1, D)
    shared["q_norm"] = g("mla_q_norm").reshape(1, cfg.QR)
    shared["kv_norm"] = g("mla_kv_norm").reshape(1, cfg.KVR)
    shared["subln"] = g("da_subln").reshape(1, DV)
    shared["lambdas"] = np.concatenate([g("da_lambda_q1")[0], g("da_lambda_k1")[0], g("da_lambda_q2")[0], g("da_lambda_k2")[0]]).reshape(1, 4 * QK)
    shared["ident"] = np.eye(128, dtype=np.float32)
    sel = np.zeros((2, 4, 128), np.float32)
    sel[0, 0] = 1.0
    sel[1, 1] = 1.0
    sel[0, 2] = 0.5
    sel[1, 3] = 0.5
    shared["sel"] = sel

    xp = g("x_prompt")
    xs = g("x_sample")
    cp = g("c_prompt")
    cs = g("c_sample")
    nq = cfg.nq
    Sp, Ss = cfg.Sp, cfg.Ss
    half = cfg.n_cores // 2
    maps = []
    for c in range(cfg.n_cores):
        m = dict(shared)
        xa = np.zeros((Sp, D), np.float32)
        pos = np.zeros((Sp,), np.int64)
        flag = np.zeros((Sp, 1), np.float32)
        if c < half:
            seq, cvec, j, L = xp[0], cp[0], c, Sp
        else:
            sq = (c - half) // 2
            seq, cvec, j, L = xs[sq], cs[sq], (c - half) % 2, Ss
        order = np.concatenate([np.arange(j * nq, (j + 1) * nq), np.arange(0, j * nq), np.arange((j + 1) * nq, L)])
        xa[:L] = seq[order]
        pos[:L] = order
        flag[:L] = 1.0
        m["x_p"] = xa
        m["rope_p"] = np.concatenate([_rope_tables(pos), flag], 1)
        cc = np.stack([cvec, cvec], 0)
        m["cT"] = np.ascontiguousarray(cc.reshape(2, cfg.KC, 128).transpose(2, 1, 0))
        maps.append(m)
    return maps


def cdiv(a, b):
    return (a + b - 1) // b


def build_program(cfg, debug=()):
    nc = bass.Bass("TRN2", target_bir_lowering=False)
    K = Kern(nc)
    D, FF, H, QR, KVR, KC, FCN = cfg.D, cfg.FF, cfg.H, cfg.QR, cfg.KVR, cfg.KC, cfg.FCN
    HW = H * 128
    nq = cfg.nq
    NDB = D // 512
    NMOD = cfg.NMOD
    QKC, KVKC = cfg.QKC, cfg.KVKC

    def din(name, shape, dt=F32):
        return nc.dram_tensor(name, list(shape), dt, kind="ExternalInput").ap()

    def dscr(name, shape, dt):
        if name in debug:
            return nc.dram_tensor(name, list(shape), dt, kind="ExternalOutput").ap()
        return nc.dram_tensor(name, list(shape), dt).ap()

    NP13 = FF // 256
    NPW2 = NDB * cfg.W2Q
    seg_np = [cdiv(n, 256) for (_, n) in cfg.segs]
    seg_base = [sum(seg_np[:i]) for i in range(len(seg_np))]
    NPIN = sum(seg_np)
    NUQ_N, NUQ_R = HW // 512, cdiv(H * ROPE, 512)
    NUKV = HW // 512
    wshapes = {
        "f1_w1": (NP13, KC * 256), "f1_w3": (NP13, KC * 256), "f1_w2": (NPW2, cfg.W2F * 512),
        "w_in": (NPIN, KC * 256), "w_uq": (NUQ_N + NUQ_R, QKC * 512), "w_ukv": (2 * NUKV, KVKC * 512),
        "w_o": (D // 256, 2 * H * 256),
        "f2_w1": (NP13, KC * 256), "f2_w3": (NP13, KC * 256), "f2_w2": (NPW2, cfg.W2F * 512),
    }
    w32 = {k: din(k, (v[0], 128, v[1])) for k, v in wshapes.items()}
    wb = {k: dscr("wb_" + k, (v[0], 128, v[1]), BF16) for k, v in wshapes.items()}
    wbB = {k: Buf(K, "wb_" + k) for k in wshapes}
    w_ada = din("w_ada", (NMOD // 512, 128, KC * 512))
    b_ada = din("b_ada", (2, NMOD))
    ncol_d = din("ncol", (128, 3, KC))
    final_norm_d = din("final_norm", (1, D))
    q_norm_d = din("q_norm", (1, QR))
    kv_norm_d = din("kv_norm", (1, KVR))
    subln_d = din("subln", (1, DV))
    lambdas_d = din("lambdas", (1, 4 * QK))
    ident_d = din("ident", (128, 128))
    sel_d = din("sel", (2, 4, 128))
    x_d = [din("x_p", (cfg.S[0], D))]
    rope_d = [din("rope_p", (cfg.S[0], 161))]
    cT_d = din("cT", (128, KC, 2))
    y_d = [nc.dram_tensor(n, [nq, D], F32, kind="ExternalOutput").ap() for n in ("y_p",)]
    dbg = {}

    def dbg_out(name, shape, dt=F32):
        if name in dbg:
            return dbg[name]
        if name in debug:
            dbg[name] = nc.dram_tensor("dbg_" + name, list(shape), dt, kind="ExternalOutput").ap()
            return dbg[name]
        return None

    gbc = dscr("gbc", (3, 2, 128, D), F32)
    gbcB = Buf(K, "gbc")
    x1_d = [dscr(f"x1_{g}", (nq, D), F32) for g in range(cfg.NG)]
    x1B = [Buf(K, f"x1_{g}") for g in range(cfg.NG)]
    QT_da = [dscr(f"QT_da{g}", (H, 128, nq), BF16) for g in range(cfg.NG)]
    QT_nope = [dscr(f"QT_nope{g}", (H, 128, nq), BF16) for g in range(cfg.NG)]
    QT_pe = [dscr(f"QT_pe{g}", (H // 2, 128, nq), BF16) for g in range(cfg.NG)]
    KT_da = [dscr(f"KT_da{g}", (H, 128, cfg.S[g]), BF16) for g in range(cfg.NG)]
    KT_nope = [dscr(f"KT_nope{g}", (H, 128, cfg.S[g]), BF16) for g in range(cfg.NG)]
    KT_pe = [dscr(f"KT_pe{g}", (64, cfg.S[g]), BF16) for g in range(cfg.NG)]
    V_da = [dscr(f"V_da{g}", (H, 128, cfg.S[g] // 128, VA), BF16) for g in range(cfg.NG)]
    V_mla = [dscr(f"V_mla{g}", (H, 128, cfg.S[g] // 128, VA), BF16) for g in range(cfg.NG)]
    mixT = [dscr(f"mixT{g}", (2 * H, 128, nq), BF16) for g in range(cfg.NG)]
    QKVB = [Buf(K, f"qkv{g}") for g in range(cfg.NG)]
    mixB = [Buf(K, f"mix{g}") for g in range(cfg.NG)]
    XIN = Buf(K, "xin")

    ident32 = K.sbuf("ident32", [128, 128], F32)
    identb = K.sbuf("identb", [128, 128], BF16)
    modcol = K.sbuf("modcol", [128, 3, 2, KC, 2], F32)
    epsT = K.sbuf("epsT", [128, 1], F32)
    neglam = K.sbuf("neglam", [128, 1], F32)
    subg = K.sbuf("subg", [128, DV], F32)
    qng = K.sbuf("qng", [128, QR], F32)
    kvng = K.sbuf("kvng", [128, KVR], F32)
    ones_row = K.sbuf("ones_row", [1, 128], F32)
    CONST = Buf(K, "const", dma=True)
    MODC = Buf(K, "modcol")
    banks = [K.psum(f"bank{i}", [128, 512], F32) for i in range(8)]
    bankB = [Buf(K, f"bank{i}") for i in range(8)]
    rr_state = [0]

    def rr():
        i = rr_state[0] % 4
        rr_state[0] += 1
        return banks[i], bankB[i]

    acc = [(banks[4 + i], bankB[4 + i]) for i in range(4)]
    ev_state = [0]

    def evac_eng():
        ev_state[0] += 1
        return K.act if ev_state[0] % 2 else K.dve

    def copy_op(E, out, in_, reads, writes, partial=False):
        if E is K.act:
            return K.op(E, lambda e: e.copy(out=out, in_=in_), reads, writes, partial)
        return K.op(E, lambda e: e.tensor_copy(out=out, in_=in_), reads, writes, partial)

    def scale_copy_op(E, out, in_, sc, reads, writes):
        if E is K.act:
            return K.op(E, lambda e: e.activation(out=out, in_=in_, func=AF.Identity, scale=sc), reads, writes, True)
        return K.op(E, lambda e: e.tensor_scalar(out=out, in0=in_, scalar1=sc, scalar2=None, op0=ALU.mult), reads, writes, True)

    K.dma(K.sp, [(ident32[:], ident_d), (qng[:], q_norm_d.partition_broadcast(128)),
                 (kvng[:], kv_norm_d.partition_broadcast(128)), (subg[:], subln_d.partition_broadcast(128))],
          [XIN], [CONST], CONST)
    K.op(K.dve, lambda e: e.tensor_copy(out=identb[:], in_=ident32[:]), [CONST], [CONST], partial=True)
    K.op(K.dve, lambda e: e.memset(epsT[:], EPS), [], [CONST], partial=True)
    K.op(K.dve, lambda e: e.memset(ones_row[:], 1.0), [], [CONST], partial=True)
    K.op(K.dve, lambda e: e.tensor_scalar(out=subg[:], in0=subg[:], scalar1=0.8, scalar2=None, op0=ALU.mult), [CONST], [CONST], partial=True)

    with contextlib.ExitStack() as es:
        cTs = K.sbuf("cTs", [128, KC, 2], F32, es)
        siluT = K.sbuf("siluT", [128, KC, 2], F32, es)
        b2 = [K.sbuf(f"b2_{i}", [2, 512], F32, es) for i in range(2)]
        mrow = K.sbuf("mrow", [2, NMOD], F32, es)
        wblk = [K.sbuf(f"wblk{i}", [128, KC * 512], F32, es) for i in range(2)]
        wblkB = [Buf(K, f"wblk{i}", dma=True) for i in range(2)]
        ncols = K.sbuf("ncols", [128, 3, KC], F32, es)
        sels = K.sbuf("sels", [2, 4, 128], F32, es)
        mcol = K.sbuf("mcol", [128, 6, KC, 2], F32, es)
        gst = [K.sbuf(f"gst{i}", [128, D], F32, es) for i in range(2)]
        gstB = [Buf(K, f"gst{i}", dma=True) for i in range(2)]
        lambf = K.sbuf("lamb", [128, 4 * QK], F32, es)
        lamb = lambf[:].rearrange("p (a b) -> p a b", b=QK)
        prod = K.sbuf("prod", [128, 2, QK], F32, es)
        s12 = K.sbuf("s12", [128, 2], F32, es)
        MB = Buf(K, "mphase", dma=True)
        MROW = Buf(K, "mrow")
        K.dma(K.sp, [(cTs[:], cT_d), (ncols[:], ncol_d), (sels[:], sel_d),
                     (lambf[:], lambdas_d.partition_broadcast(128))], [XIN], [MB], MB)
        K.op(K.act, lambda e: e.activation(out=siluT[:], in_=cTs[:], func=AF.Silu), [MB], [MB], partial=True)
        NB = NMOD // 512
        for nb in range(NB):
            i = nb % 2
            K.dma(K.sp, [(wblk[i][:], w_ada[nb]), (b2[i][:], b_ada[:, nb * 512:(nb + 1) * 512])], [XIN], [wblkB[i]], wblkB[i])
            bk, bb = rr()
            for kc in range(KC):
                K.mm(bb, bk[0:2, 0:512], siluT[:, kc, :], wblk[i][:, kc * 512:(kc + 1) * 512], kc == 0, kc == KC - 1, reads=[wblkB[i], MB])
            K.op(K.dve, lambda e: e.tensor_tensor(out=mrow[:, nb * 512:(nb + 1) * 512], in0=bk[0:2, 0:512], in1=b2[i][:], op=ALU.add),
                 [bb, wblkB[i]], [MROW], partial=True)
        vec_idx = [0, 1, 3, 4, 6, 7]
        bk, bb = rr()
        for vi, v in enumerate(vec_idx):
            for kc in range(KC):
                o = (vi * KC + kc) * 2
                K.mm(bb, bk[:, o:o + 2], mrow[0:2, v * D + kc * 128: v * D + (kc + 1) * 128], ident32[0:2, 0:2], True, True, reads=[MROW, CONST])
        K.op(K.dve, lambda e: e.tensor_copy(out=mcol[:].rearrange("p a k g -> p (a k g)"), in_=bk[:, 0:6 * KC * 2]), [bb], [MB], partial=True)
        for l in range(3):
            K.op(K.dve, lambda e: e.scalar_tensor_tensor(out=modcol[:, l, 0, :, :], in0=mcol[:, 2 * l + 1, :, :], scalar=1.0,
                                                         in1=ncols[:, l, :].unsqueeze(2).broadcast_to([128, KC, 2]), op0=ALU.add, op1=ALU.mult),
                 [MB], [MODC], partial=True)
            K.op(K.dve, lambda e: e.tensor_copy(out=modcol[:, l, 1, :, :], in_=mcol[:, 2 * l, :, :]), [MB], [MODC], partial=True)
        gi = 0
        for l in range(3):
            v = 3 * l + 2
            for g in range(2):
                si = g + (0 if l == 1 else 2)
                st, sB = gst[gi % 2], gstB[gi % 2]
                gi += 1
                for db in range(NDB):
                    bk, bb = rr()
                    K.mm(bb, bk[:, 0:512], sels[0:2, si, :], mrow[0:2, v * D + db * 512: v * D + (db + 1) * 512], True, True, reads=[MROW, MB])
                    copy_op(evac_eng(), st[:, db * 512:(db + 1) * 512], bk[:, 0:512], [bb], [sB], partial=True)
                K.dma(K.pool, [(gbc[l, g], st[:])], [sB], [gbcB], sB, partial=True)
        K.op(K.dve, lambda e: e.tensor_tensor(out=prod[:, 0, :], in0=lamb[:, 0, :], in1=lamb[:, 1, :], op=ALU.mult), [MB], [MB], partial=True)
        K.op(K.dve, lambda e: e.tensor_tensor(out=prod[:, 1, :], in0=lamb[:, 2, :], in1=lamb[:, 3, :], op=ALU.mult), [MB], [MB], partial=True)
        K.op(K.dve, lambda e: e.reduce_sum(out=s12[:], in_=prod[:], axis=mybir.AxisListType.X), [MB], [MB], partial=True)
        K.op(K.act, lambda e: e.activation(out=s12[:], in_=s12[:], func=AF.Exp), [MB], [MB], partial=True)
        K.op(K.dve, lambda e: e.tensor_tensor(out=neglam[:], in0=s12[:, 1:2], in1=s12[:, 0:1], op=ALU.subtract), [MB], [CONST], partial=True)
        K.op(K.dve, lambda e: e.tensor_scalar(out=neglam[:], in0=neglam[:], scalar1=-0.2, scalar2=None, op0=ALU.add), [CONST], [CONST], partial=True)
        d = dbg_out("modcol", (128, 3 * 2 * KC * 2))
        if d is not None:
            K.dma(K.pool, [(d, modcol[:].rearrange("p l a k g -> p (l a k g)"))], [MODC], [], MB)
        d = dbg_out("neglam", (128, 1))
        if d is not None:
            K.dma(K.pool, [(d, neglam[:])], [CONST], [], MB)
        phase_barrier(K, [MB, MROW, MODC, CONST] + wblkB + gstB)

    with contextlib.ExitStack() as es:
        EMAX = max(v[1] for v in wshapes.values())
        NST = 3
        st32 = [K.sbuf(f"st32_{i}", [128, EMAX], F32, es) for i in range(NST)]
        st16 = [K.sbuf(f"st16_{i}", [128, EMAX], BF16, es) for i in range(NST)]
        s32B = [Buf(K, f"st32_{i}", dma=True) for i in range(NST)]
        s16B = [Buf(K, f"st16_{i}", dma=True) for i in range(NST)]
        ci = 0
        engs = [K.dve, K.act]
        for name, (npc, E) in wshapes.items():
            for p in range(npc):
                i = ci % NST
                ci += 1
                K.dma(K.sp, [(st32[i][:, 0:E], w32[name][p])], [XIN], [s32B[i]], s32B[i])
                copy_op(engs[ci % 2], st16[i][:, 0:E], st32[i][:, 0:E], [s32B[i]], [s16B[i]])
                K.dma(K.pool, [(wb[name][p], st16[i][:, 0:E])], [s16B[i]], [wbB[name]], s16B[i], partial=True)
        phase_barrier(K, s32B + s16B)


    NSL = 4
    HWp = HW // 256
    BIGN = max(FCN * 512, 4 * HW + 2 * 4 * H * VA + 4 * KVR + 4 * ROPE + 4 * HW + D)

    def rms_rstd(ES, src_fn, n, nsub, reads, ssb, ssB, junk, JB):
        K.op(K.dve, lambda e: e.memset(ssb[:, 0:nsub], 0.0), [], [ssB])
        for s_ in range(nsub):
            K.op(K.act, lambda e: e.activation(out=junk[:, 0:n], in_=src_fn(s_), func=AF.Square, accum_out=ssb[:, s_:s_ + 1]),
                 reads + [ssB], [JB, ssB], partial=True)
        K.op(K.act, lambda e: e.activation(out=ssb[:, 0:nsub], in_=ssb[:, 0:nsub], func=AF.Sqrt, scale=1.0 / n, bias=epsT[:, 0:1]), [ssB, CONST], [ssB])
        K.op(K.dve, lambda e: e.reciprocal(out=ssb[:, 0:nsub], in_=ssb[:, 0:nsub]), [ssB], [ssB])

    def make_ffn_env(es, tag):
        env = {}
        env["xt"] = K.sbuf("xt" + tag, [128, NSUB, D], F32, es)
        env["X"] = Buf(K, "X" + tag)
        env["hT"] = K.sbuf("hT" + tag, [128, 2 * H if False else KC, T], BF16, es)
        env["HT"] = Buf(K, "HT" + tag)
        env["big"] = K.sbuf("big" + tag, [128, BIGN], BF16, es)
        env["GT"] = Buf(K, "GT" + tag)
        env["stmp"] = [K.sbuf(f"stmp{tag}{i}", [128, T], F32, es) for i in range(2)]
        env["STMP"] = [Buf(K, f"stmp{tag}{i}") for i in range(2)]
        env["ss"] = K.sbuf("ss" + tag, [128, NSUB], F32, es)
        env["SS"] = Buf(K, "ss" + tag)
        env["xs"] = [K.sbuf(f"xs{tag}{i}", [128, D], BF16, es) for i in range(2)]
        env["XS"] = [Buf(K, f"xs{tag}{i}") for i in range(2)]
        env["junk"] = env["xs"][0]
        env["JB"] = env["XS"][0]
        env["utmp"] = env["stmp"]
        env["UT"] = env["STMP"]
        env["cnt"] = 0
        return env

    def norm_to_hT(env, l, g):
        xt, X = env["xt"], env["X"]
        rms_rstd(None, lambda s_: xt[:, s_, :], D, NSUB, [X], env["ss"], env["SS"], env["junk"], env["JB"])
        first = True
        base = env["cnt"]
        env["cnt"] += NSUB

        def prescale(s_):
            i = (base + s_) % 2
            K.op(K.act, lambda e: e.activation(out=env["xs"][i][:], in_=xt[:, s_, :], func=AF.Identity, scale=env["ss"][:, s_:s_ + 1]), [X, env["SS"]], [env["XS"][i]])

        prescale(0)
        for s_ in range(NSUB):
            i = (base + s_) % 2
            xs, XS = env["xs"][i], env["XS"][i]
            bks = []
            for kg in range(KC // 4):
                bk, bb = rr()
                bks.append((bk, bb))
                for j in range(4):
                    kc = kg * 4 + j
                    K.mm(bb, bk[:, j * 128:(j + 1) * 128], xs[:, kc * 128:(kc + 1) * 128], identb[:], True, True, reads=[XS, CONST])
                if kg == 0 and s_ + 1 < NSUB:
                    prescale(s_ + 1)
            for kg, (bk, bb) in enumerate(bks):
                E = evac_eng()
                for j in range(4):
                    kc = kg * 4 + j
                    a_ap = modcol[:, l, 0, kc, g:g + 1]
                    b_ap = modcol[:, l, 1, kc, g:g + 1]
                    o_ap = env["hT"][:, kc, s_ * 128:(s_ + 1) * 128]
                    i_ap = bk[:, j * 128:(j + 1) * 128]
                    if E is K.act:
                        K.op(E, lambda e: e.activation(out=o_ap, in_=i_ap, func=AF.Identity, scale=a_ap, bias=b_ap), [bb, MODC], [env["HT"]], partial=not first)
                    else:
                        K.op(E, lambda e: e.tensor_scalar(out=o_ap, in0=i_ap, scalar1=a_ap, scalar2=b_ap, op0=ALU.mult, op1=ALU.add), [bb, MODC], [env["HT"]], partial=not first)
                    first = False

    def plan_ffn(ring, tag):
        for fp in range(NP13):
            ring.plan(wb[tag + "_w1"][fp], wbB[tag + "_w1"], KC * 256)
            ring.plan(wb[tag + "_w3"][fp], wbB[tag + "_w3"], KC * 256)
        for p in range(NPW2):
            ring.plan(wb[tag + "_w2"][p], wbB[tag + "_w2"], cfg.W2F * 512)

    def ffn(env, ring, gate_ap, GATEB):
        hT, HT, big, GT = env["hT"], env["HT"], env["big"], env["GT"]
        for fp in range(NP13):
            w1t, w1b = ring.take()
            w3t, w3b = ring.take()
            for fl in range(2):
                fc = fp * 2 + fl
                b1, B1 = rr()
                b3, B3 = rr()
                for kc in range(KC):
                    K.mm(B1, b1[:, 0:T], w1t[:, kc * 256 + fl * 128: kc * 256 + (fl + 1) * 128], hT[:, kc, :], kc == 0, kc == KC - 1, reads=[w1b, HT])
                for kc in range(KC):
                    K.mm(B3, b3[:, 0:T], w3t[:, kc * 256 + fl * 128: kc * 256 + (fl + 1) * 128], hT[:, kc, :], kc == 0, kc == KC - 1, reads=[w3b, HT])
                i = env["cnt"] % 2
                env["cnt"] += 1
                K.op(K.act, lambda e: e.activation(out=env["stmp"][i][:], in_=b1[:, 0:T], func=AF.Silu), [B1], [env["STMP"][i]])
                K.op(K.dve, lambda e: e.tensor_tensor(out=big[:, fc * T:(fc + 1) * T], in0=env["stmp"][i][:], in1=b3[:, 0:T], op=ALU.mult),
                     [env["STMP"][i], B3], [GT], partial=(fc > 0))
            ring.done(2)
        W2F = cfg.W2F
        for db in range(NDB):
            for q in range(cfg.W2Q):
                wt, wB = ring.take()
                for s_ in range(NSUB):
                    ab, AB = acc[s_]
                    for fl in range(W2F):
                        fc = q * W2F + fl
                        first = (q == 0 and fl == 0)
                        last = (q == cfg.W2Q - 1 and fl == W2F - 1)
                        K.mm(AB, ab[:, 0:512], big[:, fc * T + s_ * 128: fc * T + (s_ + 1) * 128], wt[:, fl * 512:(fl + 1) * 512], first, last,
                             reads=[wB, GT], mark=(last or (s_ == NSUB - 1 and fl == W2F - 1)))
                ring.done(1)
            for s_ in range(NSUB):
                ab, AB = acc[s_]
                i = env["cnt"] % 2
                env["cnt"] += 1
                K.op(K.dve, lambda e: e.tensor_tensor(out=env["utmp"][i][:], in0=ab[:, 0:512], in1=gate_ap[:, db * 512:(db + 1) * 512], op=ALU.mult),
                     [AB, GATEB], [env["UT"][i]])
                K.op(K.pool, lambda e: e.tensor_tensor(out=env["xt"][:, s_, db * 512:(db + 1) * 512], in0=env["xt"][:, s_, db * 512:(db + 1) * 512],
                                                        in1=env["utmp"][i][:], op=ALU.add),
                     [env["UT"][i], env["X"]], [env["X"]], partial=True)

    with contextlib.ExitStack() as es:
        env = make_ffn_env(es, "A")
        xt, X, hT, HT, big, GT = env["xt"], env["X"], env["hT"], env["HT"], env["big"], env["GT"]
        ring = Ring(K, "ringA", NSL, cfg.slot_elems, es)
        gate1 = K.sbuf("gate1", [128, D], F32, es)
        G1B = Buf(K, "gate1")
        ropet = K.sbuf("ropet", [128, NSUB, 161], F32, es)
        RP = Buf(K, "ropet")
        own_st = K.sbuf("own_st", [128, 4 * HW + 4 * QR + 4 * HW + 4 * H * ROPE], BF16, es)
        OWN = Buf(K, "own_st")
        cqnT = K.sbuf("cqnT", [128, QKC, T], BF16, es)
        CQT = Buf(K, "cqnT")
        ckvnT = K.sbuf("ckvnT", [128, KVKC, T], BF16, es)
        CKT = Buf(K, "ckvnT")
        NSTG = 4
        stg = [K.sbuf(f"stg{i}", [128, T], BF16, es) for i in range(NSTG)]
        STG = [Buf(K, f"stg{i}") for i in range(NSTG)]
        stg_i = [0]
        rtmp = [K.sbuf(f"rtmp{i}", [128, 512], F32, es) for i in range(2)]
        RT = [Buf(K, f"rtmp{i}") for i in range(2)]
        ss2 = K.sbuf("ss2", [128, NSUB], F32, es)
        SS2 = Buf(K, "ss2")
        qkt = [K.sbuf(f"qkt{i}", [128, 512], F32, es) for i in range(2)]
        QKT = [Buf(K, f"qkt{i}") for i in range(2)]
        o_ = [0]

        def carve(buf, n):
            a = buf[:, o_[0]:o_[0] + n]
            o_[0] += n
            return a
        kda_tok = carve(big, 4 * HW).rearrange("p (s n) -> p s n", s=NSUB)
        vda = carve(big, 4 * H * VA).rearrange("p (h s c) -> p h s c", s=NSUB, h=H)
        vmla = carve(big, 4 * H * VA).rearrange("p (h s c) -> p h s c", s=NSUB, h=H)
        ckvn = carve(big, 4 * KVR).rearrange("p (s n) -> p s n", s=NSUB)
        kpe_tok = carve(big, 4 * ROPE).rearrange("p (s n) -> p s n", s=NSUB)
        knope_tok = carve(big, 4 * HW).rearrange("p (s n) -> p s n", s=NSUB)
        o_[0] = 0
        qda_tok = carve(own_st, 4 * HW).rearrange("p (s n) -> p s n", s=NSUB)
        cqn = carve(own_st, 4 * QR).rearrange("p (s n) -> p s n", s=NSUB)
        qnope_tok = carve(own_st, 4 * HW).rearrange("p (s n) -> p s n", s=NSUB)
        qpe_tok = carve(own_st, 4 * H * ROPE).rearrange("p (s n) -> p s n", s=NSUB)

        tiles = []
        for g in range(cfg.NG):
            for ti in range(cfg.ntile[g]):
                tiles.append((g, ti, ti < cfg.own_tiles))
        for (g, ti, own) in tiles:
            plan_ffn(ring, "f1")
            for si, (nm, n) in enumerate(cfg.segs):
                if nm in ("q", "cq") and not own:
                    continue
                for p in range(seg_np[si]):
                    ring.plan(wb["w_in"][seg_base[si] + p], wbB["w_in"], KC * 256)
            if own:
                for p in range(NUQ_N + NUQ_R):
                    ring.plan(wb["w_uq"][p], wbB["w_uq"], QKC * 512)
            for p in range(2 * NUKV):
                ring.plan(wb["w_ukv"][p], wbB["w_ukv"], KVKC * 512)

        def rope_apply(bk, BB, ncol_blocks, blk, rot, dst, dstB, s_, tab_off):
            half = rot // 2
            src = bk[:, 0:ncol_blocks * blk].rearrange("p (b d) -> p b d", d=blk)
            Ct = ropet[:, s_, tab_off:tab_off + rot].unsqueeze(1).broadcast_to([128, ncol_blocks, rot])
            S1 = ropet[:, s_, tab_off + rot:tab_off + rot + half].unsqueeze(1).broadcast_to([128, ncol_blocks, half])
            S2 = ropet[:, s_, tab_off + rot + half:tab_off + 2 * rot].unsqueeze(1).broadcast_to([128, ncol_blocks, half])
            A = rtmp[0][:, 0:ncol_blocks * rot].rearrange("p (b d) -> p b d", d=rot)
            Bt = rtmp[1][:, 0:ncol_blocks * rot].rearrange("p (b d) -> p b d", d=rot)
            nops = int(([d.split(":")[1] for d in debug if d.startswith("ropeops:")] or ["4"])[0])
            if nops >= 1:
                K.op(K.dve, lambda e: e.tensor_tensor(out=A, in0=src[:, :, 0:rot], in1=Ct, op=ALU.mult), [BB, RP], [RT[0]])
            if nops >= 2:
                K.op(K.dve, lambda e: e.tensor_tensor(out=Bt[:, :, 0:half], in0=src[:, :, half:rot], in1=S1, op=ALU.mult), [BB, RP], [RT[1]])
            if nops >= 3:
                K.op(K.dve, lambda e: e.tensor_tensor(out=Bt[:, :, half:rot], in0=src[:, :, 0:half], in1=S2, op=ALU.mult), [BB, RP], [RT[1]], partial=True)
            if nops >= 4:
                K.op(K.dve, lambda e: e.tensor_tensor(out=dst[:, :, 0:rot], in0=A, in1=Bt, op=ALU.add), [RT[0], RT[1]], dstB, partial=True)

        def transpose_out(src_fn, nrows, dram_ap, reads, DST):
            bk, bb = rr()
            for s_ in range(NSUB):
                K.mm(bb, bk[0:nrows, s_ * 128:(s_ + 1) * 128], src_fn(s_), identb[:], True, True, reads=reads + [CONST])
            i = stg_i[0] % NSTG
            stg_i[0] += 1
            copy_op(evac_eng(), stg[i][0:nrows, :], bk[0:nrows, 0:T], [bb], [STG[i]])
            K.dma(K.sp, [(dram_ap, stg[i][0:nrows, :])], [STG[i]], DST, partial=True)

        stopA = [d.split(":")[1] for d in debug if d.startswith("stopA:")]
        stopA = stopA[0] if stopA else None

        def load_x(g, ti):
            K.dma(K.sp, [(xt[:], x_d[g][ti * T:(ti + 1) * T, :].rearrange("(s p) d -> p s d", p=128))], [XIN], [X])

        def do_tile(g, ti, own):
            t0 = ti * T
            if ti == 0:
                K.dma(K.sp, [(gate1[:], gbc[0, g])], [gbcB], [G1B])
            if (g, ti) == (0, 0):
                load_x(g, ti)
            K.dma(K.sp, [(ropet[:], rope_d[g][t0:t0 + T, :].rearrange("(s p) c -> p s c", p=128))], [XIN], [RP])
            if stopA == "load":
                return True
            norm_to_hT(env, 0, g)
            if stopA == "norm":
                return True
            if g == 0 and ti == 0:
                d = dbg_out("hT0", (128, KC * T), BF16)
                if d is not None:
                    K.dma(K.pool, [(d, hT[:].rearrange("p k t -> p (k t)"))], [HT], [])
            ffn(env, ring, gate1[:], G1B)
            if g == 0 and ti == 0:
                d = dbg_out("gt0", (128, FCN * T), BF16)
                if d is not None:
                    K.dma(K.pool, [(d, big[:, 0:FCN * T])], [GT], [])
                d = dbg_out("gate", (128, 2 * D))
                if d is not None:
                    K.dma(K.pool, [(d[:, 0:D], gate1[:])], [G1B], [])
            if stopA == "ffn":
                return True
            if own:
                K.dma(K.pool, [(x1_d[g][t0:t0 + T, :].rearrange("(s p) d -> p s d", p=128), xt[:])], [X], [x1B[g]], partial=True)
            if stopA == "x1s":
                return True
            norm_to_hT(env, 1, g)
            nxt = tiles.index((g, ti, own)) + 1
            if nxt < len(tiles) and stopA is None:
                load_x(tiles[nxt][0], tiles[nxt][1])
            if stopA == "norm2":
                return True
            BIGW = [GT]
            for s_ in range(NSUB):
                fl_b = ropet[:, s_, 160:161].unsqueeze(1).broadcast_to([128, H, 1])
                K.op(K.pool, lambda e: e.tensor_copy(out=vda[:, :, s_, DV:VA], in_=fl_b), [RP], BIGW, partial=(s_ > 0))
                K.op(K.pool, lambda e: e.tensor_copy(out=vmla[:, :, s_, DV:VA], in_=fl_b), [RP], BIGW, partial=True)
            if stopA == "memset":
                return True
            for si, (nm, n) in enumerate(cfg.segs):
                if stopA == "seg_" + nm:
                    return True
                if nm in ("q", "cq") and not own:
                    continue
                npc = seg_np[si]
                for p in range(npc):
                    wt, wB = ring.take()
                    ncols = min(256, n - p * 256)
                    for s_ in range(NSUB):
                        if nm in ("cq", "ckv"):
                            bk, bb = acc[s_]
                            o0 = p * 256
                        else:
                            bk, bb = rr()
                            o0 = 0
                        for kc in range(KC):
                            K.mm(bb, bk[:, o0:o0 + ncols], hT[:, kc, s_ * 128:(s_ + 1) * 128], wt[:, kc * 256: kc * 256 + ncols], kc == 0, kc == KC - 1, reads=[wB, HT])
                        if stopA == "q_mm":
                            return True
                        if nm in ("q", "k"):
                            dst = (qda_tok if nm == "q" else kda_tok)[:, s_, p * 256:(p + 1) * 256].rearrange("p (b d) -> p b d", d=QK)
                            dB = [OWN] if nm == "q" else BIGW
                            qi = stg_i[0] % 2
                            stg_i[0] += 1
                            copy_op(K.act, qkt[qi][:, 0:256], bk[:, 0:256], [bb], [QKT[qi]])
                            srcv = qkt[qi][:, 0:256].rearrange("p (b d) -> p b d", d=QK)
                            copy_op(K.pool, dst[:, :, DA_ROT:QK], srcv[:, :, DA_ROT:QK], [QKT[qi]], dB, partial=True)
                            if stopA == "q_copy":
                                return True
                            rope_apply(qkt[qi], QKT[qi], 4, QK, DA_ROT, dst, dB, s_, 0)
                            if stopA == "q_rope":
                                return True
                        elif nm == "v":
                            scale_copy_op(evac_eng(), vda[:, 2 * p:2 * p + 2, s_, 0:DV], bk[:, 0:256].rearrange("p (h c) -> p h c", c=DV), ropet[:, s_, 160:161], [bb, RP], BIGW)
                        elif nm == "kpe":
                            dst = kpe_tok[:, s_, :].rearrange("p (b d) -> p b d", d=ROPE)
                            qi = stg_i[0] % 2
                            stg_i[0] += 1
                            copy_op(K.act, qkt[qi][:, 0:ROPE], bk[:, 0:ROPE], [bb], [QKT[qi]])
                            rope_apply(qkt[qi], QKT[qi], 1, ROPE, ROPE, dst, BIGW, s_, 32)
                    ring.done(1)
                if nm in ("cq", "ckv"):
                    nn = QR if nm == "cq" else KVR
                    gt_ = qng if nm == "cq" else kvng
                    dstt = cqn if nm == "cq" else ckvn
                    dB = [OWN] if nm == "cq" else BIGW
                    rms_rstd(None, lambda s_: acc[s_][0][:, 0:nn], nn, NSUB, [acc[i][1] for i in range(NSUB)], ss2, SS2, env["junk"], env["JB"])
                    for s_ in range(NSUB):
                        K.op(K.dve, lambda e: e.scalar_tensor_tensor(out=dstt[:, s_, :], in0=acc[s_][0][:, 0:nn], scalar=ss2[:, s_:s_ + 1], in1=gt_[:, 0:nn],
                                                                     op0=ALU.mult, op1=ALU.mult), [acc[s_][1], SS2, CONST], dB, partial=True)
            if stopA == "proj":
                return True
            lat = [(ckvn, KVKC, ckvnT, CKT, BIGW)]
            if own:
                lat.append((cqn, QKC, cqnT, CQT, [OWN]))
            for (srct, nk, dstT, DTB, SB_) in lat:
                for kc in range(nk):
                    bk, bb = rr()
                    for s_ in range(NSUB):
                        K.mm(bb, bk[:, s_ * 128:(s_ + 1) * 128], srct[:, s_, kc * 128:(kc + 1) * 128], identb[:], True, True, reads=SB_ + [CONST])
                    copy_op(evac_eng(), dstT[:, kc, :], bk[:, 0:T], [bb], [DTB], partial=(kc > 0))
            if own:
                for p in range(NUQ_N + NUQ_R):
                    wt, wB = ring.take()
                    for s_ in range(NSUB):
                        bk, bb = rr()
                        for kc in range(QKC):
                            K.mm(bb, bk[:, 0:512], cqnT[:, kc, s_ * 128:(s_ + 1) * 128], wt[:, kc * 512:(kc + 1) * 512], kc == 0, kc == QKC - 1, reads=[wB, CQT])
                        if p < NUQ_N:
                            copy_op(evac_eng(), qnope_tok[:, s_, p * 512:(p + 1) * 512], bk[:, 0:512], [bb], [OWN], partial=True)
                        else:
                            pr = p - NUQ_N
                            nv = min(512, H * ROPE - pr * 512)
                            dst = qpe_tok[:, s_, pr * 512: pr * 512 + nv].rearrange("p (b d) -> p b d", d=ROPE)
                            qi = stg_i[0] % 2
                            stg_i[0] += 1
                            copy_op(K.act, qkt[qi][:, 0:nv], bk[:, 0:nv], [bb], [QKT[qi]])
                            rope_apply(qkt[qi], QKT[qi], nv // ROPE, ROPE, ROPE, dst, [OWN], s_, 32)
                    ring.done(1)
            for p in range(2 * NUKV):
                wt, wB = ring.take()
                for s_ in range(NSUB):
                    bk, bb = rr()
                    for kc in range(KVKC):
                        K.mm(bb, bk[:, 0:512], ckvnT[:, kc, s_ * 128:(s_ + 1) * 128], wt[:, kc * 512:(kc + 1) * 512], kc == 0, kc == KVKC - 1, reads=[wB, CKT])
                    if p < NUKV:
                        copy_op(evac_eng(), knope_tok[:, s_, p * 512:(p + 1) * 512], bk[:, 0:512], [bb], BIGW, partial=True)
                    else:
                        pv = p - NUKV
                        scale_copy_op(evac_eng(), vmla[:, 4 * pv:4 * pv + 4, s_, 0:DV], bk[:, 0:512].rearrange("p (h c) -> p h c", c=DV), ropet[:, s_, 160:161], [bb, RP], BIGW)
                ring.done(1)
            if stopA == "mla":
                return True
            for h in range(H):
                transpose_out(lambda s_: kda_tok[:, s_, h * 128:(h + 1) * 128], 128, KT_da[g][h, :, t0:t0 + T], BIGW, [QKVB[g]])
                transpose_out(lambda s_: knope_tok[:, s_, h * 128:(h + 1) * 128], 128, KT_nope[g][h, :, t0:t0 + T], BIGW, [QKVB[g]])
            transpose_out(lambda s_: kpe_tok[:, s_, :], ROPE, KT_pe[g][:, t0:t0 + T], BIGW, [QKVB[g]])
            K.dma(K.pool, [(V_da[g][:, :, ti * NSUB:(ti + 1) * NSUB, :].rearrange("h p s c -> p h s c"), vda),
                           (V_mla[g][:, :, ti * NSUB:(ti + 1) * NSUB, :].rearrange("h p s c -> p h s c"), vmla)],
                  BIGW, [QKVB[g]], partial=True)
            if own:
                for h in range(H):
                    transpose_out(lambda s_: qda_tok[:, s_, h * 128:(h + 1) * 128], 128, QT_da[g][h, :, t0:t0 + T], [OWN], [QKVB[g]])
                    transpose_out(lambda s_: qnope_tok[:, s_, h * 128:(h + 1) * 128], 128, QT_nope[g][h, :, t0:t0 + T], [OWN], [QKVB[g]])
                for hp in range(H // 2):
                    transpose_out(lambda s_: qpe_tok[:, s_, hp * 128:(hp + 1) * 128], 128, QT_pe[g][hp, :, t0:t0 + T], [OWN], [QKVB[g]])
            return False

        for (g, ti, own) in tiles:
            if do_tile(g, ti, own):
                break
        for nm_, aps in (("x1", x1_d), ):
            for g in range(cfg.NG):
                d = dbg_out(f"{nm_}{g}", (nq, D))
                if d is not None:
                    for ti in range(cfg.own_tiles):
                        K.dma(K.sp, [(xt[:], x1_d[g][ti * T:(ti + 1) * T, :].rearrange("(s p) d -> p s d", p=128))], [x1B[g]], [X])
                        K.dma(K.pool, [(d[ti * T:(ti + 1) * T, :].rearrange("(s p) d -> p s d", p=128), xt[:])], [X], [])
        phase_barrier(K, [X, HT, GT, OWN, CQT, CKT, RP, G1B, SS2, env["SS"], env["JB"]] + env["XS"] + STG + RT + QKT + env["STMP"] + env["UT"] + ring.bufs + bankB)


    with contextlib.ExitStack() as es:
        SMAX = max(cfg.S)
        NKTM = SMAX // 128
        NQT = nq // 512
        KTb = [K.sbuf(f"KTb{i}", [128, SMAX], BF16, es) for i in range(2)]
        Vb = [K.sbuf(f"Vb{i}", [128, NKTM, VA], BF16, es) for i in range(2)]
        NQB = 3
        qbuf = [K.sbuf(f"qbuf{i}", [128, 512], BF16, es) for i in range(NQB)]
        qpebuf = [K.sbuf(f"qpebuf{i}", [64, 512], BF16, es) for i in range(NQB)]
        QBUF = [Buf(K, f"qbuf{i}") for i in range(NQB)]
        kpeb = K.sbuf("kpeb", [64, SMAX], BF16, es)
        KVQ = [Buf(K, f"kvq{i}") for i in range(2)]
        KPE = Buf(K, "kpeb")
        NE = 6
        ebuf = [K.sbuf(f"ebuf{i}", [128, 512], BF16, es) for i in range(NE)]
        EB = [Buf(K, f"ebuf{i}") for i in range(NE)]
        accS = [K.sbuf(f"accS{i}", [128, 8, VA], F32, es) for i in range(2)]
        ACS = [Buf(K, f"accS{i}") for i in range(2)]
        rc = K.sbuf("rc", [128, 8], F32, es)
        rl = K.sbuf("rl", [128, 8], F32, es)
        RC = Buf(K, "rc")
        t1 = K.sbuf("t1", [128, DV], F32, es)
        T1 = Buf(K, "t1")
        o32 = K.sbuf("o32", [128, 4, DV], F32, es)
        O32 = Buf(K, "o32")
        ssq = K.sbuf("ssq", [128, 4], F32, es)
        SSQ = Buf(K, "ssq")
        junkB = K.sbuf("junkB", [128, DV], BF16, es)
        JKB = Buf(K, "junkB")
        obf = [K.sbuf(f"obf{i}", [128, 4, DV], BF16, es) for i in range(2)]
        OBF = [Buf(K, f"obf{i}") for i in range(2)]
        mstg = [K.sbuf(f"mstg{i}", [128, 512], BF16, es) for i in range(2)]
        rsum = K.sbuf("rsum", [1, 512], F32, es)
        RSUM = Buf(K, "rsum")
        rbc = K.sbuf("rbc", [128, 512], F32, es)
        RBC = Buf(K, "rbc")
        MST = [Buf(K, f"mstg{i}") for i in range(2)]
        da_slots = {(0, 0): 0, (0, 1): 1, (0, 2): 2, (1, 0): 3, (1, 1): 4, (1, 2): 5, (0, 3): 6, (1, 3): 7}

        def acc_slot(sl):
            b = 4 + sl // 3
            o = (sl % 3) * VA
            return banks[b][:, o:o + VA], bankB[b]

        units = [(g, fam, h) for g in range(cfg.NG) for fam in ("da", "mla") for h in range(H)]

        def load_unit(u):
            g, fam, h = units[u]
            par = u % 2
            Sg = cfg.S[g]
            nkt = Sg // 128
            if fam == "mla" and h == 0:
                K.dma(K.sp, [(kpeb[0:64, 0:Sg], KT_pe[g])], [QKVB[g]], [KPE])
            if fam == "da":
                pairs = [(KTb[par][:, 0:Sg], KT_da[g][h]), (Vb[par][:, 0:nkt, :], V_da[g][h])]
            else:
                pairs = [(KTb[par][:, 0:Sg], KT_nope[g][h]), (Vb[par][:, 0:nkt, :], V_mla[g][h])]
            K.dma(K.sp, pairs, [QKVB[g]], [KVQ[par]])

        def load_q(n):
            if n >= len(units) * NQT:
                return
            u, qt = n // NQT, n % NQT
            g, fam, h = units[u]
            qi = n % NQB
            q0 = qt * 512
            if fam == "da":
                pairs = [(qbuf[qi][:], QT_da[g][h, :, q0:q0 + 512])]
            else:
                pairs = [(qbuf[qi][:], QT_nope[g][h, :, q0:q0 + 512]),
                         (qpebuf[qi][0:64, :], QT_pe[g][h // 2, (h % 2) * 64:(h % 2) * 64 + 64, q0:q0 + 512])]
            K.dma(K.sp, pairs, [QKVB[g]], [QBUF[qi]])

        steps = []
        for u, (g, fam, h) in enumerate(units):
            for qt in range(NQT):
                for kt in range(cfg.S[g] // 128):
                    steps.append((u, qt, kt))
        pending = []
        epi_cnt = [0]
        mst_i = [0]
        SC_MLA = float((NOPE + ROPE) ** -0.5)
        SC_DA = float(QK ** -0.5)

        def emit_qk(i):
            u, qt, kt = steps[i]
            g, fam, h = units[u]
            par = u % 2
            qi = (u * NQT + qt) % NQB
            if fam == "da":
                b0, b1 = 2 * (i % 2), 2 * (i % 2) + 1
                K.mm(bankB[b0], banks[b0][:, 0:512], KTb[par][0:64, kt * 128:(kt + 1) * 128], qbuf[qi][0:64, :], True, True, reads=[KVQ[par], QBUF[qi]])
                K.mm(bankB[b1], banks[b1][:, 0:512], KTb[par][64:128, kt * 128:(kt + 1) * 128], qbuf[qi][64:128, :], True, True, reads=[KVQ[par], QBUF[qi]])
            else:
                b0 = i % 4
                K.mm(bankB[b0], banks[b0][:, 0:512], KTb[par][:, kt * 128:(kt + 1) * 128], qbuf[qi][:], True, False, reads=[KVQ[par], QBUF[qi]])
                K.mm(bankB[b0], banks[b0][:, 0:512], kpeb[0:64, kt * 128:(kt + 1) * 128], qpebuf[qi][0:64, :], False, True, reads=[KVQ[par], KPE, QBUF[qi]])

        def emit_exp_pv(i):
            u, qt, kt = steps[i]
            g, fam, h = units[u]
            par = u % 2
            nkt = cfg.S[g] // 128
            if fam == "da":
                for j in range(2):
                    b = 2 * (i % 2) + j
                    ei = (2 * i + j) % NE
                    K.op(K.act, lambda e: e.activation(out=ebuf[ei][:], in_=banks[b][:, 0:512], func=AF.Exp, scale=SC_DA), [bankB[b]], [EB[ei]])
                for j in range(2):
                    ei = (2 * i + j) % NE
                    for qs in range(4):
                        sl = da_slots[(j, qs)]
                        ap, AB = acc_slot(sl)
                        K.mm(AB, ap, ebuf[ei][:, qs * 128:(qs + 1) * 128], Vb[par][:, kt, :], kt == 0 and sl in (0, 3, 6), kt == nkt - 1 and sl in (2, 5, 7),
                             reads=[EB[ei], KVQ[par]], mark=(qs == 3 or (kt == nkt - 1 and sl in (2, 5, 7))))
            else:
                b = i % 4
                ei = i % NE
                K.op(K.act, lambda e: e.activation(out=ebuf[ei][:], in_=banks[b][:, 0:512], func=AF.Exp, scale=SC_MLA), [bankB[b]], [EB[ei]])
                K.mm(bankB[4], banks[4][:, 0:512], Vb[par][:, kt, 0:DV], ebuf[ei][:], kt == 0, kt == nkt - 1, reads=[EB[ei], KVQ[par]], mark=False)
                K.mm(bankB[5], banks[5][0:1, 0:512], Vb[par][:, kt, DV:VA], ebuf[ei][:], kt == 0, kt == nkt - 1, reads=[EB[ei], KVQ[par]], mark=True)
                if kt == nkt - 1:
                    bankB[4].w = dict(bankB[5].w)
                    bankB[4].r = {}
            if kt == nkt - 1:
                epilogue(i, u, qt)

        def epilogue(i, u, qt):
            g, fam, h = units[u]
            ep = epi_cnt[0] % 2
            epi_cnt[0] += 1
            if fam == "mla":
                K.op(K.dve, lambda e: e.reciprocal(out=rsum[0:1, :], in_=banks[5][0:1, 0:512]), [bankB[5]], [RSUM])
                K.mm(bankB[7], banks[7][:, 0:512], ones_row[0:1, :], rsum[0:1, :], True, True, reads=[RSUM, CONST])
                K.op(K.act, lambda e: e.copy(out=rbc[:], in_=banks[7][:, 0:512]), [bankB[7]], [RBC])
                mi = mst_i[0] % 2
                mst_i[0] += 1
                K.op(K.dve, lambda e: e.tensor_tensor(out=mstg[mi][:], in0=banks[4][:, 0:512], in1=rbc[:], op=ALU.mult), [bankB[4], RBC], [MST[mi]])
                K.dma(K.pool, [(mixT[g][H + h, :, qt * 512:(qt + 1) * 512], mstg[mi][:])], [MST[mi]], [mixB[g]], partial=True)
                return
            aS, AS = accS[ep], ACS[ep]
            nb = 3
            for b in range(nb):
                ns = 3 if b < 2 else 2
                K.op(K.dve, lambda e: e.tensor_copy(out=aS[:, 3 * b:3 * b + ns, :], in_=banks[4 + b][:, 0:ns * VA].rearrange("p (s c) -> p s c", c=VA)),
                     [bankB[4 + b]], [AS], partial=(b > 0))
            K.op(K.dve, lambda e: e.reciprocal(out=rc[:, 0:8], in_=aS[:, 0:8, DV]), [AS], [RC])
            ob, OB = obf[ep], OBF[ep]
            K.op(K.dve, lambda e: e.tensor_scalar(out=rl[:, 0:8], in0=rc[:, 0:8], scalar1=neglam[:, 0:1], scalar2=None, op0=ALU.mult), [RC, CONST], [RC], partial=True)
            for qs in range(4):
                s0, s1 = da_slots[(0, qs)], da_slots[(1, qs)]
                K.op(K.dve, lambda e: e.tensor_scalar(out=t1[:], in0=aS[:, s1, 0:DV], scalar1=rl[:, s1:s1 + 1], scalar2=None, op0=ALU.mult), [AS, RC], [T1])
                K.op(K.dve, lambda e: e.scalar_tensor_tensor(out=o32[:, qs, :], in0=aS[:, s0, 0:DV], scalar=rc[:, s0:s0 + 1], in1=t1[:], op0=ALU.mult, op1=ALU.add),
                     [AS, RC, T1], [O32], partial=(qs > 0))
            rms_rstd(None, lambda s_: o32[:, s_, :], DV, 4, [O32], ssq, SSQ, junkB, JKB)
            for qs in range(4):
                K.op(K.dve, lambda e: e.scalar_tensor_tensor(out=ob[:, qs, :], in0=o32[:, qs, :], scalar=ssq[:, qs:qs + 1], in1=subg[:], op0=ALU.mult, op1=ALU.mult),
                     [O32, SSQ, CONST], [OB], partial=(qs > 0))
            hh = h

            def fin():
                bk, bb = banks[7], bankB[7]
                for qs in range(4):
                    K.mm(bb, bk[:, qs * 128:(qs + 1) * 128], ob[:, qs, :], identb[:], True, True, reads=[OB, CONST])
                mi = mst_i[0] % 2
                mst_i[0] += 1
                K.op(K.dve, lambda e: e.tensor_copy(out=mstg[mi][:], in_=bk[:, 0:512]), [bb], [MST[mi]])
                K.dma(K.pool, [(mixT[g][hh, :, qt * 512:(qt + 1) * 512], mstg[mi][:])], [MST[mi]], [mixB[g]], partial=True)
            pending.append((i + 4, fin))

        load_unit(0)
        load_q(0)
        load_q(1)
        qk_next = [0]

        def ensure_qk(upto):
            while qk_next[0] <= min(upto, len(steps) - 1):
                emit_qk(qk_next[0])
                qk_next[0] += 1

        for i in range(len(steps)):
            u, qt, kt = steps[i]
            if qt == 0 and kt == 0 and u + 1 < len(units):
                load_unit(u + 1)
            if kt == 0:
                load_q(u * NQT + qt + 2)
            ensure_qk(i + (2 if units[u][1] == "mla" else 1))
            emit_exp_pv(i)
            while pending and pending[0][0] <= i:
                pending.pop(0)[1]()
        while pending:
            pending.pop(0)[1]()
        phase_barrier(K, KVQ + QBUF + [KPE, RC, T1, O32, SSQ, JKB, RSUM, RBC] + EB + ACS + OBF + MST + bankB)

    with contextlib.ExitStack() as es:
        env = make_ffn_env(es, "C")
        xt, X = env["xt"], env["X"]
        ring = Ring(K, "ringC", NSL, cfg.slot_elems, es)
        mT = K.sbuf("mT", [128, 2 * H, T], BF16, es)
        MT = Buf(K, "mT")
        gate23 = K.sbuf("gate23", [128, 2, D], F32, es)
        G23 = Buf(K, "gate23")
        fing = K.sbuf("fing", [128, D], F32, es)
        FING = Buf(K, "fing")
        K.dma(K.sp, [(fing[:], final_norm_d.partition_broadcast(128))], [XIN], [FING])
        ctiles = [(g, ti) for g in range(cfg.NG) for ti in range(cfg.own_tiles)]
        for _ in ctiles:
            for p in range(D // 256):
                ring.plan(wb["w_o"][p], wbB["w_o"], 2 * H * 256)
            plan_ffn(ring, "f2")
        for (g, ti) in ctiles:
            t0 = ti * T
            if ti == 0:
                K.dma(K.sp, [(gate23[:, l - 1, :], gbc[l, g]) for l in (1, 2)], [gbcB], [G23])
            K.dma(K.sp, [(xt[:], x1_d[g][t0:t0 + T, :].rearrange("(s p) d -> p s d", p=128))], [x1B[g]], [X])
            K.dma(K.sp, [(mT[:], mixT[g][:, :, t0:t0 + T].rearrange("h p t -> p h t"))], [mixB[g]], [MT])
            for p in range(D // 256):
                wt, wB = ring.take()
                for s_ in range(NSUB):
                    bk, bb = rr()
                    for kc in range(2 * H):
                        K.mm(bb, bk[:, 0:256], mT[:, kc, s_ * 128:(s_ + 1) * 128], wt[:, kc * 256:(kc + 1) * 256], kc == 0, kc == 2 * H - 1, reads=[wB, MT])
                    i = env["cnt"] % 2
                    env["cnt"] += 1
                    K.op(K.dve, lambda e: e.tensor_tensor(out=env["utmp"][i][:, 0:256], in0=bk[:, 0:256], in1=gate23[:, 0, p * 256:(p + 1) * 256], op=ALU.mult),
                         [bb, G23], [env["UT"][i]])
                    K.op(K.pool, lambda e: e.tensor_tensor(out=xt[:, s_, p * 256:(p + 1) * 256], in0=xt[:, s_, p * 256:(p + 1) * 256], in1=env["utmp"][i][:, 0:256], op=ALU.add),
                         [env["UT"][i], X], [X], partial=True)
                ring.done(1)
            d = dbg_out(f"x2{g}", (nq, D))
            if d is not None:
                K.dma(K.pool, [(d[t0:t0 + T, :].rearrange("(s p) d -> p s d", p=128), xt[:])], [X], [])
            norm_to_hT(env, 2, g)
            ffn(env, ring, gate23[:, 1, :], G23)
            d = dbg_out(f"x3{g}", (nq, D))
            if d is not None:
                K.dma(K.pool, [(d[t0:t0 + T, :].rearrange("(s p) d -> p s d", p=128), xt[:])], [X], [])
            rms_rstd(None, lambda s_: xt[:, s_, :], D, NSUB, [X], env["ss"], env["SS"], env["junk"], env["JB"])
            for s_ in range(NSUB):
                K.op(K.dve, lambda e: e.scalar_tensor_tensor(out=xt[:, s_, :], in0=xt[:, s_, :], scalar=env["ss"][:, s_:s_ + 1], in1=fing[:], op0=ALU.mult, op1=ALU.mult),
                     [X, env["SS"], FING], [X], partial=True)
            K.dma(K.pool, [(y_d[g][t0:t0 + T, :].rearrange("(s p) d -> p s d", p=128), xt[:])], [X], [])
        phase_barrier(K, [X, env["HT"], env["GT"], MT, G23, FING, env["SS"], env["JB"]] + env["XS"] + env["STMP"] + env["UT"] + ring.bufs + bankB)

    return nc, K, locals()


def phase_barrier(K, bufs):
    for E in (K.pe, K.act, K.dve, K.pool, K.sp):
        for b in bufs:
            for ev in list(b.w.values()) + list(b.r.values()):
                E.wait(ev, False)


_CACHE = {}


def kernel(**inputs):
    cfg = Cfg()
    if "nc" not in _CACHE:
        _CACHE["nc"] = build_program(cfg)[0]
    nc = _CACHE["nc"]
    maps = prepare_inputs(cfg, inputs)
    res = run_bass_kernel_spmd(nc, maps, core_ids=list(range(cfg.n_cores)))
    nq = cfg.nq
    half = cfg.n_cores // 2
    y_p = np.zeros((1, cfg.Sp, cfg.D), np.float32)
    y_s = np.zeros((half // 2, cfg.Ss, cfg.D), np.float32)
    for c in range(cfg.n_cores):
        r = res.results[c]["y_p"]
        if c < half:
            y_p[0, c * nq:(c + 1) * nq] = r
        else:
            sq, j = (c - half) // 2, (c - half) % 2
            y_s[sq, j * nq:(j + 1) * nq] = r
    return (y_p, y_s)
```

```python
import contextlib
import math
import numpy as np
import concourse.bass as bass
import concourse.mybir as mybir
from concourse.bass_utils import run_bass_kernel_spmd

F32 = mybir.dt.float32
BF16 = mybir.dt.bfloat16
AF = mybir.ActivationFunctionType
ALU = mybir.AluOpType

EPS = 1e-6
ROPE_THETA = 500000.0
T = 512
NSUB = 4
QK = 64
DV = 128
NOPE = 128
ROPE = 64
DA_ROT = 16
VA = DV + 1


class Cfg:
    def __init__(self, D=2048, FF=5632, H=8, QR=512, KVR=256, n_cores=8, sp=16384, ss=8192):
        self.D, self.FF, self.H, self.QR, self.KVR = D, FF, H, QR, KVR
        self.n_cores = n_cores
        self.Sp, self.Ss = sp, ss
        self.NG = 1
        nq = sp // (n_cores // 2)
        assert 2 * nq == ss and nq % T == 0
        self.nq = nq
        self.S = (sp,)
        self.KC = D // 128
        self.FCN = FF // 128
        self.QKC = QR // 128
        self.KVKC = KVR // 128
        self.ntile = (sp // T,)
        self.own_tiles = nq // T
        assert FF % 256 == 0 and D % 512 == 0 and QR % 128 == 0 and KVR % 128 == 0
        assert self.FCN % 4 == 0
        self.W2Q = 4
        self.W2F = self.FCN // self.W2Q
        self.NMOD = 9 * D
        HW = H * 128
        self.segs = [("q", HW), ("k", HW), ("v", HW), ("cq", QR), ("ckv", KVR), ("kpe", ROPE)]
        self.slot_elems = max(self.KC * 256, self.W2F * 512, self.QKC * 512, self.KVKC * 512, 2 * H * 256)
        assert H % 4 == 0


class Ev:
    __slots__ = ("sem", "sid", "val")

    def __init__(self, sem, sid, val):
        self.sem, self.sid, self.val = sem, sid, val


class Buf:
    def __init__(self, K, name, dma=False):
        self.name = name
        self.w = {}
        self.r = {}


class Eng:
    def __init__(self, K, name, eng, compute=True):
        self.K, self.name, self.e = K, name, eng
        self.seen = {}
        self.sid = K.new_sid()
        if compute:
            self.sem = K.new_sem("e_" + name)
            self.cnt = 0
        self.pend_r = []

    def wait(self, ev, same_ok):
        if self.seen.get(ev.sid, 0) >= ev.val:
            return
        self.e.wait_ge(ev.sem, ev.val)
        self.seen[ev.sid] = ev.val


class Kern:
    def __init__(self, nc):
        self.nc = nc
        self.es = contextlib.ExitStack()
        self._sid = 0
        self.nsem = 0
        self.pe = Eng(self, "pe", nc.tensor)
        self.act = Eng(self, "act", nc.scalar)
        self.dve = Eng(self, "dve", nc.vector)
        self.pool = Eng(self, "pool", nc.gpsimd)
        self.sp = Eng(self, "sp", nc.sync, compute=False)
        self.dsems = {"sp": [[self.new_sem("dsp%d" % i), self.new_sid(), 0] for i in range(24)],
                      "pool": [[self.new_sem("dpl%d" % i), self.new_sid(), 0] for i in range(8)]}
        self.dnext = {"sp": 0, "pool": 0}
        self.dram_bufs = []

    def new_sid(self):
        self._sid += 1
        return self._sid

    def new_sem(self, name):
        self.nsem += 1
        return self.es.enter_context(self.nc.semaphore(name))

    def sbuf(self, name, shape, dt, es=None):
        return (es or self.es).enter_context(self.nc.sbuf_tensor(name, list(shape), dt))

    def psum(self, name, shape, dt, es=None):
        return (es or self.es).enter_context(self.nc.psum_tensor(name, list(shape), dt))

    def _waits(self, E, reads, writes):
        for b in reads:
            for ev in b.w.values():
                E.wait(ev, False)
        for b in writes:
            for ev in b.w.values():
                E.wait(ev, True)
            for ev in b.r.values():
                E.wait(ev, True)

    def op(self, E, fn, reads=(), writes=(), partial=False):
        self._waits(E, reads, writes)
        ins = fn(E.e)
        E.cnt += 1
        ins.then_inc(E.sem, 1)
        ev = Ev(E.sem, E.sid, E.cnt)
        for b in reads:
            b.r[E.sid] = ev
        for b in writes:
            if partial:
                b.w[E.sid] = ev
            else:
                b.w = {E.sid: ev}
            b.r = {}
        return ev

    def mm(self, bank, out, lhsT, rhs, start, stop, reads=(), mark=None, **kw):
        E = self.pe
        for b in reads:
            for ev in b.w.values():
                E.wait(ev, False)
        if start:
            for ev in bank.w.values():
                E.wait(ev, True)
            for ev in bank.r.values():
                E.wait(ev, True)
        ins = E.e.matmul(out, lhsT, rhs, start=start, stop=stop, **kw)
        for b in reads:
            if b not in E.pend_r:
                E.pend_r.append(b)
        if mark is None:
            mark = stop
        if mark:
            E.cnt += 1
            ins.then_inc(E.sem, 1)
            ev = Ev(E.sem, E.sid, E.cnt)
            for b in E.pend_r:
                b.r[E.sid] = ev
            E.pend_r = []
            if stop:
                bank.w = {E.sid: ev}
                bank.r = {}
        return ins

    def dma(self, Q, pairs, reads, writes, sb=None, partial=False, **kw):
        for b in reads:
            for ev in b.w.values():
                Q.wait(ev, False)
        for b in writes:
            for ev in list(b.w.values()) + list(b.r.values()):
                Q.wait(ev, False)
        evs = []
        for (o, i) in pairs:
            pool = self.dsems[Q.name]
            st = pool[self.dnext[Q.name] % len(pool)]
            self.dnext[Q.name] += 1
            if st[2] > 0:
                Q.wait(Ev(st[0], st[1], st[2]), False)
            ins = Q.e.dma_start(out=o, in_=i, **kw)
            st[2] += 16
            ins.then_inc(st[0], 16)
            evs.append(Ev(st[0], st[1], st[2]))
        for b in reads:
            for ev in evs:
                b.r[ev.sid] = ev
        for b in writes:
            if not partial:
                b.w = {}
            for ev in evs:
                b.w[ev.sid] = ev
            b.r = {}
        return evs


class Ring:
    def __init__(self, K, name, ns, elems, es):
        self.K = K
        self.ns = ns
        self.tiles = [K.sbuf(f"{name}{i}", [128, elems], BF16, es) for i in range(ns)]
        self.bufs = [Buf(K, f"{name}{i}", dma=True) for i in range(ns)]
        self.pieces = []
        self.issued = 0
        self.taken = 0
        self.consumed = 0

    def plan(self, src_ap, dbuf, elems):
        self.pieces.append((src_ap, dbuf, elems))

    def _issue(self):
        while self.issued < len(self.pieces) and self.issued - self.ns < self.consumed:
            i = self.issued
            src, dbuf, elems = self.pieces[i]
            s = i % self.ns
            self.K.dma(self.K.sp, [(self.tiles[s][:, 0:elems], src)], [dbuf], [self.bufs[s]])
            self.issued += 1

    def take(self):
        i = self.taken
        assert i < len(self.pieces), "ring plan exhausted"
        self._issue()
        assert i < self.issued, "ring too small for this consumption pattern"
        self.taken += 1
        s = i % self.ns
        return self.tiles[s], self.bufs[s]

    def done(self, n=1):
        self.consumed += n
        assert self.consumed <= self.taken
        self._issue()


def _pieces_kxn(w, ncols_piece):
    Kd, N = w.shape
    kc = Kd // 128
    npc = (N + ncols_piece - 1) // ncols_piece
    out = np.zeros((npc, 128, kc, ncols_piece), np.float32)
    for p in range(npc):
        c0 = p * ncols_piece
        c1 = min(N, c0 + ncols_piece)
        blk = w[:, c0:c1].reshape(kc, 128, c1 - c0).transpose(1, 0, 2)
        out[p, :, :, : c1 - c0] = blk
    return out.reshape(npc, 128, kc * ncols_piece)


def _pieces_w2(w2, cfg):
    FF, D = w2.shape
    ndb = D // 512
    out = np.zeros((ndb, cfg.W2Q, 128, cfg.W2F, 512), np.float32)
    w = w2.reshape(cfg.W2Q, cfg.W2F, 128, ndb, 512)
    out[:] = w.transpose(3, 0, 2, 1, 4)
    return out.reshape(ndb * cfg.W2Q, 128, cfg.W2F * 512)


def _rope_tables(pos):
    pos = pos.astype(np.float32)

    def tab(dim):
        inv = (np.float32(ROPE_THETA) ** (-np.arange(0, dim, 2, dtype=np.float32) / np.float32(dim))).astype(np.float32)
        ang = (pos[:, None] * inv[None, :]).astype(np.float32)
        c, s = np.cos(ang).astype(np.float32), np.sin(ang).astype(np.float32)
        return np.concatenate([c, c], 1), np.concatenate([-s, s], 1)

    c16, s16 = tab(DA_ROT)
    c64, s64 = tab(ROPE)
    return np.concatenate([c16, s16, c64, s64], 1).astype(np.float32)


def prepare_inputs(cfg, inp):
    D, H = cfg.D, cfg.H
    g = lambda k: np.asarray(inp[k], np.float32)
    shared = {}
    for name, tag in (("ffn1", "f1"), ("ffn2", "f2")):
        shared[f"{tag}_w1"] = _pieces_kxn(g(f"{name}_w1")[0], 256)
        shared[f"{tag}_w3"] = _pieces_kxn(g(f"{name}_w3")[0], 256)
        shared[f"{tag}_w2"] = _pieces_w2(g(f"{name}_w2")[0], cfg)
    w_in = g("w_in")[0]
    segs = []
    c0 = 0
    for (nm, n) in cfg.segs:
        segs.append(_pieces_kxn(w_in[:, c0:c0 + n], 256))
        c0 += n
    shared["w_in"] = np.concatenate(segs, 0)
    wuq = g("mla_w_uq")[0].reshape(cfg.QR, H, NOPE + ROPE)
    wuq = np.concatenate([wuq[:, :, :NOPE].reshape(cfg.QR, H * NOPE), wuq[:, :, NOPE:].reshape(cfg.QR, H * ROPE)], 1)
    shared["w_uq"] = np.concatenate([_pieces_kxn(wuq[:, :H * NOPE], 512), _pieces_kxn(wuq[:, H * NOPE:], 512)], 0)
    wukv = g("mla_w_ukv")[0].reshape(cfg.KVR, H, NOPE + DV)
    wukv = np.concatenate([wukv[:, :, :NOPE].reshape(cfg.KVR, H * NOPE), wukv[:, :, NOPE:].reshape(cfg.KVR, H * DV)], 1)
    shared["w_ukv"] = np.concatenate([_pieces_kxn(wukv[:, :H * NOPE], 512), _pieces_kxn(wukv[:, H * NOPE:], 512)], 0)
    shared["w_o"] = _pieces_kxn(g("w_o")[0], 256)
    shared["w_ada"] = _pieces_kxn(g("w_ada")[0], 512)
    shared["b_ada"] = np.ascontiguousarray(np.broadcast_to(g("b_ada")[0][None, :], (2, cfg.NMOD)))
    ncol = np.stack([g("ffn1_norm")[0], g("attn_norm")[0], g("ffn2_norm")[0]], 0)
    shared["ncol"] = np.ascontiguousarray(ncol.reshape(3, cfg.KC, 128).transpose(2, 0, 1))
    shared["final_norm"] = g("final_norm").reshape(1, D)
    shared["q_norm"] = g("mla_q_norm").reshape(1, cfg.QR)
    shared["kv_norm"] = g("mla_kv_norm").reshape(1, cfg.KVR)
    shared["subln"] = g("da_subln").reshape(1, DV)
    shared["lambdas"] = np.concatenate([g("da_lambda_q1")[0], g("da_lambda_k1")[0], g("da_lambda_q2")[0], g("da_lambda_k2")[0]]).reshape(1, 4 * QK)
    shared["ident"] = np.eye(128, dtype=np.float32)
    sel = np.zeros((2, 4, 128), np.float32)
    sel[0, 0] = 1.0
    sel[1, 1] = 1.0
    sel[0, 2] = 0.5
    sel[1, 3] = 0.5
    shared["sel"] = sel

    xp = g("x_prompt")
    xs = g("x_sample")
    cp = g("c_prompt")
    cs = g("c_sample")
    nq = cfg.nq
    Sp, Ss = cfg.Sp, cfg.Ss
    half = cfg.n_cores // 2
    maps = []
    for c in range(cfg.n_cores):
        m = dict(shared)
        xa = np.zeros((Sp, D), np.float32)
        pos = np.zeros((Sp,), np.int64)
        flag = np.zeros((Sp, 1), np.float32)
        if c < half:
            seq, cvec, j, L = xp[0], cp[0], c, Sp
        else:
            sq = (c - half) // 2
            seq, cvec, j, L = xs[sq], cs[sq], (c - half) % 2, Ss
        order = np.concatenate([np.arange(j * nq, (j + 1) * nq), np.arange(0, j * nq), np.arange((j + 1) * nq, L)])
        xa[:L] = seq[order]
        pos[:L] = order
        flag[:L] = 1.0
        m["x_p"] = xa
        m["rope_p"] = np.concatenate([_rope_tables(pos), flag], 1)
        cc = np.stack([cvec, cvec], 0)
        m["cT"] = np.ascontiguousarray(cc.reshape(2, cfg.KC, 128).transpose(2, 1, 0))
        maps.append(m)
    return maps


def cdiv(a, b):
    return (a + b - 1) // b


def build_program(cfg, debug=()):
    nc = bass.Bass("TRN2", target_bir_lowering=False)
    K = Kern(nc)
    D, FF, H, QR, KVR, KC, FCN = cfg.D, cfg.FF, cfg.H, cfg.QR, cfg.KVR, cfg.KC, cfg.FCN
    HW = H * 128
    nq = cfg.nq
    NDB = D // 512
    NMOD = cfg.NMOD
    QKC, KVKC = cfg.QKC, cfg.KVKC

    def din(name, shape, dt=F32):
        return nc.dram_tensor(name, list(shape), dt, kind="ExternalInput").ap()

    def dscr(name, shape, dt):
        if name in debug:
            return nc.dram_tensor(name, list(shape), dt, kind="ExternalOutput").ap()
        return nc.dram_tensor(name, list(shape), dt).ap()

    NP13 = FF // 256
    NPW2 = NDB * cfg.W2Q
    seg_np = [cdiv(n, 256) for (_, n) in cfg.segs]
    seg_base = [sum(seg_np[:i]) for i in range(len(seg_np))]
    NPIN = sum(seg_np)
    NUQ_N, NUQ_R = HW // 512, cdiv(H * ROPE, 512)
    NUKV = HW // 512
    wshapes = {
        "f1_w1": (NP13, KC * 256), "f1_w3": (NP13, KC * 256), "f1_w2": (NPW2, cfg.W2F * 512),
        "w_in": (NPIN, KC * 256), "w_uq": (NUQ_N + NUQ_R, QKC * 512), "w_ukv": (2 * NUKV, KVKC * 512),
        "w_o": (D // 256, 2 * H * 256),
        "f2_w1": (NP13, KC * 256), "f2_w3": (NP13, KC * 256), "f2_w2": (NPW2, cfg.W2F * 512),
    }
    w32 = {k: din(k, (v[0], 128, v[1])) for k, v in wshapes.items()}
    wb = {k: dscr("wb_" + k, (v[0], 128, v[1]), BF16) for k, v in wshapes.items()}
    wbB = {k: Buf(K, "wb_" + k) for k in wshapes}
    w_ada = din("w_ada", (NMOD // 512, 128, KC * 512))
    b_ada = din("b_ada", (2, NMOD))
    ncol_d = din("ncol", (128, 3, KC))
    final_norm_d = din("final_norm", (1, D))
    q_norm_d = din("q_norm", (1, QR))
    kv_norm_d = din("kv_norm", (1, KVR))
    subln_d = din("subln", (1, DV))
    lambdas_d = din("lambdas", (1, 4 * QK))
    ident_d = din("ident", (128, 128))
    sel_d = din("sel", (2, 4, 128))
    x_d = [din("x_p", (cfg.S[0], D))]
    rope_d = [din("rope_p", (cfg.S[0], 161))]
    cT_d = din("cT", (128, KC, 2))
    y_d = [nc.dram_tensor(n, [nq, D], F32, kind="ExternalOutput").ap() for n in ("y_p",)]
    dbg = {}

    def dbg_out(name, shape, dt=F32):
        if name in dbg:
            return dbg[name]
        if name in debug:
            dbg[name] = nc.dram_tensor("dbg_" + name, list(shape), dt, kind="ExternalOutput").ap()
            return dbg[name]
        return None

    gbc = dscr("gbc", (3, 2, 128, D), F32)
    gbcB = Buf(K, "gbc")
    x1_d = [dscr(f"x1_{g}", (nq, D), F32) for g in range(cfg.NG)]
    x1B = [Buf(K, f"x1_{g}") for g in range(cfg.NG)]
    QT_da = [dscr(f"QT_da{g}", (H, 128, nq), BF16) for g in range(cfg.NG)]
    QT_nope = [dscr(f"QT_nope{g}", (H, 128, nq), BF16) for g in range(cfg.NG)]
    QT_pe = [dscr(f"QT_pe{g}", (H // 2, 128, nq), BF16) for g in range(cfg.NG)]
    KT_da = [dscr(f"KT_da{g}", (H, 128, cfg.S[g]), BF16) for g in range(cfg.NG)]
    KT_nope = [dscr(f"KT_nope{g}", (H, 128, cfg.S[g]), BF16) for g in range(cfg.NG)]
    KT_pe = [dscr(f"KT_pe{g}", (64, cfg.S[g]), BF16) for g in range(cfg.NG)]
    V_da = [dscr(f"V_da{g}", (H, 128, cfg.S[g] // 128, VA), BF16) for g in range(cfg.NG)]
    V_mla = [dscr(f"V_mla{g}", (H, 128, cfg.S[g] // 128, VA), BF16) for g in range(cfg.NG)]
    mixT = [dscr(f"mixT{g}", (2 * H, 128, nq), BF16) for g in range(cfg.NG)]
    QKVB = [Buf(K, f"qkv{g}") for g in range(cfg.NG)]
    mixB = [Buf(K, f"mix{g}") for g in range(cfg.NG)]
    XIN = Buf(K, "xin")

    ident32 = K.sbuf("ident32", [128, 128], F32)
    identb = K.sbuf("identb", [128, 128], BF16)
    modcol = K.sbuf("modcol", [128, 3, 2, KC, 2], F32)
    epsT = K.sbuf("epsT", [128, 1], F32)
    neglam = K.sbuf("neglam", [128, 1], F32)
    subg = K.sbuf("subg", [128, DV], F32)
    qng = K.sbuf("qng", [128, QR], F32)
    kvng = K.sbuf("kvng", [128, KVR], F32)
    ones_row = K.sbuf("ones_row", [1, 128], F32)
    CONST = Buf(K, "const", dma=True)
    MODC = Buf(K, "modcol")
    banks = [K.psum(f"bank{i}", [128, 512], F32) for i in range(8)]
    bankB = [Buf(K, f"bank{i}") for i in range(8)]
    rr_state = [0]

    def rr():
        i = rr_state[0] % 4
        rr_state[0] += 1
        return banks[i], bankB[i]

    acc = [(banks[4 + i], bankB[4 + i]) for i in range(4)]
    ev_state = [0]

    def evac_eng():
        ev_state[0] += 1
        return K.act if ev_state[0] % 2 else K.dve

    def copy_op(E, out, in_, reads, writes, partial=False):
        if E is K.act:
            return K.op(E, lambda e: e.copy(out=out, in_=in_), reads, writes, partial)
        return K.op(E, lambda e: e.tensor_copy(out=out, in_=in_), reads, writes, partial)

    def scale_copy_op(E, out, in_, sc, reads, writes):
        if E is K.act:
            return K.op(E, lambda e: e.activation(out=out, in_=in_, func=AF.Identity, scale=sc), reads, writes, True)
        return K.op(E, lambda e: e.tensor_scalar(out=out, in0=in_, scalar1=sc, scalar2=None, op0=ALU.mult), reads, writes, True)

    K.dma(K.sp, [(ident32[:], ident_d), (qng[:], q_norm_d.partition_broadcast(128)),
                 (kvng[:], kv_norm_d.partition_broadcast(128)), (subg[:], subln_d.partition_broadcast(128))],
          [XIN], [CONST], CONST)
    K.op(K.dve, lambda e: e.tensor_copy(out=identb[:], in_=ident32[:]), [CONST], [CONST], partial=True)
    K.op(K.dve, lambda e: e.memset(epsT[:], EPS), [], [CONST], partial=True)
    K.op(K.dve, lambda e: e.memset(ones_row[:], 1.0), [], [CONST], partial=True)
    K.op(K.dve, lambda e: e.tensor_scalar(out=subg[:], in0=subg[:], scalar1=0.8, scalar2=None, op0=ALU.mult), [CONST], [CONST], partial=True)

    with contextlib.ExitStack() as es:
        cTs = K.sbuf("cTs", [128, KC, 2], F32, es)
        siluT = K.sbuf("siluT", [128, KC, 2], F32, es)
        b2 = [K.sbuf(f"b2_{i}", [2, 512], F32, es) for i in range(2)]
        mrow = K.sbuf("mrow", [2, NMOD], F32, es)
        wblk = [K.sbuf(f"wblk{i}", [128, KC * 512], F32, es) for i in range(2)]
        wblkB = [Buf(K, f"wblk{i}", dma=True) for i in range(2)]
        ncols = K.sbuf("ncols", [128, 3, KC], F32, es)
        sels = K.sbuf("sels", [2, 4, 128], F32, es)
        mcol = K.sbuf("mcol", [128, 6, KC, 2], F32, es)
        gst = [K.sbuf(f"gst{i}", [128, D], F32, es) for i in range(2)]
        gstB = [Buf(K, f"gst{i}", dma=True) for i in range(2)]
        lambf = K.sbuf("lamb", [128, 4 * QK], F32, es)
        lamb = lambf[:].rearrange("p (a b) -> p a b", b=QK)
        prod = K.sbuf("prod", [128, 2, QK], F32, es)
        s12 = K.sbuf("s12", [128, 2], F32, es)
        MB = Buf(K, "mphase", dma=True)
        MROW = Buf(K, "mrow")
        K.dma(K.sp, [(cTs[:], cT_d), (ncols[:], ncol_d), (sels[:], sel_d),
                     (lambf[:], lambdas_d.partition_broadcast(128))], [XIN], [MB], MB)
        K.op(K.act, lambda e: e.activation(out=siluT[:], in_=cTs[:], func=AF.Silu), [MB], [MB], partial=True)
        NB = NMOD // 512
        for nb in range(NB):
            i = nb % 2
            K.dma(K.sp, [(wblk[i][:], w_ada[nb]), (b2[i][:], b_ada[:, nb * 512:(nb + 1) * 512])], [XIN], [wblkB[i]], wblkB[i])
            bk, bb = rr()
            for kc in range(KC):
                K.mm(bb, bk[0:2, 0:512], siluT[:, kc, :], wblk[i][:, kc * 512:(kc + 1) * 512], kc == 0, kc == KC - 1, reads=[wblkB[i], MB])
            K.op(K.dve, lambda e: e.tensor_tensor(out=mrow[:, nb * 512:(nb + 1) * 512], in0=bk[0:2, 0:512], in1=b2[i][:], op=ALU.add),
                 [bb, wblkB[i]], [MROW], partial=True)
        vec_idx = [0, 1, 3, 4, 6, 7]
        bk, bb = rr()
        for vi, v in enumerate(vec_idx):
            for kc in range(KC):
                o = (vi * KC + kc) * 2
                K.mm(bb, bk[:, o:o + 2], mrow[0:2, v * D + kc * 128: v * D + (kc + 1) * 128], ident32[0:2, 0:2], True, True, reads=[MROW, CONST])
        K.op(K.dve, lambda e: e.tensor_copy(out=mcol[:].rearrange("p a k g -> p (a k g)"), in_=bk[:, 0:6 * KC * 2]), [bb], [MB], partial=True)
        for l in range(3):
            K.op(K.dve, lambda e: e.scalar_tensor_tensor(out=modcol[:, l, 0, :, :], in0=mcol[:, 2 * l + 1, :, :], scalar=1.0,
                                                         in1=ncols[:, l, :].unsqueeze(2).broadcast_to([128, KC, 2]), op0=ALU.add, op1=ALU.mult),
                 [MB], [MODC], partial=True)
            K.op(K.dve, lambda e: e.tensor_copy(out=modcol[:, l, 1, :, :], in_=mcol[:, 2 * l, :, :]), [MB], [MODC], partial=True)
        gi = 0
        for l in range(3):
            v = 3 * l + 2
            for g in range(2):
                si = g + (0 if l == 1 else 2)
                st, sB = gst[gi % 2], gstB[gi % 2]
                gi += 1
                for db in range(NDB):
                    bk, bb = rr()
                    K.mm(bb, bk[:, 0:512], sels[0:2, si, :], mrow[0:2, v * D + db * 512: v * D + (db + 1) * 512], True, True, reads=[MROW, MB])
                    copy_op(evac_eng(), st[:, db * 512:(db + 1) * 512], bk[:, 0:512], [bb], [sB], partial=True)
                K.dma(K.pool, [(gbc[l, g], st[:])], [sB], [gbcB], sB, partial=True)
        K.op(K.dve, lambda e: e.tensor_tensor(out=prod[:, 0, :], in0=lamb[:, 0, :], in1=lamb[:, 1, :], op=ALU.mult), [MB], [MB], partial=True)
        K.op(K.dve, lambda e: e.tensor_tensor(out=prod[:, 1, :], in0=lamb[:, 2, :], in1=lamb[:, 3, :], op=ALU.mult), [MB], [MB], partial=True)
        K.op(K.dve, lambda e: e.reduce_sum(out=s12[:], in_=prod[:], axis=mybir.AxisListType.X), [MB], [MB], partial=True)
        K.op(K.act, lambda e: e.activation(out=s12[:], in_=s12[:], func=AF.Exp), [MB], [MB], partial=True)
        K.op(K.dve, lambda e: e.tensor_tensor(out=neglam[:], in0=s12[:, 1:2], in1=s12[:, 0:1], op=ALU.subtract), [MB], [CONST], partial=True)
        K.op(K.dve, lambda e: e.tensor_scalar(out=neglam[:], in0=neglam[:], scalar1=-0.2, scalar2=None, op0=ALU.add), [CONST], [CONST], partial=True)
        d = dbg_out("modcol", (128, 3 * 2 * KC * 2))
        if d is not None:
            K.dma(K.pool, [(d, modcol[:].rearrange("p l a k g -> p (l a k g)"))], [MODC], [], MB)
        d = dbg_out("neglam", (128, 1))
        if d is not None:
            K.dma(K.pool, [(d, neglam[:])], [CONST], [], MB)
        phase_barrier(K, [MB, MROW, MODC, CONST] + wblkB + gstB)

    with contextlib.ExitStack() as es:
        EMAX = max(v[1] for v in wshapes.values())
        NST = 3
        st32 = [K.sbuf(f"st32_{i}", [128, EMAX], F32, es) for i in range(NST)]
        st16 = [K.sbuf(f"st16_{i}", [128, EMAX], BF16, es) for i in range(NST)]
        s32B = [Buf(K, f"st32_{i}", dma=True) for i in range(NST)]
        s16B = [Buf(K, f"st16_{i}", dma=True) for i in range(NST)]
        ci = 0
        engs = [K.dve, K.act]
        for name, (npc, E) in wshapes.items():
            for p in range(npc):
                i = ci % NST
                ci += 1
                K.dma(K.sp, [(st32[i][:, 0:E], w32[name][p])], [XIN], [s32B[i]], s32B[i])
                copy_op(engs[ci % 2], st16[i][:, 0:E], st32[i][:, 0:E], [s32B[i]], [s16B[i]])
                K.dma(K.pool, [(wb[name][p], st16[i][:, 0:E])], [s16B[i]], [wbB[name]], s16B[i], partial=True)
        phase_barrier(K, s32B + s16B)


    NSL = 4
    HWp = HW // 256
    BIGN = max(FCN * 512, 4 * HW + 2 * 4 * H * VA + 4 * KVR + 4 * ROPE + 4 * HW + D)

    def rms_rstd(ES, src_fn, n, nsub, reads, ssb, ssB, junk, JB):
        K.op(K.dve, lambda e: e.memset(ssb[:, 0:nsub], 0.0), [], [ssB])
        for s_ in range(nsub):
            K.op(K.act, lambda e: e.activation(out=junk[:, 0:n], in_=src_fn(s_), func=AF.Square, accum_out=ssb[:, s_:s_ + 1]),
                 reads + [ssB], [JB, ssB], partial=True)
        K.op(K.act, lambda e: e.activation(out=ssb[:, 0:nsub], in_=ssb[:, 0:nsub], func=AF.Sqrt, scale=1.0 / n, bias=epsT[:, 0:1]), [ssB, CONST], [ssB])
        K.op(K.dve, lambda e: e.reciprocal(out=ssb[:, 0:nsub], in_=ssb[:, 0:nsub]), [ssB], [ssB])

    def make_ffn_env(es, tag):
        env = {}
        env["xt"] = K.sbuf("xt" + tag, [128, NSUB, D], F32, es)
        env["X"] = Buf(K, "X" + tag)
        env["hT"] = K.sbuf("hT" + tag, [128, 2 * H if False else KC, T], BF16, es)
        env["HT"] = Buf(K, "HT" + tag)
        env["big"] = K.sbuf("big" + tag, [128, BIGN], BF16, es)
        env["GT"] = Buf(K, "GT" + tag)
        env["stmp"] = [K.sbuf(f"stmp{tag}{i}", [128, T], F32, es) for i in range(2)]
        env["STMP"] = [Buf(K, f"stmp{tag}{i}") for i in range(2)]
        env["ss"] = K.sbuf("ss" + tag, [128, NSUB], F32, es)
        env["SS"] = Buf(K, "ss" + tag)
        env["xs"] = [K.sbuf(f"xs{tag}{i}", [128, D], BF16, es) for i in range(2)]
        env["XS"] = [Buf(K, f"xs{tag}{i}") for i in range(2)]
        env["junk"] = env["xs"][0]
        env["JB"] = env["XS"][0]
        env["utmp"] = env["stmp"]
        env["UT"] = env["STMP"]
        env["cnt"] = 0
        return env

    def norm_to_hT(env, l, g):
        xt, X = env["xt"], env["X"]
        rms_rstd(None, lambda s_: xt[:, s_, :], D, NSUB, [X], env["ss"], env["SS"], env["junk"], env["JB"])
        first = True
        base = env["cnt"]
        env["cnt"] += NSUB

        def prescale(s_):
            i = (base + s_) % 2
            K.op(K.act, lambda e: e.activation(out=env["xs"][i][:], in_=xt[:, s_, :], func=AF.Identity, scale=env["ss"][:, s_:s_ + 1]), [X, env["SS"]], [env["XS"][i]])

        prescale(0)
        for s_ in range(NSUB):
            i = (base + s_) % 2
            xs, XS = env["xs"][i], env["XS"][i]
            bks = []
            for kg in range(KC // 4):
                bk, bb = rr()
                bks.append((bk, bb))
                for j in range(4):
                    kc = kg * 4 + j
                    K.mm(bb, bk[:, j * 128:(j + 1) * 128], xs[:, kc * 128:(kc + 1) * 128], identb[:], True, True, reads=[XS, CONST])
                if kg == 0 and s_ + 1 < NSUB:
                    prescale(s_ + 1)
            for kg, (bk, bb) in enumerate(bks):
                E = evac_eng()
                for j in range(4):
                    kc = kg * 4 + j
                    a_ap = modcol[:, l, 0, kc, g:g + 1]
                    b_ap = modcol[:, l, 1, kc, g:g + 1]
                    o_ap = env["hT"][:, kc, s_ * 128:(s_ + 1) * 128]
                    i_ap = bk[:, j * 128:(j + 1) * 128]
                    if E is K.act:
                        K.op(E, lambda e: e.activation(out=o_ap, in_=i_ap, func=AF.Identity, scale=a_ap, bias=b_ap), [bb, MODC], [env["HT"]], partial=not first)
                    else:
                        K.op(E, lambda e: e.tensor_scalar(out=o_ap, in0=i_ap, scalar1=a_ap, scalar2=b_ap, op0=ALU.mult, op1=ALU.add), [bb, MODC], [env["HT"]], partial=not first)
                    first = False

    def plan_ffn(ring, tag):
        for fp in range(NP13):
            ring.plan(wb[tag + "_w1"][fp], wbB[tag + "_w1"], KC * 256)
            ring.plan(wb[tag + "_w3"][fp], wbB[tag + "_w3"], KC * 256)
        for p in range(NPW2):
            ring.plan(wb[tag + "_w2"][p], wbB[tag + "_w2"], cfg.W2F * 512)

    def ffn(env, ring, gate_ap, GATEB):
        hT, HT, big, GT = env["hT"], env["HT"], env["big"], env["GT"]
        for fp in range(NP13):
            w1t, w1b = ring.take()
            w3t, w3b = ring.take()
            for fl in range(2):
                fc = fp * 2 + fl
                b1, B1 = rr()
                b3, B3 = rr()
                for kc in range(KC):
                    K.mm(B1, b1[:, 0:T], w1t[:, kc * 256 + fl * 128: kc * 256 + (fl + 1) * 128], hT[:, kc, :], kc == 0, kc == KC - 1, reads=[w1b, HT])
                for kc in range(KC):
                    K.mm(B3, b3[:, 0:T], w3t[:, kc * 256 + fl * 128: kc * 256 + (fl + 1) * 128], hT[:, kc, :], kc == 0, kc == KC - 1, reads=[w3b, HT])
                i = env["cnt"] % 2
                env["cnt"] += 1
                K.op(K.act, lambda e: e.activation(out=env["stmp"][i][:], in_=b1[:, 0:T], func=AF.Silu), [B1], [env["STMP"][i]])
                K.op(K.dve, lambda e: e.tensor_tensor(out=big[:, fc * T:(fc + 1) * T], in0=env["stmp"][i][:], in1=b3[:, 0:T], op=ALU.mult),
                     [env["STMP"][i], B3], [GT], partial=(fc > 0))
            ring.done(2)
        W2F = cfg.W2F
        for db in range(NDB):
            for q in range(cfg.W2Q):
                wt, wB = ring.take()
                for s_ in range(NSUB):
                    ab, AB = acc[s_]
                    for fl in range(W2F):
                        fc = q * W2F + fl
                        first = (q == 0 and fl == 0)
                        last = (q == cfg.W2Q - 1 and fl == W2F - 1)
                        K.mm(AB, ab[:, 0:512], big[:, fc * T + s_ * 128: fc * T + (s_ + 1) * 128], wt[:, fl * 512:(fl + 1) * 512], first, last,
                             reads=[wB, GT], mark=(last or (s_ == NSUB - 1 and fl == W2F - 1)))
                ring.done(1)
            for s_ in range(NSUB):
                ab, AB = acc[s_]
                i = env["cnt"] % 2
                env["cnt"] += 1
                K.op(K.dve, lambda e: e.tensor_tensor(out=env["utmp"][i][:], in0=ab[:, 0:512], in1=gate_ap[:, db * 512:(db + 1) * 512], op=ALU.mult),
                     [AB, GATEB], [env["UT"][i]])
                K.op(K.dve, lambda e: e.tensor_tensor(out=env["xt"][:, s_, db * 512:(db + 1) * 512], in0=env["xt"][:, s_, db * 512:(db + 1) * 512],
                                                        in1=env["utmp"][i][:], op=ALU.add),
                     [env["UT"][i], env["X"]], [env["X"]], partial=True)

    with contextlib.ExitStack() as es:
        env = make_ffn_env(es, "A")
        xt, X, hT, HT, big, GT = env["xt"], env["X"], env["hT"], env["HT"], env["big"], env["GT"]
        ring = Ring(K, "ringA", NSL, cfg.slot_elems, es)
        gate1 = K.sbuf("gate1", [128, D], F32, es)
        G1B = Buf(K, "gate1")
        ropet = K.sbuf("ropet", [128, NSUB, 161], F32, es)
        RP = Buf(K, "ropet")
        own_st = K.sbuf("own_st", [128, 4 * HW + 4 * QR + 4 * HW + 4 * H * ROPE], BF16, es)
        OWN = Buf(K, "own_st")
        cqnT = K.sbuf("cqnT", [128, QKC, T], BF16, es)
        CQT = Buf(K, "cqnT")
        ckvnT = K.sbuf("ckvnT", [128, KVKC, T], BF16, es)
        CKT = Buf(K, "ckvnT")
        NSTG = 4
        stg = [K.sbuf(f"stg{i}", [128, T], BF16, es) for i in range(NSTG)]
        STG = [Buf(K, f"stg{i}") for i in range(NSTG)]
        stg_i = [0]
        rtmp = [K.sbuf(f"rtmp{i}", [128, 512], F32, es) for i in range(2)]
        RT = [Buf(K, f"rtmp{i}") for i in range(2)]
        ss2 = K.sbuf("ss2", [128, NSUB], F32, es)
        SS2 = Buf(K, "ss2")
        qkt = [K.sbuf(f"qkt{i}", [128, 512], F32, es) for i in range(2)]
        QKT = [Buf(K, f"qkt{i}") for i in range(2)]
        o_ = [0]

        def carve(buf, n):
            a = buf[:, o_[0]:o_[0] + n]
            o_[0] += n
            return a
        kda_tok = carve(big, 4 * HW).rearrange("p (s n) -> p s n", s=NSUB)
        vda = carve(big, 4 * H * VA).rearrange("p (h s c) -> p h s c", s=NSUB, h=H)
        vmla = carve(big, 4 * H * VA).rearrange("p (h s c) -> p h s c", s=NSUB, h=H)
        ckvn = carve(big, 4 * KVR).rearrange("p (s n) -> p s n", s=NSUB)
        kpe_tok = carve(big, 4 * ROPE).rearrange("p (s n) -> p s n", s=NSUB)
        knope_tok = carve(big, 4 * HW).rearrange("p (s n) -> p s n", s=NSUB)
        o_[0] = 0
        qda_tok = carve(own_st, 4 * HW).rearrange("p (s n) -> p s n", s=NSUB)
        cqn = carve(own_st, 4 * QR).rearrange("p (s n) -> p s n", s=NSUB)
        qnope_tok = carve(own_st, 4 * HW).rearrange("p (s n) -> p s n", s=NSUB)
        qpe_tok = carve(own_st, 4 * H * ROPE).rearrange("p (s n) -> p s n", s=NSUB)

        tiles = []
        for g in range(cfg.NG):
            for ti in range(cfg.ntile[g]):
                tiles.append((g, ti, ti < cfg.own_tiles))
        for (g, ti, own) in tiles:
            plan_ffn(ring, "f1")
            for si, (nm, n) in enumerate(cfg.segs):
                if nm in ("q", "cq") and not own:
                    continue
                for p in range(seg_np[si]):
                    ring.plan(wb["w_in"][seg_base[si] + p], wbB["w_in"], KC * 256)
            if own:
                for p in range(NUQ_N + NUQ_R):
                    ring.plan(wb["w_uq"][p], wbB["w_uq"], QKC * 512)
            for p in range(2 * NUKV):
                ring.plan(wb["w_ukv"][p], wbB["w_ukv"], KVKC * 512)

        def rope_apply(bk, BB, ncol_blocks, blk, rot, dst, dstB, s_, tab_off):
            half = rot // 2
            src = bk[:, 0:ncol_blocks * blk].rearrange("p (b d) -> p b d", d=blk)
            Ct = ropet[:, s_, tab_off:tab_off + rot].unsqueeze(1).broadcast_to([128, ncol_blocks, rot])
            S1 = ropet[:, s_, tab_off + rot:tab_off + rot + half].unsqueeze(1).broadcast_to([128, ncol_blocks, half])
            S2 = ropet[:, s_, tab_off + rot + half:tab_off + 2 * rot].unsqueeze(1).broadcast_to([128, ncol_blocks, half])
            A = rtmp[0][:, 0:ncol_blocks * rot].rearrange("p (b d) -> p b d", d=rot)
            Bt = rtmp[1][:, 0:ncol_blocks * rot].rearrange("p (b d) -> p b d", d=rot)
            nops = int(([d.split(":")[1] for d in debug if d.startswith("ropeops:")] or ["4"])[0])
            if nops >= 1:
                K.op(K.dve, lambda e: e.tensor_tensor(out=A, in0=src[:, :, 0:rot], in1=Ct, op=ALU.mult), [BB, RP], [RT[0]])
            if nops >= 2:
                K.op(K.dve, lambda e: e.tensor_tensor(out=Bt[:, :, 0:half], in0=src[:, :, half:rot], in1=S1, op=ALU.mult), [BB, RP], [RT[1]])
            if nops >= 3:
                K.op(K.dve, lambda e: e.tensor_tensor(out=Bt[:, :, half:rot], in0=src[:, :, 0:half], in1=S2, op=ALU.mult), [BB, RP], [RT[1]], partial=True)
            if nops >= 4:
                K.op(K.dve, lambda e: e.tensor_tensor(out=dst[:, :, 0:rot], in0=A, in1=Bt, op=ALU.add), [RT[0], RT[1]], dstB, partial=True)

        def transpose_out(src_fn, nrows, dram_ap, reads, DST):
            bk, bb = rr()
            for s_ in range(NSUB):
                K.mm(bb, bk[0:nrows, s_ * 128:(s_ + 1) * 128], src_fn(s_), identb[:], True, True, reads=reads + [CONST])
            i = stg_i[0] % NSTG
            stg_i[0] += 1
            copy_op(evac_eng(), stg[i][0:nrows, :], bk[0:nrows, 0:T], [bb], [STG[i]])
            K.dma(K.sp, [(dram_ap, stg[i][0:nrows, :])], [STG[i]], DST, partial=True)

        stopA = [d.split(":")[1] for d in debug if d.startswith("stopA:")]
        stopA = stopA[0] if stopA else None

        def load_x(g, ti):
            K.dma(K.sp, [(xt[:], x_d[g][ti * T:(ti + 1) * T, :].rearrange("(s p) d -> p s d", p=128))], [XIN], [X])

        def do_tile(g, ti, own):
            t0 = ti * T
            if ti == 0:
                K.dma(K.sp, [(gate1[:], gbc[0, g])], [gbcB], [G1B])
            if (g, ti) == (0, 0):
                load_x(g, ti)
            K.dma(K.sp, [(ropet[:], rope_d[g][t0:t0 + T, :].rearrange("(s p) c -> p s c", p=128))], [XIN], [RP])
            if stopA == "load":
                return True
            norm_to_hT(env, 0, g)
            if stopA == "norm":
                return True
            if g == 0 and ti == 0:
                d = dbg_out("hT0", (128, KC * T), BF16)
                if d is not None:
                    K.dma(K.pool, [(d, hT[:].rearrange("p k t -> p (k t)"))], [HT], [])
            ffn(env, ring, gate1[:], G1B)
            if g == 0 and ti == 0:
                d = dbg_out("gt0", (128, FCN * T), BF16)
                if d is not None:
                    K.dma(K.pool, [(d, big[:, 0:FCN * T])], [GT], [])
                d = dbg_out("gate", (128, 2 * D))
                if d is not None:
                    K.dma(K.pool, [(d[:, 0:D], gate1[:])], [G1B], [])
            if stopA == "ffn":
                return True
            if own:
                K.dma(K.pool, [(x1_d[g][t0:t0 + T, :].rearrange("(s p) d -> p s d", p=128), xt[:])], [X], [x1B[g]], partial=True)
            if stopA == "x1s":
                return True
            norm_to_hT(env, 1, g)
            nxt = tiles.index((g, ti, own)) + 1
            if nxt < len(tiles) and stopA is None:
                load_x(tiles[nxt][0], tiles[nxt][1])
            if stopA == "norm2":
                return True
            BIGW = [GT]
            for s_ in range(NSUB):
                fl_b = ropet[:, s_, 160:161].unsqueeze(1).broadcast_to([128, H, 1])
                K.op(K.pool, lambda e: e.tensor_copy(out=vda[:, :, s_, DV:VA], in_=fl_b), [RP], BIGW, partial=(s_ > 0))
                K.op(K.pool, lambda e: e.tensor_copy(out=vmla[:, :, s_, DV:VA], in_=fl_b), [RP], BIGW, partial=True)
            if stopA == "memset":
                return True
            for si, (nm, n) in enumerate(cfg.segs):
                if stopA == "seg_" + nm:
                    return True
                if nm in ("q", "cq") and not own:
                    continue
                npc = seg_np[si]
                for p in range(npc):
                    wt, wB = ring.take()
                    ncols = min(256, n - p * 256)
                    for s_ in range(NSUB):
                        if nm in ("cq", "ckv"):
                            bk, bb = acc[s_]
                            o0 = p * 256
                        else:
                            bk, bb = rr()
                            o0 = 0
                        for kc in range(KC):
                            K.mm(bb, bk[:, o0:o0 + ncols], hT[:, kc, s_ * 128:(s_ + 1) * 128], wt[:, kc * 256: kc * 256 + ncols], kc == 0, kc == KC - 1, reads=[wB, HT])
                        if stopA == "q_mm":
                            return True
                        if nm in ("q", "k"):
                            dst = (qda_tok if nm == "q" else kda_tok)[:, s_, p * 256:(p + 1) * 256].rearrange("p (b d) -> p b d", d=QK)
                            dB = [OWN] if nm == "q" else BIGW
                            qi = stg_i[0] % 2
                            stg_i[0] += 1
                            copy_op(K.act, qkt[qi][:, 0:256], bk[:, 0:256], [bb], [QKT[qi]])
                            srcv = qkt[qi][:, 0:256].rearrange("p (b d) -> p b d", d=QK)
                            copy_op(K.pool, dst[:, :, DA_ROT:QK], srcv[:, :, DA_ROT:QK], [QKT[qi]], dB, partial=True)
                            if stopA == "q_copy":
                                return True
                            rope_apply(qkt[qi], QKT[qi], 4, QK, DA_ROT, dst, dB, s_, 0)
                            if stopA == "q_rope":
                                return True
                        elif nm == "v":
                            scale_copy_op(evac_eng(), vda[:, 2 * p:2 * p + 2, s_, 0:DV], bk[:, 0:256].rearrange("p (h c) -> p h c", c=DV), ropet[:, s_, 160:161], [bb, RP], BIGW)
                        elif nm == "kpe":
                            dst = kpe_tok[:, s_, :].rearrange("p (b d) -> p b d", d=ROPE)
                            qi = stg_i[0] % 2
                            stg_i[0] += 1
                            copy_op(K.act, qkt[qi][:, 0:ROPE], bk[:, 0:ROPE], [bb], [QKT[qi]])
                            rope_apply(qkt[qi], QKT[qi], 1, ROPE, ROPE, dst, BIGW, s_, 32)
                    ring.done(1)
                if nm in ("cq", "ckv"):
                    nn = QR if nm == "cq" else KVR
                    gt_ = qng if nm == "cq" else kvng
                    dstt = cqn if nm == "cq" else ckvn
                    dB = [OWN] if nm == "cq" else BIGW
                    rms_rstd(None, lambda s_: acc[s_][0][:, 0:nn], nn, NSUB, [acc[i][1] for i in range(NSUB)], ss2, SS2, env["junk"], env["JB"])
                    for s_ in range(NSUB):
                        K.op(K.dve, lambda e: e.scalar_tensor_tensor(out=dstt[:, s_, :], in0=acc[s_][0][:, 0:nn], scalar=ss2[:, s_:s_ + 1], in1=gt_[:, 0:nn],
                                                                     op0=ALU.mult, op1=ALU.mult), [acc[s_][1], SS2, CONST], dB, partial=True)
            if stopA == "proj":
                return True
            lat = [(ckvn, KVKC, ckvnT, CKT, BIGW)]
            if own:
                lat.append((cqn, QKC, cqnT, CQT, [OWN]))
            for (srct, nk, dstT, DTB, SB_) in lat:
                for kc in range(nk):
                    bk, bb = rr()
                    for s_ in range(NSUB):
                        K.mm(bb, bk[:, s_ * 128:(s_ + 1) * 128], srct[:, s_, kc * 128:(kc + 1) * 128], identb[:], True, True, reads=SB_ + [CONST])
                    copy_op(evac_eng(), dstT[:, kc, :], bk[:, 0:T], [bb], [DTB], partial=(kc > 0))
            if own:
                for p in range(NUQ_N + NUQ_R):
                    wt, wB = ring.take()
                    for s_ in range(NSUB):
                        bk, bb = rr()
                        for kc in range(QKC):
                            K.mm(bb, bk[:, 0:512], cqnT[:, kc, s_ * 128:(s_ + 1) * 128], wt[:, kc * 512:(kc + 1) * 512], kc == 0, kc == QKC - 1, reads=[wB, CQT])
                        if p < NUQ_N:
                            copy_op(evac_eng(), qnope_tok[:, s_, p * 512:(p + 1) * 512], bk[:, 0:512], [bb], [OWN], partial=True)
                        else:
                            pr = p - NUQ_N
                            nv = min(512, H * ROPE - pr * 512)
                            dst = qpe_tok[:, s_, pr * 512: pr * 512 + nv].rearrange("p (b d) -> p b d", d=ROPE)
                            qi = stg_i[0] % 2
                            stg_i[0] += 1
                            copy_op(K.act, qkt[qi][:, 0:nv], bk[:, 0:nv], [bb], [QKT[qi]])
                            rope_apply(qkt[qi], QKT[qi], nv // ROPE, ROPE, ROPE, dst, [OWN], s_, 32)
                    ring.done(1)
            for p in range(2 * NUKV):
                wt, wB = ring.take()
                for s_ in range(NSUB):
                    bk, bb = rr()
                    for kc in range(KVKC):
                        K.mm(bb, bk[:, 0:512], ckvnT[:, kc, s_ * 128:(s_ + 1) * 128], wt[:, kc * 512:(kc + 1) * 512], kc == 0, kc == KVKC - 1, reads=[wB, CKT])
                    if p < NUKV:
                        copy_op(evac_eng(), knope_tok[:, s_, p * 512:(p + 1) * 512], bk[:, 0:512], [bb], BIGW, partial=True)
                    else:
                        pv = p - NUKV
                        scale_copy_op(evac_eng(), vmla[:, 4 * pv:4 * pv + 4, s_, 0:DV], bk[:, 0:512].rearrange("p (h c) -> p h c", c=DV), ropet[:, s_, 160:161], [bb, RP], BIGW)
                ring.done(1)
            if stopA == "mla":
                return True
            for h in range(H):
                transpose_out(lambda s_: kda_tok[:, s_, h * 128:(h + 1) * 128], 128, KT_da[g][h, :, t0:t0 + T], BIGW, [QKVB[g]])
                transpose_out(lambda s_: knope_tok[:, s_, h * 128:(h + 1) * 128], 128, KT_nope[g][h, :, t0:t0 + T], BIGW, [QKVB[g]])
            transpose_out(lambda s_: kpe_tok[:, s_, :], ROPE, KT_pe[g][:, t0:t0 + T], BIGW, [QKVB[g]])
            K.dma(K.pool, [(V_da[g][:, :, ti * NSUB:(ti + 1) * NSUB, :].rearrange("h p s c -> p h s c"), vda),
                           (V_mla[g][:, :, ti * NSUB:(ti + 1) * NSUB, :].rearrange("h p s c -> p h s c"), vmla)],
                  BIGW, [QKVB[g]], partial=True)
            if own:
                for h in range(H):
                    transpose_out(lambda s_: qda_tok[:, s_, h * 128:(h + 1) * 128], 128, QT_da[g][h, :, t0:t0 + T], [OWN], [QKVB[g]])
                    transpose_out(lambda s_: qnope_tok[:, s_, h * 128:(h + 1) * 128], 128, QT_nope[g][h, :, t0:t0 + T], [OWN], [QKVB[g]])
                for hp in range(H // 2):
                    transpose_out(lambda s_: qpe_tok[:, s_, hp * 128:(hp + 1) * 128], 128, QT_pe[g][hp, :, t0:t0 + T], [OWN], [QKVB[g]])
            return False

        for (g, ti, own) in tiles:
            if do_tile(g, ti, own):
                break
        for nm_, aps in (("x1", x1_d), ):
            for g in range(cfg.NG):
                d = dbg_out(f"{nm_}{g}", (nq, D))
                if d is not None:
                    for ti in range(cfg.own_tiles):
                        K.dma(K.sp, [(xt[:], x1_d[g][ti * T:(ti + 1) * T, :].rearrange("(s p) d -> p s d", p=128))], [x1B[g]], [X])
                        K.dma(K.pool, [(d[ti * T:(ti + 1) * T, :].rearrange("(s p) d -> p s d", p=128), xt[:])], [X], [])
        phase_barrier(K, [X, HT, GT, OWN, CQT, CKT, RP, G1B, SS2, env["SS"], env["JB"]] + env["XS"] + STG + RT + QKT + env["STMP"] + env["UT"] + ring.bufs + bankB)


    with contextlib.ExitStack() as es:
        SMAX = max(cfg.S)
        NKTM = SMAX // 128
        NQT = nq // 512
        KTb = [K.sbuf(f"KTb{i}", [128, SMAX], BF16, es) for i in range(2)]
        Vb = [K.sbuf(f"Vb{i}", [128, NKTM, VA], BF16, es) for i in range(2)]
        NQB = 3
        qbuf = [K.sbuf(f"qbuf{i}", [128, 512], BF16, es) for i in range(NQB)]
        qpebuf = [K.sbuf(f"qpebuf{i}", [64, 512], BF16, es) for i in range(NQB)]
        QBUF = [Buf(K, f"qbuf{i}") for i in range(NQB)]
        kpeb = K.sbuf("kpeb", [64, SMAX], BF16, es)
        KVQ = [Buf(K, f"kvq{i}") for i in range(2)]
        KPE = Buf(K, "kpeb")
        NE = 6
        ebuf = [K.sbuf(f"ebuf{i}", [128, 512], BF16, es) for i in range(NE)]
        EB = [Buf(K, f"ebuf{i}") for i in range(NE)]
        accS = [K.sbuf(f"accS{i}", [128, 8, VA], F32, es) for i in range(2)]
        ACS = [Buf(K, f"accS{i}") for i in range(2)]
        rc = K.sbuf("rc", [128, 8], F32, es)
        rl = K.sbuf("rl", [128, 8], F32, es)
        RC = Buf(K, "rc")
        t1 = K.sbuf("t1", [128, DV], F32, es)
        T1 = Buf(K, "t1")
        o32 = K.sbuf("o32", [128, 4, DV], F32, es)
        O32 = Buf(K, "o32")
        ssq = K.sbuf("ssq", [128, 4], F32, es)
        SSQ = Buf(K, "ssq")
        junkB = K.sbuf("junkB", [128, DV], BF16, es)
        JKB = Buf(K, "junkB")
        obf = [K.sbuf(f"obf{i}", [128, 4, DV], BF16, es) for i in range(2)]
        OBF = [Buf(K, f"obf{i}") for i in range(2)]
        mstg = [K.sbuf(f"mstg{i}", [128, 512], BF16, es) for i in range(2)]
        rsum = K.sbuf("rsum", [1, 512], F32, es)
        RSUM = Buf(K, "rsum")
        rbc = K.sbuf("rbc", [128, 512], F32, es)
        RBC = Buf(K, "rbc")
        MST = [Buf(K, f"mstg{i}") for i in range(2)]
        da_slots = {(0, 0): 0, (0, 1): 1, (0, 2): 2, (1, 0): 3, (1, 1): 4, (1, 2): 5, (0, 3): 6, (1, 3): 7}

        def acc_slot(sl):
            b = 4 + sl // 3
            o = (sl % 3) * VA
            return banks[b][:, o:o + VA], bankB[b]

        units = [(g, fam, h) for g in range(cfg.NG) for fam in ("da", "mla") for h in range(H)]

        def load_unit(u):
            g, fam, h = units[u]
            par = u % 2
            Sg = cfg.S[g]
            nkt = Sg // 128
            if fam == "mla" and h == 0:
                K.dma(K.sp, [(kpeb[0:64, 0:Sg], KT_pe[g])], [QKVB[g]], [KPE])
            if fam == "da":
                pairs = [(KTb[par][:, 0:Sg], KT_da[g][h]), (Vb[par][:, 0:nkt, :], V_da[g][h])]
            else:
                pairs = [(KTb[par][:, 0:Sg], KT_nope[g][h]), (Vb[par][:, 0:nkt, :], V_mla[g][h])]
            K.dma(K.sp, pairs, [QKVB[g]], [KVQ[par]])

        def load_q(n):
            if n >= len(units) * NQT:
                return
            u, qt = n // NQT, n % NQT
            g, fam, h = units[u]
            qi = n % NQB
            q0 = qt * 512
            if fam == "da":
                pairs = [(qbuf[qi][:], QT_da[g][h, :, q0:q0 + 512])]
            else:
                pairs = [(qbuf[qi][:], QT_nope[g][h, :, q0:q0 + 512]),
                         (qpebuf[qi][0:64, :], QT_pe[g][h // 2, (h % 2) * 64:(h % 2) * 64 + 64, q0:q0 + 512])]
            K.dma(K.sp, pairs, [QKVB[g]], [QBUF[qi]])

        steps = []
        for u, (g, fam, h) in enumerate(units):
            for qt in range(NQT):
                for kt in range(cfg.S[g] // 128):
                    steps.append((u, qt, kt))
        pending = []
        epi_cnt = [0]
        mst_i = [0]
        SC_MLA = float((NOPE + ROPE) ** -0.5)
        SC_DA = float(QK ** -0.5)

        def emit_qk(i):
            u, qt, kt = steps[i]
            g, fam, h = units[u]
            par = u % 2
            qi = (u * NQT + qt) % NQB
            if fam == "da":
                b0, b1 = 2 * (i % 2), 2 * (i % 2) + 1
                K.mm(bankB[b0], banks[b0][:, 0:512], KTb[par][0:64, kt * 128:(kt + 1) * 128], qbuf[qi][0:64, :], True, True, reads=[KVQ[par], QBUF[qi]])
                K.mm(bankB[b1], banks[b1][:, 0:512], KTb[par][64:128, kt * 128:(kt + 1) * 128], qbuf[qi][64:128, :], True, True, reads=[KVQ[par], QBUF[qi]])
            else:
                b0 = i % 4
                K.mm(bankB[b0], banks[b0][:, 0:512], KTb[par][:, kt * 128:(kt + 1) * 128], qbuf[qi][:], True, False, reads=[KVQ[par], QBUF[qi]])
                K.mm(bankB[b0], banks[b0][:, 0:512], kpeb[0:64, kt * 128:(kt + 1) * 128], qpebuf[qi][0:64, :], False, True, reads=[KVQ[par], KPE, QBUF[qi]])

        def emit_exp_pv(i):
            u, qt, kt = steps[i]
            g, fam, h = units[u]
            par = u % 2
            nkt = cfg.S[g] // 128
            if fam == "da":
                for j in range(2):
                    b = 2 * (i % 2) + j
                    ei = (2 * i + j) % NE
                    K.op(K.act, lambda e: e.activation(out=ebuf[ei][:], in_=banks[b][:, 0:512], func=AF.Exp, scale=SC_DA), [bankB[b]], [EB[ei]])
                for j in range(2):
                    ei = (2 * i + j) % NE
                    for qs in range(4):
                        sl = da_slots[(j, qs)]
                        ap, AB = acc_slot(sl)
                        K.mm(AB, ap, ebuf[ei][:, qs * 128:(qs + 1) * 128], Vb[par][:, kt, :], kt == 0 and sl in (0, 3, 6), kt == nkt - 1 and sl in (2, 5, 7),
                             reads=[EB[ei], KVQ[par]], mark=(qs == 3 or (kt == nkt - 1 and sl in (2, 5, 7))))
            else:
                b = i % 4
                ei = i % NE
                K.op(K.act, lambda e: e.activation(out=ebuf[ei][:], in_=banks[b][:, 0:512], func=AF.Exp, scale=SC_MLA), [bankB[b]], [EB[ei]])
                K.mm(bankB[4], banks[4][:, 0:512], Vb[par][:, kt, 0:DV], ebuf[ei][:], kt == 0, kt == nkt - 1, reads=[EB[ei], KVQ[par]], mark=False)
                K.mm(bankB[5], banks[5][0:1, 0:512], Vb[par][:, kt, DV:VA], ebuf[ei][:], kt == 0, kt == nkt - 1, reads=[EB[ei], KVQ[par]], mark=True)
                if kt == nkt - 1:
                    bankB[4].w = dict(bankB[5].w)
                    bankB[4].r = {}
            if kt == nkt - 1:
                epilogue(i, u, qt)

        def epilogue(i, u, qt):
            g, fam, h = units[u]
            ep = epi_cnt[0] % 2
            epi_cnt[0] += 1
            if fam == "mla":
                K.op(K.dve, lambda e: e.reciprocal(out=rsum[0:1, :], in_=banks[5][0:1, 0:512]), [bankB[5]], [RSUM])
                K.mm(bankB[7], banks[7][:, 0:512], ones_row[0:1, :], rsum[0:1, :], True, True, reads=[RSUM, CONST])
                K.op(K.act, lambda e: e.copy(out=rbc[:], in_=banks[7][:, 0:512]), [bankB[7]], [RBC])
                mi = mst_i[0] % 2
                mst_i[0] += 1
                K.op(K.dve, lambda e: e.tensor_tensor(out=mstg[mi][:], in0=banks[4][:, 0:512], in1=rbc[:], op=ALU.mult), [bankB[4], RBC], [MST[mi]])
                K.dma(K.pool, [(mixT[g][H + h, :, qt * 512:(qt + 1) * 512], mstg[mi][:])], [MST[mi]], [mixB[g]], partial=True)
                return
            aS, AS = accS[ep], ACS[ep]
            nb = 3
            for b in range(nb):
                ns = 3 if b < 2 else 2
                K.op(K.dve, lambda e: e.tensor_copy(out=aS[:, 3 * b:3 * b + ns, :], in_=banks[4 + b][:, 0:ns * VA].rearrange("p (s c) -> p s c", c=VA)),
                     [bankB[4 + b]], [AS], partial=(b > 0))
            K.op(K.dve, lambda e: e.reciprocal(out=rc[:, 0:8], in_=aS[:, 0:8, DV]), [AS], [RC])
            ob, OB = obf[ep], OBF[ep]
            K.op(K.dve, lambda e: e.tensor_scalar(out=rl[:, 0:8], in0=rc[:, 0:8], scalar1=neglam[:, 0:1], scalar2=None, op0=ALU.mult), [RC, CONST], [RC], partial=True)
            for qs in range(4):
                s0, s1 = da_slots[(0, qs)], da_slots[(1, qs)]
                K.op(K.dve, lambda e: e.tensor_scalar(out=t1[:], in0=aS[:, s1, 0:DV], scalar1=rl[:, s1:s1 + 1], scalar2=None, op0=ALU.mult), [AS, RC], [T1])
                K.op(K.dve, lambda e: e.scalar_tensor_tensor(out=o32[:, qs, :], in0=aS[:, s0, 0:DV], scalar=rc[:, s0:s0 + 1], in1=t1[:], op0=ALU.mult, op1=ALU.add),
                     [AS, RC, T1], [O32], partial=(qs > 0))
            rms_rstd(None, lambda s_: o32[:, s_, :], DV, 4, [O32], ssq, SSQ, junkB, JKB)
            for qs in range(4):
                K.op(K.dve, lambda e: e.scalar_tensor_tensor(out=ob[:, qs, :], in0=o32[:, qs, :], scalar=ssq[:, qs:qs + 1], in1=subg[:], op0=ALU.mult, op1=ALU.mult),
                     [O32, SSQ, CONST], [OB], partial=(qs > 0))
            hh = h

            def fin():
                bk, bb = banks[7], bankB[7]
                for qs in range(4):
                    K.mm(bb, bk[:, qs * 128:(qs + 1) * 128], ob[:, qs, :], identb[:], True, True, reads=[OB, CONST])
                mi = mst_i[0] % 2
                mst_i[0] += 1
                K.op(K.dve, lambda e: e.tensor_copy(out=mstg[mi][:], in_=bk[:, 0:512]), [bb], [MST[mi]])
                K.dma(K.pool, [(mixT[g][hh, :, qt * 512:(qt + 1) * 512], mstg[mi][:])], [MST[mi]], [mixB[g]], partial=True)
            pending.append((i + 4, fin))

        load_unit(0)
        load_q(0)
        load_q(1)
        qk_next = [0]

        def ensure_qk(upto):
            while qk_next[0] <= min(upto, len(steps) - 1):
                emit_qk(qk_next[0])
                qk_next[0] += 1

        for i in range(len(steps)):
            u, qt, kt = steps[i]
            if qt == 0 and kt == 0 and u + 1 < len(units):
                load_unit(u + 1)
            if kt == 0:
                load_q(u * NQT + qt + 2)
            ensure_qk(i + (2 if units[u][1] == "mla" else 1))
            emit_exp_pv(i)
            while pending and pending[0][0] <= i:
                pending.pop(0)[1]()
        while pending:
            pending.pop(0)[1]()
        phase_barrier(K, KVQ + QBUF + [KPE, RC, T1, O32, SSQ, JKB, RSUM, RBC] + EB + ACS + OBF + MST + bankB)

    with contextlib.ExitStack() as es:
        env = make_ffn_env(es, "C")
        xt, X = env["xt"], env["X"]
        ring = Ring(K, "ringC", NSL, cfg.slot_elems, es)
        mT = K.sbuf("mT", [128, 2 * H, T], BF16, es)
        MT = Buf(K, "mT")
        gate23 = K.sbuf("gate23", [128, 2, D], F32, es)
        G23 = Buf(K, "gate23")
        fing = K.sbuf("fing", [128, D], F32, es)
        FING = Buf(K, "fing")
        K.dma(K.sp, [(fing[:], final_norm_d.partition_broadcast(128))], [XIN], [FING])
        ctiles = [(g, ti) for g in range(cfg.NG) for ti in range(cfg.own_tiles)]
        for _ in ctiles:
            for p in range(D // 256):
                ring.plan(wb["w_o"][p], wbB["w_o"], 2 * H * 256)
            plan_ffn(ring, "f2")
        for (g, ti) in ctiles:
            t0 = ti * T
            if ti == 0:
                K.dma(K.sp, [(gate23[:, l - 1, :], gbc[l, g]) for l in (1, 2)], [gbcB], [G23])
            K.dma(K.sp, [(xt[:], x1_d[g][t0:t0 + T, :].rearrange("(s p) d -> p s d", p=128))], [x1B[g]], [X])
            K.dma(K.sp, [(mT[:], mixT[g][:, :, t0:t0 + T].rearrange("h p t -> p h t"))], [mixB[g]], [MT])
            for p in range(D // 256):
                wt, wB = ring.take()
                for s_ in range(NSUB):
                    bk, bb = rr()
                    for kc in range(2 * H):
                        K.mm(bb, bk[:, 0:256], mT[:, kc, s_ * 128:(s_ + 1) * 128], wt[:, kc * 256:(kc + 1) * 256], kc == 0, kc == 2 * H - 1, reads=[wB, MT])
                    i = env["cnt"] % 2
                    env["cnt"] += 1
                    K.op(K.dve, lambda e: e.tensor_tensor(out=env["utmp"][i][:, 0:256], in0=bk[:, 0:256], in1=gate23[:, 0, p * 256:(p + 1) * 256], op=ALU.mult),
                         [bb, G23], [env["UT"][i]])
                    K.op(K.dve, lambda e: e.tensor_tensor(out=xt[:, s_, p * 256:(p + 1) * 256], in0=xt[:, s_, p * 256:(p + 1) * 256], in1=env["utmp"][i][:, 0:256], op=ALU.add),
                         [env["UT"][i], X], [X], partial=True)
                ring.done(1)
            d = dbg_out(f"x2{g}", (nq, D))
            if d is not None:
                K.dma(K.pool, [(d[t0:t0 + T, :].rearrange("(s p) d -> p s d", p=128), xt[:])], [X], [])
            norm_to_hT(env, 2, g)
            ffn(env, ring, gate23[:, 1, :], G23)
            d = dbg_out(f"x3{g}", (nq, D))
            if d is not None:
                K.dma(K.pool, [(d[t0:t0 + T, :].rearrange("(s p) d -> p s d", p=128), xt[:])], [X], [])
            rms_rstd(None, lambda s_: xt[:, s_, :], D, NSUB, [X], env["ss"], env["SS"], env["junk"], env["JB"])
            for s_ in range(NSUB):
                K.op(K.dve, lambda e: e.scalar_tensor_tensor(out=xt[:, s_, :], in0=xt[:, s_, :], scalar=env["ss"][:, s_:s_ + 1], in1=fing[:], op0=ALU.mult, op1=ALU.mult),
                     [X, env["SS"], FING], [X], partial=True)
            K.dma(K.pool, [(y_d[g][t0:t0 + T, :].rearrange("(s p) d -> p s d", p=128), xt[:])], [X], [])
        phase_barrier(K, [X, env["HT"], env["GT"], MT, G23, FING, env["SS"], env["JB"]] + env["XS"] + env["STMP"] + env["UT"] + ring.bufs + bankB)

    return nc, K, locals()


def phase_barrier(K, bufs):
    for E in (K.pe, K.act, K.dve, K.pool, K.sp):
        for b in bufs:
            for ev in list(b.w.values()) + list(b.r.values()):
                E.wait(ev, False)


_CACHE = {}


def kernel(**inputs):
    cfg = Cfg()
    if "nc" not in _CACHE:
        _CACHE["nc"] = build_program(cfg)[0]
    nc = _CACHE["nc"]
    maps = prepare_inputs(cfg, inputs)
    res = run_bass_kernel_spmd(nc, maps, core_ids=list(range(cfg.n_cores)))
    nq = cfg.nq
    half = cfg.n_cores // 2
    y_p = np.zeros((1, cfg.Sp, cfg.D), np.float32)
    y_s = np.zeros((half // 2, cfg.Ss, cfg.D), np.float32)
    for c in range(cfg.n_cores):
        r = res.results[c]["y_p"]
        if c < half:
            y_p[0, c * nq:(c + 1) * nq] = r
        else:
            sq, j = (c - half) // 2, (c - half) % 2
            y_s[sq, j * nq:(j + 1) * nq] = r
    return (y_p, y_s)
```
